# Optimizing a Trainium2 kernel written in Bass

```python
import math
import jax, jax.numpy as jnp
from jax import lax
import numpy as np

D_MODEL = 1024
BATCH = 2
SEQ = 8192
DEPTH = 4
DEC_BATCH = 128
DEC_SEQ = 4
PAST_LEN = 8192
PAGE_SIZE = 128

D_MIX = D_MODEL
HEAD_DIM = 64
ATT_WIDTH = D_MIX // 2
ATT_HEADS = ATT_WIDTH // HEAD_DIM
ATT_KV_HEADS = ATT_HEADS // 4
GROUP = ATT_HEADS // ATT_KV_HEADS
KV_WIDTH = ATT_KV_HEADS * HEAD_DIM
WINDOW = 128
ROPE_DIM = HEAD_DIM // 4
ROPE_THETA = 500000.0
DN_DK = 128
DN_DV = 128
DN_WIDTH = D_MIX - ATT_WIDTH
DN_HEADS = DN_WIDTH // DN_DV
DN_CONV_CH = DN_HEADS * (2 * DN_DK + DN_DV)
CONV_W = 4
CHUNK = 64
N_IN = 2 * ATT_WIDTH + 2 * KV_WIDTH + DN_CONV_CH + DN_WIDTH + 2 * DN_HEADS
EPS = 1e-6

kernel_name = 'hybrid_swa_sink_gated_delta_step'

F32 = jnp.float32


def _rms(x, g):
    xf = x.astype(F32)
    y = xf * lax.rsqrt(jnp.mean(xf * xf, -1, keepdims=True) + EPS)
    return (y * g.astype(F32)).astype(x.dtype)


def _l2n(x):
    xf = x.astype(F32)
    return xf * lax.rsqrt(jnp.sum(xf * xf, -1, keepdims=True) + EPS)


def _rope(x, pos):
    half = ROPE_DIM // 2
    inv = jnp.power(jnp.float32(ROPE_THETA), -jnp.arange(half, dtype=F32) / half)
    ang = pos.astype(F32)[:, None] * inv[None, :]
    cos = jnp.cos(ang)[:, None, :]
    sin = jnp.sin(ang)[:, None, :]
    xf = x.astype(F32)
    x1, x2, rest = xf[..., :half], xf[..., half:ROPE_DIM], xf[..., ROPE_DIM:]
    out = jnp.concatenate([x1 * cos - x2 * sin, x2 * cos + x1 * sin, rest], -1)
    return out.astype(x.dtype)


def _split_cols(z):
    sizes = (ATT_WIDTH, KV_WIDTH, KV_WIDTH, ATT_WIDTH, DN_CONV_CH, DN_WIDTH, DN_HEADS, DN_HEADS)
    idx = [int(i) for i in np.cumsum(sizes)[:-1]]
    return jnp.split(z, idx, axis=-1)


def _sink_attend(q, k, v, mask, sinks):
    s = jnp.einsum('...qkgd,...skd->...kgqs', q.astype(F32), k.astype(F32)) * (HEAD_DIM ** -0.5)
    s = jnp.where(mask, s, -jnp.inf)
    sink = sinks.astype(F32).reshape(ATT_KV_HEADS, GROUP, 1, 1)
    m = jnp.maximum(jnp.max(s, -1, keepdims=True), sink)
    p = jnp.exp(s - m)
    denom = jnp.sum(p, -1, keepdims=True) + jnp.exp(sink - m)
    o = jnp.einsum('...kgqs,...skd->...qkgd', p / denom, v.astype(F32))
    return o.astype(q.dtype)


def _swa_banded(q, k, v, sinks):
    B, S = q.shape[:2]
    nb = S // WINDOW
    qb = q.reshape(B, nb, WINDOW, ATT_KV_HEADS, GROUP, HEAD_DIM)

    def with_prev(t):
        t = t.reshape(B, nb, WINDOW, ATT_KV_HEADS, HEAD_DIM)
        prev = jnp.concatenate([jnp.zeros_like(t[:, :1]), t[:, :-1]], axis=1)
        return jnp.concatenate([prev, t], axis=2)

    i = jnp.arange(WINDOW)[:, None] + WINDOW
    j = jnp.arange(2 * WINDOW)[None, :]
    band = (j <= i) & (j > i - WINDOW)
    first = (jnp.arange(nb)[:, None, None] > 0) | (j >= WINDOW)[None]
    mask = (band[None] & first)[:, None, None]
    o = _sink_attend(qb, with_prev(k), with_prev(v), mask, sinks)
    return o.reshape(B, S, ATT_WIDTH)


def _swa_cached(q, k, v, win_k, win_v, sinks):
    B, L = q.shape[:2]
    wb = win_k.shape[1]
    kk = jnp.concatenate([win_k.astype(k.dtype), k], 1)
    vv = jnp.concatenate([win_v.astype(v.dtype), v], 1)
    kpos = PAST_LEN - wb + jnp.arange(wb + L)
    qpos = PAST_LEN + jnp.arange(L)
    mask = (kpos[None, :] <= qpos[:, None]) & (kpos[None, :] > qpos[:, None] - WINDOW)
    o = _sink_attend(q.reshape(B, L, ATT_KV_HEADS, GROUP, HEAD_DIM), kk, vv, mask, sinks)
    return o.reshape(B, L, ATT_WIDTH), kk[:, -wb:], vv[:, -wb:]


def _gated_delta(q, k, v, beta, g, s0):
    B, L, H, DK = q.shape
    DV = v.shape[-1]
    c = min(CHUNK, L)
    pad = (-L) % c
    n = (L + pad) // c

    def prep(t):
        t = jnp.pad(t.astype(F32), [(0, 0), (0, pad)] + [(0, 0)] * (t.ndim - 2))
        t = t.reshape((B, n, c) + t.shape[2:])
        return jnp.moveaxis(t, 3, 2)

    q, k, v, beta, g = prep(q), prep(k), prep(v), prep(beta), prep(g)
    G = jnp.cumsum(g, axis=-1)
    ci = jnp.arange(c)
    causal = ci[:, None] >= ci[None, :]
    strict = ci[:, None] > ci[None, :]
    decay = jnp.exp(jnp.where(causal, G[..., :, None] - G[..., None, :], -jnp.inf))
    kb = k * beta[..., None]
    a_kk = jnp.where(strict, jnp.einsum('bnhid,bnhjd->bnhij', kb, k) * decay, 0.0)
    rhs = jnp.concatenate([v * beta[..., None], kb * jnp.exp(G)[..., None]], -1)
    sol = lax.linalg.triangular_solve(a_kk, rhs, left_side=True, lower=True, unit_diagonal=True)
    u, w = sol[..., :DV], sol[..., DV:]
    a_qk = jnp.einsum('bnhid,bnhjd->bnhij', q, k) * decay
    qg = q * jnp.exp(G)[..., None]
    kd = k * jnp.exp(G[..., -1:] - G)[..., None]
    gl = jnp.exp(G[..., -1])
    xs = tuple(jnp.moveaxis(t, 1, 0) for t in (u, w, a_qk, qg, kd, gl))

    def step(S, inp):
        u_c, w_c, aqk_c, qg_c, kd_c, gl_c = inp
        vnew = u_c - jnp.einsum('bhck,bhkv->bhcv', w_c, S)
        o = jnp.einsum('bhck,bhkv->bhcv', qg_c, S) + jnp.einsum('bhij,bhjv->bhiv', aqk_c, vnew)
        S = S * gl_c[..., None, None] + jnp.einsum('bhck,bhcv->bhkv', kd_c, vnew)
        return S, o

    S, o = lax.scan(step, s0.astype(F32), xs)
    o = jnp.moveaxis(o, 0, 1)
    o = jnp.moveaxis(o, 2, 3).reshape(B, n * c, H, DV)[:, :L]
    return o, S


def _delta_branch(dqkv, dg, db, da, conv_buf, s0, conv_w, a_log, dt_bias, dn_norm_g):
    B, L = dqkv.shape[:2]
    xx = jnp.concatenate([conv_buf.astype(dqkv.dtype), dqkv], 1)
    y = xx[:, 0:L] * conv_w[0]
    for j in range(1, CONV_W):
        y = y + xx[:, j:j + L] * conv_w[j]
    y = jax.nn.silu(y)
    new_buf = xx[:, xx.shape[1] - (CONV_W - 1):]
    q, k, v = jnp.split(y, [DN_HEADS * DN_DK, 2 * DN_HEADS * DN_DK], -1)
    q = _l2n(q.reshape(B, L, DN_HEADS, DN_DK)) * (DN_DK ** -0.5)
    k = _l2n(k.reshape(B, L, DN_HEADS, DN_DK))
    v = v.reshape(B, L, DN_HEADS, DN_DV)
    beta = jax.nn.sigmoid(db.astype(F32))
    g = -jnp.exp(a_log.astype(F32)) * jax.nn.softplus(da.astype(F32) + dt_bias.astype(F32))
    o, s_new = _gated_delta(q, k, v, beta, g, s0)
    o = _rms(o, dn_norm_g) * jax.nn.silu(dg.astype(F32).reshape(B, L, DN_HEADS, DN_DV))
    return o.reshape(B, L, DN_WIDTH).astype(dqkv.dtype), new_buf, s_new


def _layer(x, pos, win_k, win_v, conv_buf, s0, norm_g, w_in, q_norm_g, k_norm_g, sinks,
           conv_w, a_log, dt_bias, dn_norm_g, w_out):
    B, L, _ = x.shape
    z = jnp.einsum('bld,de->ble', _rms(x, norm_g), w_in)
    aq, ak, av, ag, dqkv, dg, db, da = _split_cols(z)
    q = _rope(_rms(aq.reshape(B, L, ATT_HEADS, HEAD_DIM), q_norm_g), pos)
    k = _rope(_rms(ak.reshape(B, L, ATT_KV_HEADS, HEAD_DIM), k_norm_g), pos)
    v = av.reshape(B, L, ATT_KV_HEADS, HEAD_DIM)
    if win_k is None:
        att = _swa_banded(q, k, v, sinks)
        wb = min(WINDOW, L)
        nk, nv = k[:, L - wb:], v[:, L - wb:]
    else:
        att, nk, nv = _swa_cached(q, k, v, win_k, win_v, sinks)
    att = att * jax.nn.silu(ag)
    dn, nbuf, ns = _delta_branch(dqkv, dg, db, da, conv_buf, s0, conv_w, a_log, dt_bias, dn_norm_g)
    y = jnp.einsum('ble,ed->bld', jnp.concatenate([att, dn], -1), w_out)
    return x + y, nk, nv, nbuf, ns


def setup_inputs(seed: int = 0) -> dict:
    key = jax.random.key(seed)
    ks = jax.random.split(key, 20)
    wb = min(WINDOW, PAST_LEN)
    nrm = jax.random.normal
    x_prompt = nrm(ks[0], (BATCH, SEQ, D_MODEL), F32)
    x_sample = nrm(ks[1], (DEC_BATCH, DEC_SEQ, D_MODEL), F32)
    cache_win_k = nrm(ks[2], (DEPTH, DEC_BATCH, wb, ATT_KV_HEADS, HEAD_DIM), F32)
    cache_win_v = nrm(ks[3], (DEPTH, DEC_BATCH, wb, ATT_KV_HEADS, HEAD_DIM), F32)
    state_conv = nrm(ks[4], (DEPTH, DEC_BATCH, CONV_W - 1, DN_CONV_CH), F32)
    state_delta = 0.1 * nrm(ks[5], (DEPTH, DEC_BATCH, DN_HEADS, DN_DK, DN_DV), F32)
    norm_g = 1.0 + 0.02 * nrm(ks[6], (DEPTH, D_MODEL), F32)
    w_in = nrm(ks[7], (DEPTH, D_MODEL, N_IN), F32) * (D_MODEL ** -0.5)
    q_norm_g = 1.0 + 0.02 * nrm(ks[8], (DEPTH, HEAD_DIM), F32)
    k_norm_g = 1.0 + 0.02 * nrm(ks[9], (DEPTH, HEAD_DIM), F32)
    sinks = 0.5 * nrm(ks[10], (DEPTH, ATT_HEADS), F32)
    conv_w = nrm(ks[11], (DEPTH, CONV_W, DN_CONV_CH), F32) * (CONV_W ** -0.5)
    a_log = jnp.log(jax.random.uniform(ks[12], (DEPTH, DN_HEADS), F32, 1.0, 16.0))
    dt = jnp.exp(jax.random.uniform(ks[13], (DEPTH, DN_HEADS), F32, math.log(1e-3), math.log(1e-1)))
    dt_bias = dt + jnp.log(-jnp.expm1(-dt))
    dn_norm_g = 1.0 + 0.02 * nrm(ks[14], (DEPTH, DN_DV), F32)
    w_out = nrm(ks[15], (DEPTH, D_MIX, D_MODEL), F32) * (D_MIX ** -0.5)
    return {'x_prompt': x_prompt, 'x_sample': x_sample, 'cache_win_k': cache_win_k,
            'cache_win_v': cache_win_v, 'state_conv': state_conv, 'state_delta': state_delta,
            'norm_g': norm_g, 'w_in': w_in, 'q_norm_g': q_norm_g, 'k_norm_g': k_norm_g,
            'sinks': sinks, 'conv_w': conv_w, 'a_log': a_log, 'dt_bias': dt_bias,
            'dn_norm_g': dn_norm_g, 'w_out': w_out}


def reference(x_prompt, x_sample, cache_win_k, cache_win_v, state_conv, state_delta,
              norm_g, w_in, q_norm_g, k_norm_g, sinks, conv_w, a_log, dt_bias, dn_norm_g, w_out):
    bp, lp = x_prompt.shape[:2]
    ls = x_sample.shape[1]
    pos_p = jnp.arange(lp)
    pos_s = PAST_LEN + jnp.arange(ls)
    xp, xs = x_prompt, x_sample
    pk, pv, pc, pd, sk, sv, sc, sd = [], [], [], [], [], [], [], []
    for l in range(DEPTH):
        w = (norm_g[l], w_in[l], q_norm_g[l], k_norm_g[l], sinks[l], conv_w[l], a_log[l],
             dt_bias[l], dn_norm_g[l], w_out[l])
        buf0 = jnp.zeros((bp, CONV_W - 1, DN_CONV_CH), xp.dtype)
        s0 = jnp.zeros((bp, DN_HEADS, DN_DK, DN_DV), F32)
        xp, k1, v1, c1, d1 = _layer(xp, pos_p, None, None, buf0, s0, *w)
        xs, k2, v2, c2, d2 = _layer(xs, pos_s, cache_win_k[l], cache_win_v[l], state_conv[l],
                                    state_delta[l], *w)
        pk.append(k1); pv.append(v1); pc.append(c1); pd.append(d1.astype(x_prompt.dtype))
        sk.append(k2); sv.append(v2); sc.append(c2); sd.append(d2.astype(state_delta.dtype))
    return (xp, xs, jnp.stack(pk), jnp.stack(pv), jnp.stack(pc), jnp.stack(pd),
            jnp.stack(sk), jnp.stack(sv), jnp.stack(sc), jnp.stack(sd))
```

```python
import numpy as np
import concourse.bass as bass
import concourse.mybir as mybir
from concourse.bass_utils import run_bass_kernel_spmd
from contextlib import ExitStack

F32 = mybir.dt.float32
BF16 = mybir.dt.bfloat16
ALU = mybir.AluOpType
AF = mybir.ActivationFunctionType
AX = mybir.AxisListType

D = 1024
NIN = 3336
EPS = 1e-6
import os
DBG = set(os.environ.get('K_DBG', '').split(','))
STOP = os.environ.get('K_STOP', '')
NEG = -1.0e9
C_AQ, C_AK, C_AV, C_AG, C_DQ, C_DG, C_DB, C_DA = 0, 512, 640, 768, 1280, 2816, 3328, 3332


class Buf:
    def __init__(self, name, t, nsub=1):
        self.name = name
        self.t = t
        self.nsub = nsub

    def keys(self):
        return [(self.name, i) for i in range(self.nsub)]

    def k(self, *idx):
        return [(self.name, i) for i in idx]

    def __getitem__(self, key):
        return self.t[key]


def _keys(lst):
    out = []
    for x in lst:
        if isinstance(x, Buf):
            out.extend(x.keys())
        elif isinstance(x, list):
            out.extend(x)
        else:
            out.append(x)
    return out


class Sched:
    ENGS = ['pe', 'act', 'dve', 'pool', 'sp']

    def __init__(self, nc, es, ndma=None):
        self.nc = nc
        ndma = ndma or {'sp': 24, 'act': 4, 'pool': 12}
        self.ops = {e: [] for e in self.ENGS}
        self.esem = {e: es.enter_context(nc.semaphore('se_' + e)) for e in self.ENGS}
        self.ecount = {e: 0 for e in self.ENGS}
        self.dsems = {q: [es.enter_context(nc.semaphore('sd_%s%d' % (q, i))) for i in range(n)]
                      for q, n in ndma.items()}
        self.dcount = {q: [0] * n for q, n in ndma.items()}
        self.dnext = {q: 0 for q in ndma}
        self.lastw = {}
        self.readers = {}
        self.waited = {e: {} for e in self.ENGS}
        self.semobj = {}
        self.nops = 0
        self.bar_tok = None

    def barrier(self, nops):
        keys = list(set(self.lastw) | set(self.readers))
        tok = None
        for e in ['act', 'dve', 'pool']:
            tok = self.op(e, nops[e], w=keys)
        self.bar_tok = tok

    def op(self, eng, fn, r=(), w=(), dma=False):
        r = _keys(r)
        w = _keys(w)
        deps = []
        for k in r:
            if k in self.lastw:
                deps.append(self.lastw[k])
            elif self.bar_tok is not None:
                deps.append(self.bar_tok)
        for k in w:
            if k in self.lastw:
                deps.append(self.lastw[k])
            elif self.bar_tok is not None:
                deps.append(self.bar_tok)
            deps.extend(self.readers.get(k, ()))
        if dma:
            j = self.dnext[eng]
            self.dnext[eng] = (j + 1) % len(self.dsems[eng])
            sem = self.dsems[eng][j]
            prev = self.dcount[eng][j]
            if prev > 0:
                deps.append((id(sem), prev))
            self.dcount[eng][j] = prev + 16
            tok = (id(sem), prev + 16)
            sig = (sem, 16)
        else:
            sem = self.esem[eng]
            self.ecount[eng] += 1
            tok = (id(sem), self.ecount[eng])
            sig = (sem, 1)
        self.semobj[id(sem)] = sem
        need = {}
        for (sid, v) in deps:
            if (not dma) and eng == 'pe' and sid == id(self.esem['pe']):
                continue
            if self.waited[eng].get(sid, 0) >= v:
                continue
            if need.get(sid, 0) < v:
                need[sid] = v
        for sid, v in need.items():
            self.waited[eng][sid] = v
        waits = [(self.semobj[sid], v) for sid, v in need.items()]
        self.ops[eng].append((waits, fn, sig))
        for k in w:
            self.lastw[k] = tok
            self.readers[k] = []
        for k in r:
            self.readers.setdefault(k, []).append(tok)
        self.nops += 1
        return tok

    def emit(self):
        nc = self.nc
        finals = []
        for q in self.dsems:
            for sem, c in zip(self.dsems[q], self.dcount[q]):
                if c > 0:
                    finals.append((sem, c))
        for e in self.ENGS:
            if e != 'sp' and self.ecount[e] > 0:
                finals.append((self.esem[e], self.ecount[e]))
        sched = self

        def run(engname, eng):
            for (waits, fn, sig) in sched.ops[engname]:
                for (sem, v) in waits:
                    eng.wait_ge(sem, v)
                ins = fn(eng)
                ins.then_inc(sig[0], sig[1])
            if engname == 'sp':
                for (sem, v) in finals:
                    eng.wait_ge(sem, v)

        with nc.Block() as block:
            @block.tensor
            def _(e):
                run('pe', e)

            @block.scalar
            def _(e):
                run('act', e)

            @block.vector
            def _(e):
                run('dve', e)

            @block.gpsimd
            def _(e):
                run('pool', e)

            @block.sync
            def _(e):
                run('sp', e)


class Rec:
    def __init__(self, sched):
        self.s = sched
        self.cur = None
        self.lists = None

    def op(self, *a, **k):
        if self.cur is None:
            return self.s.op(*a, **k)
        self.cur.append((a, k))

    def play(self, item):
        a, k = item
        return self.s.op(*a, **k)

    def barrier(self, *a, **k):
        return self.s.barrier(*a, **k)

    def emit(self):
        return self.s.emit()


def interleave(rec, A, B):
    na, nb = len(A), len(B)
    i = j = 0
    while i < na or j < nb:
        if j >= nb or (i < na and i * nb <= j * na):
            rec.play(A[i])
            i += 1
        else:
            rec.play(B[j])
            j += 1


def bc(ap, shape):
    return ap.to_broadcast(list(shape))


class Builder:
    def __init__(self, NT, DEPTH, G, NSEQ=16):
        self.NT, self.L, self.G, self.NSEQ = NT, DEPTH, G, NSEQ
        self.TP = NT * 128
        self.TS = NSEQ * 4
        self.nc = bass.Bass("TRN2", target_bir_lowering=False)
        self.uid = 0
        self.sb_c = self.SB_LO
        self.sb_p = self.SB_HI
        self.sb_s = self.SB_HI

    SB_LO = 16512
    SB_HI = 229344

    def sb(self, name, shape, dt=F32, nsub=1, region='c'):
        n = 1
        for d in shape[1:]:
            n *= d
        nbytes = ((n * (2 if dt == BF16 else 4)) + 31) // 32 * 32
        if region == 'c':
            off = self.sb_c
            self.sb_c += nbytes
        elif region == 'p':
            self.sb_p -= nbytes
            off = self.sb_p
        else:
            self.sb_s -= nbytes
            off = self.sb_s
        assert self.sb_c <= min(self.sb_p, self.sb_s), ('SBUF overflow', name, self.sb_c, self.sb_p, self.sb_s)
        t = self.nc.alloc_sbuf_tensor_at('sb_' + name, list(shape), dt, offset=off)
        return Buf(name, t, nsub)

    def din(self, name, shape):
        t = self.nc.dram_tensor(name, list(shape), F32, kind="ExternalInput")
        return Buf('d_' + name, t)

    def dout(self, name, shape):
        t = self.nc.dram_tensor(name, list(shape), F32, kind="ExternalOutput")
        return Buf('d_' + name, t)

    def dump(self, name, ap, shape, rkeys):
        if 'dump' not in DBG:
            return
        t = self.nc.dram_tensor('dbg_' + name, list(shape), F32, kind="ExternalOutput")
        b = Buf('dd_' + name, t)
        self.dma(t.ap(), ap, rkeys, [b])

    def bank(self):
        if self.pool == 'f':
            b = self.banks[self.bif % 3]
            self.bif += 1
        elif self.pool == 'b':
            b = self.banks[3 + self.bib % 3]
            self.bib += 1
        else:
            b = self.banks[self.bi % len(self.banks)]
            self.bi += 1
        return b

    def dma(self, out_ap, in_ap, r, w, q='sp'):
        self.S.op(q, lambda e, o=out_ap, i=in_ap: e.dma_start(out=o, in_=i), r=r, w=w, dma=True)

    def build(self):
        nc = self.nc
        L, NT, G, NSEQ, TP, TS = self.L, self.NT, self.G, self.NSEQ, self.TP, self.TS
        with ExitStack() as es:
            self.es = es
            S = self.S = Rec(Sched(nc, es))
            I = self.I = {}
            for name, shape in [
                ('xpT', [D, TP]), ('xsT', [D, TS]), ('w_in', [L, D, NIN]), ('w_out', [L, D, D]),
                ('ckT', [L, NSEQ, 128, 128]), ('ck', [L, NSEQ, 128, 128]), ('cv', [L, NSEQ, 128, 128]),
                ('scT', [L, 1536, NSEQ * 3]), ('sd', [L, NSEQ, 4, 128, 128]),
                ('normg', [128, L * 8]), ('convw', [128, L * 48]), ('dng', [128, L]),
                ('gqk', [128, L * 640]), ('sinkb', [128, L * 8]), ('alogb', [128, L * 4]), ('dtbb', [128, L * 4]),
                ('cosp', [128, NT * 8]), ('sinp', [128, NT * 8]), ('coss', [128, 8]), ('sins', [128, 8]),
                ('cmat', [128, 11 * 128]),
            ]:
                I[name] = self.din(name, shape)
            O = self.O = {}
            for name, shape in [
                ('ypT', [D, TP]), ('ysT', [D, TS]),
                ('pwk', [L, 128, 128]), ('pwv', [L, 128, 128]), ('pconv', [L, 128, 36]), ('pdelta', [L, 4, 128, 128]),
                ('swk', [L, NSEQ, 128, 128]), ('swv', [L, NSEQ, 128, 128]), ('sconv', [L, NSEQ, 3, 1536]),
                ('sdelta', [L, NSEQ, 4, 128, 128]),
            ]:
                O[name] = self.dout(name, shape)

            self.banks = [Buf('ps%d' % i, es.enter_context(nc.psum_tensor('ps%d' % i, [128, 512], F32))) for i in range(6)]
            self.bi = 0
            self.bif = 0
            self.bib = 0
            self.pool = 'all'
            self.bank_long = Buf('ps6', es.enter_context(nc.psum_tensor('ps6', [128, 512], F32)))
            self.psb = Buf('psb', es.enter_context(nc.psum_tensor('psb', [128, 1024], BF16)))

            cm = self.sb('cmat', [128, 11, 128])
            self.dma(cm[:, :, :], I['cmat'].t.ap().rearrange("p (a b) -> p a b", a=11), [I['cmat']], [cm])
            self.cm = cm
            ONES, IDENT, UM, NML, NMU, MPREV, MCUR, US, VS, NMLS, NMUS = range(11)
            identb = self.sb('identb', [128, 128], BF16)
            self.dma(identb[:, :], I['cmat'].t.ap()[:, 128:256], [I['cmat']], [identb], q='pool')
            msp = self.sb('msp', [128, 64], region='s')
            msc = self.sb('msc', [128, 64], region='s')
            P_ = {}
            for name, n in [('normg', L * 8), ('convw', L * 48), ('dng', L), ('sinkb', L * 8),
                            ('alogb', L * 4), ('dtbb', L * 4), ('cosp', NT * 8), ('sinp', NT * 8), ('coss', 8), ('sins', 8)]:
                P_[name] = self.sb('c_' + name, [128, n])
                self.dma(P_[name][:, :], I[name].t.ap(), [I[name]], [P_[name]])
            esink = self.sb('esink', [128, L * 8])
            negA = self.sb('negA', [128, L * 4])
            S.op('act', lambda e: e.activation(out=esink[:, :], in_=P_['sinkb'][:, :], func=AF.Exp), r=[P_['sinkb']], w=[esink])
            S.op('act', lambda e: e.activation(out=negA[:, :], in_=P_['alogb'][:, :], func=AF.Exp), r=[P_['alogb']], w=[negA])
            S.op('dve', lambda e: e.tensor_scalar(out=negA[:, :], in0=negA[:, :], scalar1=-1.0, scalar2=None, op0=ALU.mult), r=[negA], w=[negA])
            self.P_ = P_

            xTb = [self.sb('xT%d' % i, [128, 8, 128], region='p') for i in range(3)]
            xsT = self.sb('xsT', [128, 8, TS], region='s')
            Wi = [self.sb('Wi%d' % i, [128, 8, NIN], BF16) for i in range(1)]
            Wo = [self.sb('Wo%d' % i, [128, 8, D], BF16) for i in range(1)]
            gqk_l = self.sb('gqk_l', [128, 640])
            rstd = self.sb('rstd', [128, 128])
            xnT = self.sb('xnT', [128, 8, 128], BF16)
            zq = self.sb('zq', [128, 640])
            kvs = self.sb('kvs', [128, 136])
            gat = self.sb('gat', [128, 512])
            ss10 = self.sb('ss10', [128, 10])
            qr = self.sb('qr', [128, 10, 64])
            rt = [self.sb('rt%d' % i, [128, 10, 8]) for i in range(4)]
            QTraw = self.sb('QT', [128, 512])
            KT1 = [self.sb('KT%d' % i, [128, 128], region='p') for i in range(2)]
            KT = [KT1 for l in range(L)]
            VA1 = [self.sb('VA%d' % i, [128, 2, 65], BF16, region='p') for i in range(2)]
            VA = [VA1 for l in range(L)]
            KTs = self.sb('KTs', [128, 64], region='s')
            VAs = self.sb('VAs', [128, 2, 65], BF16, region='s')
            pT = [self.sb('pT%d' % i, [128, 4, 128], BF16) for i in range(4)]
            den = self.sb('den', [128, 8])
            att = self.sb('att', [128, 8, 64])
            attg = self.sb('attg', [128, 512], BF16)
            mixTP = [self.sb('mixT%d' % i, [128, 8, 128], BF16) for i in range(2)]
            zdT = self.sb('zdT', [128, 12, 131], region='p')
            zdTs = self.sb('zdTs', [128, 12, NSEQ, 7], region='s')
            scst = self.sb('scst', [128, 12, NSEQ * 3], region='s')
            convst1 = self.sb('convst', [128, 12, 3], region='p')
            convst = [convst1 for l in range(L)]
            gateTP = [self.sb('gateT%d' % i, [128, 4, 128]) for i in range(2)]
            cacc = self.sb('cacc', [128, 12, 128])
            sqt = cacc
            rn8 = None
            qkn = self.sb('qkn', [128, 8, 128])
            sc4 = {n: self.sb('sc_' + n, [128, 4]) for n in ['beta', 'nbeta', 'g', 'sp', 'Gc', 'Gt', 'eG', 'bg', 'ekd', 'gl']}
            gb = self.sb('gb', [128, 4, 128])
            glbc = self.sb('glbc', [128, 4, 64], region='s')
            qgTP = [self.sb('qgT%d' % i, [128, 4, 128]) for i in range(2)]
            argL = gb
            tLU = self.sb('tLU', [128, 8, 128])
            DLU = tLU
            rn8 = tLU
            CxP = [self.sb('CxP%d' % i, [128, 4, 128]) for i in range(2)]
            CtP = [self.sb('CtP%d' % i, [128, 4, 128]) for i in range(2)]
            R0P = [self.sb('R0P%d' % i, [128, 4, 128]) for i in range(2)]
            XT = self.sb('XT', [128, 4, 128])
            XtT = self.sb('XtT', [128, 4, 128])
            RT = self.sb('RT', [128, 4, 128])
            aqkTP = [self.sb('aqkT%d' % i, [128, 4, 128]) for i in range(2)]
            kdP = [self.sb('kd%d' % i, [128, 4, 128]) for i in range(2)]
            kbgP = [self.sb('kbg%d' % i, [128, 4, 128]) for i in range(2)]
            vbP = [self.sb('vb%d' % i, [128, 4, 128]) for i in range(2)]
            glP = [self.sb('gl%d' % i, [128, 4]) for i in range(2)]
            u_sb = self.sb('u_sb', [128, 4, 128], region='p')
            wT = self.sb('wT', [128, 4, 128])
            vnew = self.sb('vnew', [128, 4, 128])
            vnT = self.sb('vnT', [128, 4, 64], region='s')
            oT = self.sb('oT', [128, 4, 128])
            osq = self.sb('osq', [128, 4, 128])
            Sst1 = self.sb('Sst', [128, 4, 128], region='p')
            Sst = [Sst1 for l in range(L)]
            Sq = [self.sb('Sq%d' % i, [128, 4, 128], region='s') for i in range(2)]
            vmb = [self.sb('vm%d' % i, [64, 4, 128], region='s') for i in range(2)]
            KcT = self.sb('KcT', [128, NSEQ, 128], region='s')
            VcA = self.sb('VcA', [128, NSEQ, 2, 65], BF16, region='s')
            OTs = self.sb('OTs', [65, 2, 256], region='s')

            for i in range(2):
                S.op('pool', lambda e, t=VA1[i]: e.memset(t[:, :, 64:65], 1.0), w=[VA1[i]])

            wparity = [0]

            def load_weights(l):
                p = 0
                self.dma(gqk_l[:, :], I['gqk'].t.ap()[:, l * 640:(l + 1) * 640], [I['gqk']], [gqk_l])
                for kc in range(8):
                    self.dma(Wi[p][:, kc, :], I['w_in'].t.ap()[l, kc * 128:(kc + 1) * 128, :], [I['w_in']], [Wi[p]], q='pool')
                for kc in range(8):
                    self.dma(Wo[p][:, kc, :], I['w_out'].t.ap()[l, kc * 128:(kc + 1) * 128, :], [I['w_out']], [Wo[p]], q='pool')
                return Wi[p], Wo[p]

            def mm(out, lhsT, rhs, start=True, stop=True):
                return lambda e: e.matmul(out, lhsT=lhsT, rhs=rhs, start=start, stop=stop)

            def mms(lst):
                def f(e):
                    ins = None
                    for (o, a, b, st, sp) in lst:
                        ins = e.matmul(o, lhsT=a, rhs=b, start=st, stop=sp)
                    return ins
                return f

            def trs(lst, ident):
                def f(e):
                    ins = None
                    for (o, a, idn) in lst:
                        ins = e.transpose(o, a, idn)
                    return ins
                return f

            def tile_layer(l, T, xt_ap, W_i, W_o, samp, tidx, first, last_of_seq, xbuf, record=False, post=None):
                xkeys = [xbuf]
                pp2 = 0 if samp else tidx % 2
                mixT, gateT, qgT, aqkT, kd, kbg, vb, glv = mixTP[pp2], gateTP[pp2], qgTP[pp2], aqkTP[pp2], kdP[pp2], kbgP[pp2], vbP[pp2], glP[pp2]
                if record:
                    S.lists = {'f': [], 'b': []}
                    S.cur = S.lists['f']
                    self.pool = 'f'
                QT = Buf('QT', QTraw[:, 0:4 * T].rearrange("p (a b) -> p a b", a=4))
                np_ = l * 8
                S.op('act', lambda e: e.activation(out=sqt[:, 0:8, 0:T], in_=xt_ap(None), func=AF.Square), r=xkeys, w=[sqt])
                b = self.bank()
                S.op('pe', mms([(b[:, 0:T], cm[:, ONES, :], sqt[:, kc, 0:T], kc == 0, kc == 7) for kc in range(8)]), r=[sqt, cm], w=[b])
                S.op('act', lambda e, b=b: e.activation(out=rstd[:, 0:T], in_=b[:, 0:T], func=AF.Ln, scale=1.0 / D, bias=EPS), r=[b], w=[rstd])
                S.op('act', lambda e: e.activation(out=rstd[:, 0:T], in_=rstd[:, 0:T], func=AF.Exp, scale=-0.5), r=[rstd], w=[rstd])

                def fxn(e):
                    ins = None
                    for kc in range(8):
                        ins = e.scalar_tensor_tensor(out=xnT[:, kc, 0:T], in0=xt_ap(kc), scalar=P_['normg'][:, np_ + kc:np_ + kc + 1],
                                                     in1=rstd[:, 0:T], op0=ALU.mult, op1=ALU.mult)
                    return ins
                S.op('dve', fxn, r=xkeys + [rstd, P_['normg']], w=[xnT])

                def tokproj(c0, n, dst_ops):
                    b = self.bank()
                    S.op('pe', mms([(b[0:T, 0:n], xnT[:, kc, 0:T], W_i[:, kc, c0:c0 + n], kc == 0, kc == 7) for kc in range(8)]),
                         r=[xnT, W_i], w=[b])
                    return b
                bq = tokproj(C_AQ, 512, None)
                S.op('act', lambda e, b=bq: e.copy(out=zq[0:T, 0:512].rearrange("p (j g d) -> p g j d", g=2, d=64), in_=b[0:T, 0:512].rearrange("p (g j d) -> p g j d", g=2, d=64)), r=[bq], w=[zq])
                bkv = tokproj(C_AK, 256, None)
                S.op('act', lambda e, b=bkv: e.copy(out=zq[0:T, 512:640], in_=b[0:T, 0:128]), r=[bkv], w=[zq])
                S.op('act', lambda e, b=bkv: e.copy(out=kvs[0:T, 0:128], in_=b[0:T, 128:256]), r=[bkv], w=[kvs])
                bg_ = tokproj(C_AG, 512, None)
                S.op('act', lambda e, b=bg_: e.activation(out=gat[0:T, :], in_=b[0:T, 0:512], func=AF.Silu), r=[bg_], w=[gat])
                bs = tokproj(C_DB, 8, None)
                S.op('act', lambda e, b=bs: e.copy(out=kvs[0:T, 128:136], in_=b[0:T, 0:8]), r=[bs], w=[kvs])
                if samp:
                    for j in range(3):
                        bz = tokproj(C_DQ + 512 * j, 512, None)
                        S.op('act', lambda e, b=bz, j=j: e.copy(out=cacc[0:T, 4 * j:4 * j + 4, :].rearrange("p a b -> p (a b)"), in_=b[0:T, 0:512]), r=[bz], w=[cacc])
                    for i in range(1, 4):
                        if 'nosconv' in DBG:
                            break
                        self.dma(O['sconv'].t.ap()[l, :, i - 1, :], cacc[i:T:4, :, :].rearrange("p a b -> p (a b)"), [cacc], [O['sconv']])

                if samp and STOP == 's1':
                    return
                S.op('act', lambda e: e.activation(out=tLU[0:T, 0:5, :].rearrange("p a b -> p (a b)"), in_=zq[0:T, :], func=AF.Square), r=[zq], w=[tLU])
                S.op('dve', lambda e: e.tensor_reduce(out=ss10[0:T, :], in_=tLU[0:T, 0:5, :].rearrange("p a (c b) -> p (a c) b", c=2), axis=AX.X, op=ALU.add),
                     r=[tLU], w=[ss10])
                S.op('act', lambda e: e.activation(out=ss10[0:T, :], in_=ss10[0:T, :], func=AF.Ln, scale=1.0 / 64, bias=EPS), r=[ss10], w=[ss10])
                S.op('act', lambda e: e.activation(out=ss10[0:T, :], in_=ss10[0:T, :], func=AF.Exp, scale=-0.5), r=[ss10], w=[ss10])
                S.op('dve', lambda e: e.tensor_tensor(out=qr[0:T, :, :], in0=zq[0:T, :].rearrange("p (a b) -> p a b", a=10),
                                                      in1=bc(ss10[0:T, :, None], [T, 10, 64]), op=ALU.mult), r=[zq, ss10], w=[qr])
                S.op('dve', lambda e: e.tensor_tensor(out=qr[0:T, :, :], in0=qr[0:T, :, :],
                                                      in1=gqk_l[0:T, :].rearrange("p (a b) -> p a b", a=10), op=ALU.mult),
                     r=[qr, gqk_l], w=[qr])
                if samp:
                    cos_ap = P_['coss'][0:T, :]
                    sin_ap = P_['sins'][0:T, :]
                else:
                    cos_ap = P_['cosp'][0:T, tidx * 8:(tidx + 1) * 8]
                    sin_ap = P_['sinp'][0:T, tidx * 8:(tidx + 1) * 8]
                cosb = bc(cos_ap.unsqueeze(1), [T, 10, 8])
                sinb = bc(sin_ap.unsqueeze(1), [T, 10, 8])
                x1 = qr[0:T, :, 0:8]
                x2 = qr[0:T, :, 8:16]
                S.op('dve', lambda e: e.tensor_tensor(out=rt[0][0:T], in0=x1, in1=cosb, op=ALU.mult), r=[qr, P_['cosp'], P_['coss']], w=[rt[0]])
                S.op('dve', lambda e: e.tensor_tensor(out=rt[1][0:T], in0=x2, in1=sinb, op=ALU.mult), r=[qr, P_['sinp'], P_['sins']], w=[rt[1]])
                S.op('pool', lambda e: e.tensor_tensor(out=rt[2][0:T], in0=x2, in1=cosb, op=ALU.mult), r=[qr, P_['cosp'], P_['coss']], w=[rt[2]])
                S.op('pool', lambda e: e.tensor_tensor(out=rt[3][0:T], in0=x1, in1=sinb, op=ALU.mult), r=[qr, P_['sinp'], P_['sins']], w=[rt[3]])
                S.op('dve', lambda e: e.tensor_tensor(out=x1, in0=rt[0][0:T], in1=rt[1][0:T], op=ALU.subtract), r=[rt[0], rt[1], rt[2], rt[3]], w=[qr])
                S.op('dve', lambda e: e.tensor_tensor(out=x2, in0=rt[2][0:T], in1=rt[3][0:T], op=ALU.add), r=[rt[2], rt[3]], w=[qr])
                if samp and 'noswk' in DBG:
                    pass
                elif samp:
                    for i in range(4):
                        self.dma(O['swk'].t.ap()[l, :, 124 + i, :], qr[i:T:4, 8:10, :].rearrange("p a b -> p (a b)"), [qr], [O['swk']])
                        self.dma(O['swv'].t.ap()[l, :, 124 + i, :], kvs[i:T:4, 0:128], [kvs], [O['swv']])
                    self.dma(O['swk'].t.ap()[l, :, 0:124, :], I['ck'].t.ap()[l, :, 4:128, :], [I['ck']], [O['swk']])
                    self.dma(O['swv'].t.ap()[l, :, 0:124, :], I['cv'].t.ap()[l, :, 4:128, :], [I['cv']], [O['swv']])
                elif last_of_seq:
                    self.dma(O['pwk'].t.ap()[l], qr[0:T, 8:10, :], [qr], [O['pwk']])
                    self.dma(O['pwv'].t.ap()[l], kvs[0:T, 0:128], [kvs], [O['pwv']])
                par = tidx % 2
                KTc = KTs if samp else KT[l][par]
                VAc = VAs if samp else VA[l][par]
                b = self.bank()
                S.op('pe', trs([(b[:, j * T:(j + 1) * T], qr[0:T, 2 * j:2 * j + 2, :].rearrange("p a b -> p (a b)"), cm[0:T, IDENT, 0:T]) for j in range(4)], None), r=[qr, cm], w=[b])
                S.op('act', lambda e, b=b: e.copy(out=QT[:, :, :], in_=b[:, 0:4 * T].rearrange("p (a b) -> p a b", a=4)), r=[b], w=[QT])
                b2 = self.bank()
                S.op('pe', trs([(b2[:, 0:T], qr[0:T, 8:10, :].rearrange("p a b -> p (a b)"), cm[0:T, IDENT, 0:T])], None), r=[qr, cm], w=[b2])
                S.op('act', lambda e, b=b2: e.copy(out=KTc[:, 0:T], in_=b[:, 0:T]), r=[b2], w=[KTc])
                S.op('dve', lambda e: e.tensor_copy(out=VAc[0:T, :, 0:64], in_=kvs[0:T, 0:128].rearrange("p (a b) -> p a b", a=2)), r=[kvs], w=[VAc])

                if samp and STOP == 's2':
                    return
                poA = self.bank()
                poB = self.bank()
                if not samp:
                    blocks = []
                    if not first:
                        blocks.append((KT[l][1 - par], VA[l][1 - par], MPREV))
                    blocks.append((KTc, VAc, MCUR))
                    pts = []
                    n = 0
                    for (ktb, vab, mslot) in blocks:
                        for kvh in range(2):
                            h0 = 64 * kvh
                            bsc = self.bank()
                            S.op('pe', mm(bsc[:, :], ktb[h0:h0 + 64, :], QT[h0:h0 + 64, :, :].rearrange("p a b -> p (a b)")), r=[ktb, QT], w=[bsc])
                            p = pT[n]
                            n += 1
                            S.op('act', lambda e, p=p, b=bsc: e.activation(out=p[:, :, :], in_=b[:, :].rearrange("p (a b) -> p a b", a=4),
                                                                          func=AF.Exp, scale=0.125), r=[bsc], w=[p])
                            S.op('dve', lambda e, p=p, m=mslot: e.tensor_tensor(out=p[:, :, :], in0=p[:, :, :], in1=bc(cm[:, m, :].unsqueeze(1), [128, 4, 128]),
                                                                               op=ALU.mult), r=[p, cm], w=[p])
                            pts.append((p, vab, kvh))
                    for h in range(8):
                        kvh, j = h // 4, h % 4
                        po = poA if kvh == 0 else poB
                        lst = [(p, vab) for (p, vab, kv) in pts if kv == kvh]
                        S.op('pe', mms([(po[:, j * 65:(j + 1) * 65], p[:, j, :], vab[:, kvh, :], i == 0, i == len(lst) - 1) for i, (p, vab) in enumerate(lst)]),
                             r=[x[0] for x in lst] + [x[1] for x in lst], w=[po])
                else:
                    self.dma(KcT[:, :, :], I['ckT'].t.ap()[l].rearrange("s c k -> c s k"), [I['ckT']], [KcT])
                    for a_ in range(2):
                        self.dma(VcA[:, :, a_, 0:64], I['cv'].t.ap()[l][:, :, a_ * 64:(a_ + 1) * 64].rearrange("s k c -> k s c"), [I['cv']], [VcA], q='pool')
                    pc = pT[0]
                    pp = pT[1]
                    pcf = pc[0:64, :, :].rearrange("p a b -> p (a b)")
                    ppf = pp[:, :, :].rearrange("p a b -> p (a b)")
                    for kvh in range(2):
                        bsc = self.bank()
                        S.op('pe', mm(bsc[0:64, 0:256], KTs[64 * kvh:64 * kvh + 64, 0:64], QT[64 * kvh:64 * kvh + 64, :, :].rearrange("p a b -> p (a b)")),
                             r=[KTs, QT], w=[bsc])
                        S.op('act', lambda e, b=bsc, kvh=kvh: e.activation(out=pcf[:, kvh * 256:(kvh + 1) * 256], in_=b[0:64, 0:256], func=AF.Exp, scale=0.125), r=[bsc], w=[pc])
                    S.op('dve', lambda e: e.tensor_tensor(out=pcf.rearrange("p (a b) -> p a b", a=8), in0=pcf.rearrange("p (a b) -> p a b", a=8),
                                                           in1=bc(msc[0:64, :].unsqueeze(1), [64, 8, 64]), op=ALU.mult), r=[pc, msc], w=[pc])
                    if STOP == 's2a':
                        return
                    for kvh in range(2):
                        bsp = self.bank()
                        lst = []
                        for s in range(NSEQ):
                            for j in range(4):
                                lst.append((bsp[:, j * 64 + 4 * s:j * 64 + 4 * s + 4], KcT[64 * kvh:64 * kvh + 64, s, :], QT[64 * kvh:64 * kvh + 64, j, 4 * s:4 * s + 4], True, True))
                        S.op('pe', mms(lst), r=[KcT, QT], w=[bsp])
                        S.op('act', lambda e, b=bsp, kvh=kvh: e.activation(out=ppf[:, kvh * 256:(kvh + 1) * 256], in_=b[:, 0:256], func=AF.Exp, scale=0.125), r=[bsp], w=[pp])
                    S.op('dve', lambda e: e.tensor_tensor(out=ppf.rearrange("p (a b) -> p a b", a=8), in0=ppf.rearrange("p (a b) -> p a b", a=8),
                                                           in1=bc(msp[:, :].unsqueeze(1), [128, 8, 64]), op=ALU.mult), r=[pp, msp], w=[pp])
                    if STOP == 's2b':
                        return
                    for kvh in range(2):
                        bo = self.bank()
                        lst = [(bo[0:65, 0:256], VAs[0:64, kvh, :], pcf[:, kvh * 256:(kvh + 1) * 256], True, False)]
                        for s in range(NSEQ):
                            for j in range(4):
                                c0 = j * 64 + 4 * s
                                lst.append((bo[0:65, c0:c0 + 4], VcA[:, s, kvh, :], ppf[:, kvh * 256 + c0:kvh * 256 + c0 + 4], False, (s == NSEQ - 1 and j == 3)))
                        S.op('pe', mms(lst), r=[VAs, VcA, pc, pp], w=[bo])
                        S.op('act', lambda e, b=bo, kvh=kvh: e.copy(out=OTs[:, kvh, :], in_=b[0:65, 0:256]), r=[bo], w=[OTs])
                    if STOP == 's2c':
                        return
                    for kvh in range(2):
                        po = poA if kvh == 0 else poB
                        S.op('pe', trs([(po[0:64, j * 65:(j + 1) * 65], OTs[:, kvh, j * 64:(j + 1) * 64], cm[0:65, IDENT, 0:65]) for j in range(4)], None),
                             r=[OTs, cm], w=[po])
                if samp and STOP == 's2d':
                    return
                for i, po in enumerate((poA, poB)):
                    S.op('dve', lambda e, po=po, i=i: e.tensor_tensor(out=den[0:T, 4 * i:4 * i + 4], in0=po[0:T, 0:260].rearrange("p (a b) -> p a b", a=4)[:, :, 64],
                                                                      in1=esink[0:T, l * 8 + 4 * i:l * 8 + 4 * i + 4], op=ALU.add), r=[po, esink], w=[den])
                S.op('dve', lambda e: e.reciprocal(out=den[0:T, :], in_=den[0:T, :]), r=[den], w=[den])
                for i, po in enumerate((poA, poB)):
                    S.op('dve', lambda e, po=po, i=i: e.tensor_tensor(out=att[0:T, 4 * i:4 * i + 4, :], in0=po[0:T, 0:260].rearrange("p (a b) -> p a b", a=4)[:, :, 0:64],
                                                                      in1=bc(den[0:T, 4 * i:4 * i + 4, None], [T, 4, 64]), op=ALU.mult), r=[po, den], w=[att])
                S.op('dve', lambda e: e.tensor_tensor(out=attg[0:T, :], in0=att[0:T, :, :].rearrange("p a b -> p (a b)"), in1=gat[0:T, :], op=ALU.mult),
                     r=[att, gat], w=[attg])
                S.op('pe', trs([(self.psb[:, j * T:(j + 1) * T], attg[0:T, j * 128:(j + 1) * 128], identb[0:T, 0:T]) for j in range(4)], None),
                     r=[attg, identb], w=[self.psb])
                S.op('act', lambda e: e.copy(out=mixT[:, 0:4, 0:T], in_=self.psb[:, 0:4 * T].rearrange("p (a b) -> p a b", a=4)), r=[self.psb], w=[mixT])

                if samp and STOP == 's3':
                    return
                if samp:
                    self.dma(scst[:, :, :], I['scT'].t.ap()[l].rearrange("(c p) f -> p c f", p=128), [I['scT']], [scst])
                    S.op('pool', lambda e: e.tensor_copy(out=zdTs[:, :, :, 0:3], in_=scst[:, :, :].rearrange("p c (s j) -> p c s j", j=3)), r=[scst], w=[zdTs])
                else:
                    S.op('pool', lambda e: e.tensor_copy(out=zdT[:, :, 0:3], in_=convst[l][:, :, :]), r=[convst[l]], w=[zdT])
                for rnd in range(4):
                    b = self.bank()
                    lst = []
                    for cc in range(4):
                        c = rnd * 4 + cc
                        col = (C_DQ + 128 * c) if c < 12 else (C_DG + 128 * (c - 12))
                        for kc in range(8):
                            lst.append((b[:, cc * T:(cc + 1) * T], W_i[:, kc, col:col + 128], xnT[:, kc, 0:T], kc == 0, kc == 7))
                    S.op('pe', mms(lst), r=[W_i, xnT], w=[b])
                    bv = b[:, 0:4 * T].rearrange("p (a b) -> p a b", a=4)
                    if rnd < 3:
                        if samp:
                            S.op('act', lambda e, bv=bv, rnd=rnd: e.copy(out=zdTs[:, 4 * rnd:4 * rnd + 4, :, 3:7],
                                                                          in_=bv.rearrange("p a (s j) -> p a s j", j=4)), r=[b], w=[zdTs])
                        else:
                            S.op('act', lambda e, bv=bv, rnd=rnd: e.copy(out=zdT[:, 4 * rnd:4 * rnd + 4, 3:3 + T], in_=bv), r=[b], w=[zdT])
                    else:
                        S.op('act', lambda e, bv=bv: e.activation(out=gateT[:, :, 0:T], in_=bv, func=AF.Silu), r=[b], w=[gateT])
                if not samp:
                    S.op('pool', lambda e: e.tensor_copy(out=convst[l][:, :, :], in_=zdT[:, :, T:T + 3]), r=[zdT], w=[convst[l]])
                cw0 = l * 48

                def tap(c, j):
                    if samp:
                        return zdTs[:, c, :, j:j + 4]
                    return zdT[:, c, j:j + T]

                def acc(c):
                    if samp:
                        return cacc[:, c, 0:T].rearrange("p (s j) -> p s j", j=4)
                    return cacc[:, c, 0:T]
                zk = [zdTs] if samp else [zdT]

                def fconv0(e):
                    ins = None
                    for c in range(12):
                        ins = e.activation(out=acc(c), in_=tap(c, 3), func=AF.Copy, scale=P_['convw'][:, cw0 + 36 + c:cw0 + 37 + c])
                    return ins
                S.op('act', fconv0, r=zk + [P_['convw']], w=[cacc])
                for j in range(3):
                    def fconv(e, j=j):
                        ins = None
                        for c in range(12):
                            ins = e.scalar_tensor_tensor(out=acc(c), in0=tap(c, j), scalar=P_['convw'][:, cw0 + 12 * j + c:cw0 + 12 * j + c + 1],
                                                         in1=acc(c), op0=ALU.mult, op1=ALU.add)
                        return ins
                    S.op('dve', fconv, r=zk + [P_['convw'], cacc], w=[cacc])
                S.op('act', lambda e: e.activation(out=cacc[:, :, 0:T], in_=cacc[:, :, 0:T], func=AF.Silu), r=[cacc], w=[cacc])
                S.op('act', lambda e: e.activation(out=qkn[:, 0:8, 0:T], in_=cacc[:, 0:8, 0:T], func=AF.Square), r=[cacc], w=[qkn])
                for half in range(2):
                    b = self.bank()
                    S.op('pe', mms([(b[:, cc * T:(cc + 1) * T], cm[:, ONES, :], qkn[:, 4 * half + cc, 0:T], True, True) for cc in range(4)]), r=[qkn, cm], w=[b])
                    sc, bi_ = (128.0, 128.0 * EPS) if half == 0 else (1.0, EPS)
                    S.op('act', lambda e, b=b, half=half, sc=sc, bi_=bi_: e.activation(out=rn8[:, 4 * half:4 * half + 4, 0:T],
                                                                                        in_=b[:, 0:4 * T].rearrange("p (a b) -> p a b", a=4),
                                                                                        func=AF.Ln, scale=sc, bias=bi_), r=[b], w=[rn8])
                S.op('act', lambda e: e.activation(out=rn8[:, 0:8, 0:T], in_=rn8[:, 0:8, 0:T], func=AF.Exp, scale=-0.5), r=[rn8], w=[rn8])
                S.op('dve', lambda e: e.tensor_tensor(out=qkn[:, :, 0:T], in0=cacc[:, 0:8, 0:T], in1=rn8[:, 0:8, 0:T], op=ALU.mult), r=[cacc, rn8], w=[qkn])
                if samp and l == 0:
                    self.dump('ycv', cacc[:, :, 0:T], [128, 12, T], [cacc])
                    self.dump('qkn', qkn[:, :, 0:T], [128, 8, T], [qkn])
                if samp and STOP == 's4':
                    return
                db = kvs[0:T, 128:132]
                da = kvs[0:T, 132:136]
                S.op('act', lambda e: e.activation(out=sc4['beta'][0:T, :], in_=db, func=AF.Sigmoid), r=[kvs], w=[sc4['beta']])
                S.op('pool', lambda e: e.tensor_scalar(out=sc4['nbeta'][0:T, :], in0=sc4['beta'][0:T, :], scalar1=-1.0, scalar2=None, op0=ALU.mult),
                     r=[sc4['beta']], w=[sc4['nbeta']])
                S.op('dve', lambda e: e.tensor_tensor(out=sc4['sp'][0:T, :], in0=da, in1=P_['dtbb'][0:T, l * 4:l * 4 + 4], op=ALU.add), r=[kvs, P_['dtbb']], w=[sc4['sp']])
                S.op('act', lambda e: e.activation(out=sc4['sp'][0:T, :], in_=sc4['sp'][0:T, :], func=AF.Exp), r=[sc4['sp']], w=[sc4['sp']])
                S.op('act', lambda e: e.activation(out=sc4['sp'][0:T, :], in_=sc4['sp'][0:T, :], func=AF.Ln, bias=1.0), r=[sc4['sp']], w=[sc4['sp']])
                S.op('dve', lambda e: e.tensor_tensor(out=sc4['g'][0:T, :], in0=sc4['sp'][0:T, :], in1=negA[0:T, l * 4:l * 4 + 4], op=ALU.mult), r=[sc4['sp'], negA], w=[sc4['g']])
                um, vm = (US, VS) if samp else (UM, ONES)
                nml, nmu = (NMLS, NMUS) if samp else (NML, NMU)
                b = self.bank()
                S.op('pe', mms([(b[0:T, 0:4], cm[0:T, um, 0:T], sc4['g'][0:T, :], True, True),
                                (b[0:T, 4:8], cm[0:T, vm, 0:T], sc4['g'][0:T, :], True, True)]), r=[cm, sc4['g']], w=[b])
                S.op('act', lambda e, b=b: e.copy(out=sc4['Gc'][0:T, :], in_=b[0:T, 0:4]), r=[b], w=[sc4['Gc']])
                S.op('act', lambda e, b=b: e.copy(out=sc4['Gt'][0:T, :], in_=b[0:T, 4:8]), r=[b], w=[sc4['Gt']])
                S.op('act', lambda e: e.activation(out=sc4['eG'][0:T, :], in_=sc4['Gc'][0:T, :], func=AF.Exp), r=[sc4['Gc']], w=[sc4['eG']])
                S.op('dve', lambda e: e.tensor_tensor(out=sc4['bg'][0:T, :], in0=sc4['beta'][0:T, :], in1=sc4['eG'][0:T, :], op=ALU.mult), r=[sc4['beta'], sc4['eG']], w=[sc4['bg']])
                S.op('dve', lambda e: e.tensor_tensor(out=sc4['ekd'][0:T, :], in0=sc4['Gt'][0:T, :], in1=sc4['Gc'][0:T, :], op=ALU.subtract), r=[sc4['Gt'], sc4['Gc']], w=[sc4['ekd']])
                S.op('act', lambda e: e.activation(out=sc4['ekd'][0:T, :], in_=sc4['ekd'][0:T, :], func=AF.Exp), r=[sc4['ekd']], w=[sc4['ekd']])
                S.op('act', lambda e: e.activation(out=glv[0:T, :], in_=sc4['Gt'][0:T, :], func=AF.Exp), r=[sc4['Gt']], w=[glv])
                S.op('dve', lambda e: e.tensor_copy(out=gb[0:T, :, :], in_=bc(sc4['g'][0:T, :, None], [T, 4, 128])), r=[sc4['g']], w=[gb])
                bG = self.bank()
                S.op('pe', mms([(bG[:, h * T:(h + 1) * T], gb[0:T, h, :], cm[0:T, um, 0:T], True, True) for h in range(4)]), r=[gb, cm], w=[bG])
                bGv = bG[:, 0:4 * T].rearrange("p (a b) -> p a b", a=4)
                S.op('act', lambda e: e.activation(out=tLU[:, 0:4, 0:T], in_=bGv, func=AF.Exp), r=[bG], w=[tLU])
                S.op('dve', lambda e: e.tensor_tensor(out=qgT[:, :, 0:T], in0=qkn[:, 0:4, 0:T], in1=tLU[:, 0:4, 0:T], op=ALU.mult), r=[qkn, tLU], w=[qgT])
                if samp:
                    bGt = self.bank()
                    S.op('pe', mms([(bGt[:, h * T:(h + 1) * T], gb[0:T, h, :], cm[0:T, vm, 0:T], True, True) for h in range(4)]), r=[gb, cm], w=[bGt])
                    S.op('act', lambda e: e.activation(out=glbc[:, :, 0:T], in_=bGt[:, 0:4 * T].rearrange("p (a b) -> p a b", a=4), func=AF.Exp), r=[bGt], w=[glbc])
                S.op('dve', lambda e: e.tensor_tensor(out=argL[0:T, :, 0:T], in0=bc(sc4['Gc'][0:T, :, None], [T, 4, T]), in1=bGv[0:T], op=ALU.subtract),
                     r=[sc4['Gc'], bG], w=[argL])
                S.op('dve', lambda e: e.tensor_tensor(out=tLU[0:T, 0:4, 0:T], in0=argL[0:T, :, 0:T], in1=bc(cm[0:T, nml, 0:T].unsqueeze(1), [T, 4, T]), op=ALU.add),
                     r=[argL, cm], w=[tLU.k(0)])
                S.op('dve', lambda e: e.tensor_tensor(out=tLU[0:T, 4:8, 0:T], in0=bc(cm[0:T, nmu, 0:T].unsqueeze(1), [T, 4, T]), in1=argL[0:T, :, 0:T], op=ALU.subtract),
                     r=[argL, cm], w=[tLU.k(0)])
                S.op('act', lambda e: e.activation(out=DLU[0:T, :, 0:T], in_=tLU[0:T, :, 0:T], func=AF.Exp), r=[tLU], w=[DLU])
                bKK = self.bank()
                S.op('pe', mms([(bKK[0:T, h * T:(h + 1) * T], qkn[:, 4 + h, 0:T], qkn[:, 4 + h, 0:T], True, True) for h in range(4)]), r=[qkn], w=[bKK])
                bQK = self.bank()
                S.op('pe', mms([(bQK[0:T, h * T:(h + 1) * T], qkn[:, 4 + h, 0:T], qkn[:, h, 0:T], True, True) for h in range(4)]), r=[qkn], w=[bQK])
                Ct = CtP[pp2]
                S.op('dve', lambda e: e.tensor_tensor(out=Ct[0:T, :, 0:T], in0=bKK[0:T, 0:4 * T].rearrange("p (a b) -> p a b", a=4), in1=DLU[0:T, 0:4, 0:T], op=ALU.mult),
                     r=[bKK, DLU], w=[Ct])
                S.op('dve', lambda e: e.tensor_tensor(out=Ct[0:T, :, 0:T], in0=Ct[0:T, :, 0:T], in1=bc(sc4['nbeta'][0:T, :, None], [T, 4, T]), op=ALU.mult),
                     r=[Ct, sc4['nbeta']], w=[Ct])
                S.op('dve', lambda e: e.tensor_tensor(out=aqkT[0:T, :, 0:T], in0=bQK[0:T, 0:4 * T].rearrange("p (a b) -> p a b", a=4), in1=DLU[0:T, 4:8, 0:T], op=ALU.mult),
                     r=[bQK, DLU], w=[aqkT])
                bC = self.bank()
                S.op('pe', trs([(bC[0:T, h * T:(h + 1) * T], Ct[0:T, h, 0:T], cm[0:T, IDENT, 0:T]) for h in range(4)], None), r=[Ct, cm], w=[bC])
                Cx = CxP[pp2]
                S.op('act', lambda e: e.copy(out=Cx[0:T, :, 0:T], in_=bC[0:T, 0:4 * T].rearrange("p (a b) -> p a b", a=4)), r=[bC], w=[Cx])
                R = R0P[pp2]
                S.op('dve', lambda e, R=R: e.tensor_tensor(out=R[0:T, :, 0:T], in0=Cx[0:T, :, 0:T], in1=bc(cm[0:T, IDENT, 0:T].unsqueeze(1), [T, 4, T]), op=ALU.add),
                     r=[Cx, cm], w=[R])
                bk = self.bank()
                S.op('pe', trs([(bk[0:T, h * 128:(h + 1) * 128], qkn[:, 4 + h, 0:T], cm[:, IDENT, :]) for h in range(4)], None), r=[qkn, cm], w=[bk])
                bv_ = self.bank()
                S.op('pe', trs([(bv_[0:T, h * 128:(h + 1) * 128], cacc[:, 8 + h, 0:T], cm[:, IDENT, :]) for h in range(4)], None), r=[cacc, cm], w=[bv_])
                bkv4 = bk[0:T, :].rearrange("p (a b) -> p a b", a=4)
                S.op('dve', lambda e: e.tensor_tensor(out=kd[0:T], in0=bkv4, in1=bc(sc4['ekd'][0:T, :, None], [T, 4, 128]), op=ALU.mult), r=[bk, sc4['ekd']], w=[kd])
                S.op('dve', lambda e: e.tensor_tensor(out=kbg[0:T], in0=bkv4, in1=bc(sc4['bg'][0:T, :, None], [T, 4, 128]), op=ALU.mult), r=[bk, sc4['bg']], w=[kbg])
                S.op('dve', lambda e: e.tensor_tensor(out=vb[0:T], in0=bv_[0:T, :].rearrange("p (a b) -> p a b", a=4), in1=bc(sc4['beta'][0:T, :, None], [T, 4, 128]), op=ALU.mult),
                     r=[bv_, sc4['beta']], w=[vb])
                if record:
                    S.cur = S.lists['b']
                    self.pool = 'b'
                nlev = 1 if samp else 6
                X, Xt = Cx, Ct
                own = (CxP[pp2], CtP[pp2], R0P[pp2])
                alt = (XT, XtT, RT)
                for lev in range(nlev):
                    lastlev = lev == nlev - 1
                    Xn, Xtn, Rn = alt if lev % 2 == 0 else own
                    bXt = self.bank()
                    S.op('pe', mms([(bXt[0:T, h * T:(h + 1) * T], X[0:T, h, 0:T], Xt[0:T, h, 0:T], True, True) for h in range(4)]), r=[X, Xt], w=[bXt])
                    if not lastlev:
                        bX = self.bank()
                        S.op('pe', mms([(bX[0:T, h * T:(h + 1) * T], Xt[0:T, h, 0:T], X[0:T, h, 0:T], True, True) for h in range(4)]), r=[X, Xt], w=[bX])
                    S.op('act', lambda e, Xtn=Xtn, b=bXt: e.copy(out=Xtn[0:T, :, 0:T], in_=b[0:T, 0:4 * T].rearrange("p (a b) -> p a b", a=4)), r=[bXt], w=[Xtn])
                    if not lastlev:
                        S.op('act', lambda e, Xn=Xn, b=bX: e.copy(out=Xn[0:T, :, 0:T], in_=b[0:T, 0:4 * T].rearrange("p (a b) -> p a b", a=4)), r=[bX], w=[Xn])
                    bR = self.bank()
                    S.op('pe', mms([(bR[0:T, h * T:(h + 1) * T], Xtn[0:T, h, 0:T], R[0:T, h, 0:T], True, True) for h in range(4)]), r=[Xtn, R], w=[bR])
                    S.op('dve', lambda e, Rn=Rn, R=R, b=bR: e.tensor_tensor(out=Rn[0:T, :, 0:T], in0=R[0:T, :, 0:T], in1=b[0:T, 0:4 * T].rearrange("p (a b) -> p a b", a=4), op=ALU.add),
                         r=[R, bR], w=[Rn])
                    R = Rn
                    X, Xt = Xn, Xtn
                if samp and STOP == 's5':
                    return
                if samp and l == 0:
                    self.dump('g', sc4['g'][0:T, :], [T, 4], [sc4['g']])
                    self.dump('beta', sc4['beta'][0:T, :], [T, 4], [sc4['beta']])
                    self.dump('Gc', sc4['Gc'][0:T, :], [T, 4], [sc4['Gc']])
                    self.dump('Gt', sc4['Gt'][0:T, :], [T, 4], [sc4['Gt']])
                    self.dump('R', R[0:T, :, 0:T], [T, 4, T], [R])
                    self.dump('DLU', DLU[0:T, :, 0:T], [T, 8, T], [DLU])
                    self.dump('aqkT', aqkT[0:T, :, 0:T], [T, 4, T], [aqkT])
                bw = self.bank()
                S.op('pe', mms([(bw[:, h * T:(h + 1) * T], kbg[0:T, h, :], R[0:T, h, 0:T], True, True) for h in range(4)]), r=[kbg, R], w=[bw])
                S.op('act', lambda e: e.copy(out=wT[:, :, 0:T], in_=bw[:, 0:4 * T].rearrange("p (a b) -> p a b", a=4)), r=[bw], w=[wT])

                if not samp:
                    bu = self.bank()
                    S.op('pe', mms([(bu[0:T, h * 128:(h + 1) * 128], R[0:T, h, 0:T], vb[0:T, h, :], True, True) for h in range(4)]), r=[vb, R], w=[bu])
                    S.op('act', lambda e: e.copy(out=u_sb[0:T], in_=bu[0:T, :].rearrange("p (a b) -> p a b", a=4)), r=[bu], w=[u_sb])
                    St = Sst[l]
                    bws = self.bank()
                    S.op('pe', mms([(bws[0:T, h * 128:(h + 1) * 128], wT[:, h, 0:T], St[:, h, :], True, True) for h in range(4)]), r=[wT, St], w=[bws])
                    S.op('dve', lambda e: e.tensor_tensor(out=vnew[0:T], in0=u_sb[0:T], in1=bws[0:T, :].rearrange("p (a b) -> p a b", a=4), op=ALU.subtract), r=[u_sb, bws], w=[vnew])
                    bo = self.bank()
                    lst = []
                    for h in range(4):
                        lst.append((bo[:, h * T:(h + 1) * T], St[:, h, :], qgT[:, h, 0:T], True, False))
                        lst.append((bo[:, h * T:(h + 1) * T], vnew[0:T, h, :], aqkT[0:T, h, 0:T], False, True))
                    S.op('pe', mms(lst), r=[St, qgT, vnew, aqkT], w=[bo])
                    S.op('act', lambda e: e.copy(out=oT[:, :, 0:T], in_=bo[:, 0:4 * T].rearrange("p (a b) -> p a b", a=4)), r=[bo], w=[oT])
                    bS = self.bank()
                    S.op('pe', mms([(bS[:, h * 128:(h + 1) * 128], kd[0:T, h, :], vnew[0:T, h, :], True, True) for h in range(4)]), r=[kd, vnew], w=[bS])

                    def fS(e):
                        ins = None
                        for h in range(4):
                            ins = e.scalar_tensor_tensor(out=St[:, h, :], in0=St[:, h, :], scalar=glv[:, h:h + 1], in1=bS[:, h * 128:(h + 1) * 128],
                                                         op0=ALU.mult, op1=ALU.add)
                        return ins
                    S.op('dve', fS, r=[St, glv, bS], w=[St])
                    if last_of_seq:
                        self.dma(O['pdelta'].t.ap()[l].rearrange("h k v -> k h v"), St[:, :, :], [St], [O['pdelta']])
                        self.dma(O['pconv'].t.ap()[l], convst[l][:, :, :].rearrange("p a b -> p (a b)"), [convst[l]], [O['pconv']])
                else:
                    bu = self.bank()
                    S.op('pe', mms([(bu[:, h * T:(h + 1) * T], vb[0:T, h, :], R[0:T, h, 0:T], True, True) for h in range(4)]), r=[vb, R], w=[bu])
                    S.op('act', lambda e: e.copy(out=vnT[:, :, 0:T], in_=bu[:, 0:4 * T].rearrange("p (a b) -> p a b", a=4)), r=[bu], w=[vnT])
                    bws = self.bank()
                    for s in range(NSEQ):
                        sq_ = Sq[s % 2]
                        self.dma(sq_[:, :, :], I['sd'].t.ap()[l, s].rearrange("h k v -> k h v"), [I['sd']], [sq_])
                        S.op('pe', mms([(bws[:, h * T + 4 * s:h * T + 4 * s + 4], sq_[:, h, :], wT[:, h, 4 * s:4 * s + 4], True, True) for h in range(4)]), r=[sq_, wT], w=[bws])
                    S.op('dve', lambda e: e.tensor_tensor(out=vnT[:, :, 0:T], in0=vnT[:, :, 0:T], in1=bws[:, 0:4 * T].rearrange("p (a b) -> p a b", a=4), op=ALU.subtract),
                         r=[vnT, bws], w=[vnT])
                    bvt = self.bank()
                    S.op('pe', trs([(bvt[0:T, h * 128:(h + 1) * 128], vnT[:, h, 0:T], cm[:, IDENT, :]) for h in range(4)], None), r=[vnT, cm], w=[bvt])
                    S.op('act', lambda e: e.copy(out=vnew[0:T], in_=bvt[0:T, :].rearrange("p (a b) -> p a b", a=4)), r=[bvt], w=[vnew])
                    bo = self.bank_long
                    bo2 = self.bank()
                    lst = [(bo2[:, h * T:(h + 1) * T], vnew[0:T, h, :], aqkT[0:T, h, 0:T], True, True) for h in range(4)]
                    S.op('pe', mms(lst), r=[vnew, aqkT], w=[bo2])
                    S.op('act', lambda e: e.copy(out=osq[:, :, 0:T], in_=bo2[:, 0:4 * T].rearrange("p (a b) -> p a b", a=4)), r=[bo2], w=[osq])
                    for s in range(NSEQ):
                        sq_ = Sq[s % 2]
                        self.dma(sq_[:, :, :], I['sd'].t.ap()[l, s].rearrange("h k v -> k h v"), [I['sd']], [sq_])
                        S.op('pe', mms([(bo[:, h * T + 4 * s:h * T + 4 * s + 4], sq_[:, h, :], qgT[:, h, 4 * s:4 * s + 4], True, True) for h in range(4)]), r=[sq_, qgT], w=[bo])
                        vms = vmb[s % 2]
                        S.op('dve', lambda e, s=s, vms=vms: e.tensor_scalar(out=vms[0:T, :, :], in0=vnew[0:T, :, :], scalar1=cm[0:T, VS, 4 * s:4 * s + 1], scalar2=None, op0=ALU.mult),
                             r=[vnew, cm], w=[vms])
                        bS = self.bank()
                        S.op('pe', mms([(bS[:, h * 128:(h + 1) * 128], kd[0:T, h, :], vms[0:T, h, :], True, True) for h in range(4)]), r=[kd, vms], w=[bS])

                        def fS(e, s=s, sq_=sq_, bS=bS):
                            ins = None
                            for h in range(4):
                                ins = e.scalar_tensor_tensor(out=sq_[:, h, :], in0=sq_[:, h, :], scalar=glbc[:, h, 4 * s:4 * s + 1], in1=bS[:, h * 128:(h + 1) * 128],
                                                             op0=ALU.mult, op1=ALU.add)
                            return ins
                        S.op('dve', fS, r=[sq_, glbc, bS], w=[sq_])
                        self.dma(O['sdelta'].t.ap()[l, s].rearrange("h k v -> k h v"), sq_[:, :, :], [sq_], [O['sdelta']])
                    S.op('dve', lambda e: e.tensor_tensor(out=oT[:, :, 0:T], in0=osq[:, :, 0:T], in1=bo[:, 0:4 * T].rearrange("p (a b) -> p a b", a=4), op=ALU.add), r=[bo, osq], w=[oT])

                if samp and STOP == 's6':
                    return
                if samp and l == 0:
                    self.dump('oT', oT[:, :, 0:T], [128, 4, T], [oT])
                    self.dump('vnew', vnew[0:T, :, :], [T, 4, 128], [vnew])
                    self.dump('kd', kd[0:T, :, :], [T, 4, 128], [kd])
                    self.dump('wT', wT[:, :, 0:T], [128, 4, T], [wT])
                S.op('act', lambda e: e.activation(out=osq[:, :, 0:T], in_=oT[:, :, 0:T], func=AF.Square), r=[oT], w=[osq])
                b = self.bank()
                S.op('pe', mms([(b[:, h * T:(h + 1) * T], cm[:, ONES, :], osq[:, h, 0:T], True, True) for h in range(4)]), r=[osq, cm], w=[b])
                S.op('act', lambda e, b=b: e.activation(out=osq[:, :, 0:T], in_=b[:, 0:4 * T].rearrange("p (a b) -> p a b", a=4), func=AF.Ln, scale=1.0 / 128, bias=EPS),
                     r=[b], w=[osq])
                S.op('act', lambda e: e.activation(out=osq[:, :, 0:T], in_=osq[:, :, 0:T], func=AF.Exp, scale=-0.5), r=[osq], w=[osq])
                S.op('dve', lambda e: e.tensor_tensor(out=oT[:, :, 0:T], in0=oT[:, :, 0:T], in1=osq[:, :, 0:T], op=ALU.mult), r=[oT, osq], w=[oT])
                S.op('dve', lambda e: e.scalar_tensor_tensor(out=mixT[:, 4:8, 0:T], in0=oT[:, :, 0:T], scalar=P_['dng'][:, l:l + 1], in1=gateT[:, :, 0:T],
                                                             op0=ALU.mult, op1=ALU.mult), r=[oT, gateT, P_['dng']], w=[mixT])
                for half in range(2):
                    b = self.bank()
                    lst = []
                    for mc in range(4):
                        m = half * 4 + mc
                        for ec in range(8):
                            lst.append((b[:, mc * T:(mc + 1) * T], W_o[:, ec, m * 128:(m + 1) * 128], mixT[:, ec, 0:T], ec == 0, ec == 7))
                    S.op('pe', mms(lst), r=[W_o, mixT], w=[b])

                    def fres(e, b=b, half=half):
                        ins = None
                        for mc in range(4):
                            xa = xt_ap(half * 4 + mc)
                            ins = e.tensor_tensor(out=xa, in0=xa, in1=b[:, mc * T:(mc + 1) * T], op=ALU.add)
                        return ins
                    S.op('dve', fres, r=xkeys + [b], w=xkeys)

                if post is not None:
                    post()
                if record:
                    lists = S.lists
                    S.cur = None
                    S.lists = None
                    self.pool = 'all'
                    return lists

            ypk = Buf('d_ypT', O['ypT'].t, nsub=NT)
            for l in range(L):
                W_i, W_o = load_weights(l)
                S.op('pool', lambda e: e.memset(Sst1[:, :, :], 0.0), w=[Sst1])
                S.op('pool', lambda e: e.memset(convst1[:, :, :], 0.0), w=[convst1])
                src = I['xpT'] if l == 0 else ypk

                def ld(t):
                    xb = xTb[t % 3]
                    rk = [I['xpT']] if l == 0 else ypk.k(t)
                    self.dma(xb[:, :, :], src.t.ap()[:, t * 128:(t + 1) * 128].rearrange("(c p) t -> p c t", p=128), rk, [xb])
                ld(0)
                prevB = []
                for t in range(NT):
                    if t + 1 < NT:
                        ld(t + 1)
                    xb = xTb[t % 3]

                    def xt_ap(kc, xb=xb):
                        if kc is None:
                            return xb[:, :, :]
                        return xb[:, kc, :]

                    def post(t=t, xb=xb):
                        self.dma(O['ypT'].t.ap()[:, t * 128:(t + 1) * 128].rearrange("(c p) t -> p c t", p=128), xb[:, :, :], [xb], ypk.k(t))
                    lists = tile_layer(l, 128, xt_ap, W_i, W_o, False, t, t == 0, t == NT - 1, xb, record=True, post=post)
                    interleave(S, lists['f'], prevB)
                    prevB = lists['b']
                interleave(S, [], prevB)
            import os
            DO_SAMPLE = os.environ.get('K_NOSAMPLE') is None
            if not DO_SAMPLE:
                S.emit()
                return nc
            scr = self.sb('barscr', [128, 8])
            S.barrier({'act': lambda e: e.activation(out=scr[:, 0:2], in_=cm[:, ONES, 0:2], func=AF.Copy),
                       'dve': lambda e: e.memset(scr[:, 2:4], 0.0),
                       'pool': lambda e: e.memset(scr[:, 4:6], 0.0)})
            S.op('pool', lambda e: e.memset(VAs[:, :, 64:65], 1.0), w=[VAs])
            S.op('pool', lambda e: e.memset(VcA[:, :, :, 64:65], 1.0), w=[VcA])
            S.op('pool', lambda e: e.tensor_copy(out=msp[:, :], in_=cm[:, NMLS, 64:128]), r=[cm], w=[msp])
            S.op('pool', lambda e: e.tensor_copy(out=msc[:, :], in_=cm[:, NMUS, 64:128]), r=[cm], w=[msc])
            for kc in range(8):
                self.dma(xsT[:, kc, :], I['xsT'].t.ap()[kc * 128:(kc + 1) * 128, :], [I['xsT']], [xsT])
            for l in range(L):
                W_i, W_o = load_weights(l)

                def xs_ap(kc):
                    if kc is None:
                        return xsT[:, :, :]
                    return xsT[:, kc, :]
                tile_layer(l, TS, xs_ap, W_i, W_o, True, 0, False, False, xsT)
            for kc in range(8):
                self.dma(O['ysT'].t.ap()[kc * 128:(kc + 1) * 128, :], xsT[:, kc, :], [xsT], [O['ysT']])
            S.emit()
        return nc


def _consts(NT, NSEQ, PAST):
    i = np.arange(128)
    cm = np.zeros((128, 11, 128), np.float32)
    cm[:, 0] = 1.0
    cm[:, 1] = np.eye(128)
    cm[:, 2] = (i[:, None] <= i[None, :])
    cm[:, 3] = np.where(i[:, None] > i[None, :], 0.0, NEG)
    cm[:, 4] = np.where(i[None, :] >= i[:, None], 0.0, NEG)
    cm[:, 5] = (i[:, None] > i[None, :])
    cm[:, 6] = (i[:, None] <= i[None, :])
    j = np.arange(64)
    same = (j[:, None] // 4) == (j[None, :] // 4)
    cm[:64, 7, :64] = same & (j[:, None] <= j[None, :])
    cm[:64, 8, :64] = same
    cm[:64, 9, :64] = np.where(same & (j[:, None] > j[None, :]), 0.0, NEG)
    cm[:64, 10, :64] = np.where(same & (j[None, :] >= j[:, None]), 0.0, NEG)
    cm[:, 9, 64:128] = (i[:, None] > (j[None, :] % 4))
    cm[:64, 10, 64:128] = same & (j[:, None] <= j[None, :])
    half = 8
    inv = np.power(np.float32(500000.0), -np.arange(half, dtype=np.float32) / half).astype(np.float32)
    pos = (np.arange(NT)[None, :] * 128 + i[:, None]).astype(np.float32)
    ang = pos[:, :, None] * inv[None, None, :]
    cosp = np.cos(ang).astype(np.float32).reshape(128, NT * 8)
    sinp = np.sin(ang).astype(np.float32).reshape(128, NT * 8)
    poss = (PAST + (i % 4)).astype(np.float32)
    angs = poss[:, None] * inv[None, :]
    return cm.reshape(128, 11 * 128), cosp, sinp, np.cos(angs).astype(np.float32), np.sin(angs).astype(np.float32)


_CACHE = {}


def run(inputs, NT, L, G, NSEQ, PAST, ncores, nprompt):
    key = (NT, L, G, NSEQ)
    if key not in _CACHE:
        _CACHE[key] = Builder(NT, L, G, NSEQ).build()
    nc = _CACHE[key]
    f = lambda a: np.ascontiguousarray(np.asarray(a, dtype=np.float32))
    x_prompt = f(inputs['x_prompt'])
    x_sample = f(inputs['x_sample'])
    cm, cosp, sinp, coss, sins = _consts(NT, NSEQ, PAST)
    bcast = lambda a: f(np.broadcast_to(np.asarray(a, np.float32).reshape(1, -1), (128, a.size)))
    shared = {
        'w_in': f(inputs['w_in']), 'w_out': f(inputs['w_out']),
        'normg': f(np.asarray(inputs['norm_g']).reshape(L, 8, 128).transpose(2, 0, 1).reshape(128, L * 8)),
        'convw': f(np.asarray(inputs['conv_w']).reshape(L, 4, 12, 128).transpose(3, 0, 1, 2).reshape(128, L * 48)),
        'dng': f(np.asarray(inputs['dn_norm_g']).T),
        'gqk': bcast(np.concatenate([np.tile(np.asarray(inputs['q_norm_g']), (1, 8)), np.tile(np.asarray(inputs['k_norm_g']), (1, 2))], axis=1)),
        'sinkb': bcast(np.asarray(inputs['sinks'])), 'alogb': bcast(np.asarray(inputs['a_log'])), 'dtbb': bcast(np.asarray(inputs['dt_bias'])),
        'cosp': cosp, 'sinp': sinp, 'coss': coss, 'sins': sins, 'cmat': cm,
    }
    ck = f(inputs['cache_win_k']).reshape(L, -1, 128, 128)
    cv = f(inputs['cache_win_v']).reshape(L, -1, 128, 128)
    sc = f(inputs['state_conv'])
    sd = f(inputs['state_delta'])
    in_maps = []
    for c in range(ncores):
        b = c % nprompt
        sl = slice(c * NSEQ, (c + 1) * NSEQ)
        m = dict(shared)
        m['xpT'] = f(x_prompt[b].T)
        m['xsT'] = f(x_sample[sl].reshape(NSEQ * 4, D).T)
        m['ck'] = f(ck[:, sl])
        m['ckT'] = f(ck[:, sl].transpose(0, 1, 3, 2))
        m['cv'] = f(cv[:, sl])
        m['scT'] = f(sc[:, sl].transpose(0, 3, 1, 2).reshape(L, 1536, NSEQ * 3))
        m['sd'] = f(sd[:, sl])
        in_maps.append(m)
    res = run_bass_kernel_spmd(nc, in_maps, core_ids=list(range(ncores)))
    R = res.results
    global LAST
    LAST = R
    TP = NT * 128
    y_prompt = np.stack([R[b]['ypT'].T for b in range(nprompt)])
    y_sample = np.concatenate([R[c]['ysT'].T.reshape(NSEQ, 4, D) for c in range(ncores)], 0)
    pwk = np.stack([R[b]['pwk'] for b in range(nprompt)], 1).reshape(L, nprompt, 128, 2, 64)
    pwv = np.stack([R[b]['pwv'] for b in range(nprompt)], 1).reshape(L, nprompt, 128, 2, 64)
    pconv = np.stack([R[b]['pconv'].reshape(L, 128, 12, 3).transpose(0, 3, 2, 1).reshape(L, 3, 1536) for b in range(nprompt)], 1)
    pdelta = np.stack([R[b]['pdelta'] for b in range(nprompt)], 1)
    swk = np.concatenate([R[c]['swk'] for c in range(ncores)], 1).reshape(L, -1, 128, 2, 64)
    swv = np.concatenate([R[c]['swv'] for c in range(ncores)], 1).reshape(L, -1, 128, 2, 64)
    sconv = np.concatenate([R[c]['sconv'] for c in range(ncores)], 1)
    sdelta = np.concatenate([R[c]['sdelta'] for c in range(ncores)], 1)
    outs = (y_prompt, y_sample, pwk, pwv, pconv, pdelta, swk, swv, sconv, sdelta)
    return tuple(np.ascontiguousarray(o, dtype=np.float32) for o in outs)


def kernel(**inputs):
    return run(inputs, NT=64, L=4, G=2, NSEQ=16, PAST=8192, ncores=8, nprompt=2)
```

```python
import os
import numpy as np
import concourse.bass as bass
import concourse.mybir as mybir
from concourse.bass_utils import run_bass_kernel_spmd
from contextlib import ExitStack

F32 = mybir.dt.float32
BF16 = mybir.dt.bfloat16
ALU = mybir.AluOpType
F32R = mybir.dt.float32r
NEU_DT = F32R if os.environ.get('K_NEUF32') is None else mybir.dt.float32
AF = mybir.ActivationFunctionType
AX = mybir.AxisListType

D = 1024
NIN = 3336
EPS = 1e-6
import os
DBG = set(os.environ.get('K_DBG', '').split(','))
STOP = os.environ.get('K_STOP', '')
NEG = -1.0e9
C_AQ, C_AK, C_AV, C_AG, C_DQ, C_DG, C_DB, C_DA = 0, 512, 640, 768, 1280, 2816, 3328, 3332


class Buf:
    def __init__(self, name, t, nsub=1):
        self.name = name
        self.t = t
        self.nsub = nsub

    def keys(self):
        return [(self.name, i) for i in range(self.nsub)]

    def k(self, *idx):
        return [(self.name, i) for i in idx]

    def __getitem__(self, key):
        return self.t[key]


def _keys(lst):
    out = []
    for x in lst:
        if isinstance(x, Buf):
            out.extend(x.keys())
        elif isinstance(x, list):
            out.extend(x)
        else:
            out.append(x)
    return out


class Sched:
    ENGS = ['pe', 'act', 'dve', 'pool', 'sp']

    def __init__(self, nc, es, ndma=None):
        self.nc = nc
        ndma = ndma or {'sp': 24, 'act': 4, 'pool': 12}
        self.ops = {e: [] for e in self.ENGS}
        self.esem = {e: es.enter_context(nc.semaphore('se_' + e)) for e in self.ENGS}
        self.ecount = {e: 0 for e in self.ENGS}
        self.dsems = {q: [es.enter_context(nc.semaphore('sd_%s%d' % (q, i))) for i in range(n)]
                      for q, n in ndma.items()}
        self.dcount = {q: [0] * n for q, n in ndma.items()}
        self.dnext = {q: 0 for q in ndma}
        self.lastw = {}
        self.readers = {}
        self.waited = {e: {} for e in self.ENGS}
        self.semobj = {}
        self.nops = 0
        self.bar_tok = None

    def barrier(self, nops):
        keys = list(set(self.lastw) | set(self.readers))
        tok = None
        for e in ['act', 'dve', 'pool']:
            tok = self.op(e, nops[e], w=keys)
        self.bar_tok = tok

    def op(self, eng, fn, r=(), w=(), dma=False):
        r = _keys(r)
        w = _keys(w)
        deps = []
        for k in r:
            if k in self.lastw:
                deps.append(self.lastw[k])
            elif self.bar_tok is not None:
                deps.append(self.bar_tok)
        for k in w:
            if k in self.lastw:
                deps.append(self.lastw[k])
            elif self.bar_tok is not None:
                deps.append(self.bar_tok)
            deps.extend(self.readers.get(k, ()))
        if dma:
            j = self.dnext[eng]
            self.dnext[eng] = (j + 1) % len(self.dsems[eng])
            sem = self.dsems[eng][j]
            prev = self.dcount[eng][j]
            if prev > 0:
                deps.append((id(sem), prev))
            self.dcount[eng][j] = prev + 16
            tok = (id(sem), prev + 16)
            sig = (sem, 16)
        else:
            sem = self.esem[eng]
            self.ecount[eng] += 1
            tok = (id(sem), self.ecount[eng])
            sig = (sem, 1)
        self.semobj[id(sem)] = sem
        need = {}
        for (sid, v) in deps:
            if (not dma) and eng == 'pe' and sid == id(self.esem['pe']):
                continue
            if self.waited[eng].get(sid, 0) >= v:
                continue
            if need.get(sid, 0) < v:
                need[sid] = v
        for sid, v in need.items():
            self.waited[eng][sid] = v
        waits = [(self.semobj[sid], v) for sid, v in need.items()]
        self.ops[eng].append((waits, fn, sig))
        for k in w:
            self.lastw[k] = tok
            self.readers[k] = []
        for k in r:
            self.readers.setdefault(k, []).append(tok)
        self.nops += 1
        return tok

    def emit(self):
        nc = self.nc
        finals = []
        for q in self.dsems:
            for sem, c in zip(self.dsems[q], self.dcount[q]):
                if c > 0:
                    finals.append((sem, c))
        for e in self.ENGS:
            if e != 'sp' and self.ecount[e] > 0:
                finals.append((self.esem[e], self.ecount[e]))
        sched = self

        def run(engname, eng):
            for (waits, fn, sig) in sched.ops[engname]:
                for (sem, v) in waits:
                    eng.wait_ge(sem, v)
                ins = fn(eng)
                ins.then_inc(sig[0], sig[1])
            if engname == 'sp':
                for (sem, v) in finals:
                    eng.wait_ge(sem, v)

        with nc.Block() as block:
            @block.tensor
            def _(e):
                run('pe', e)

            @block.scalar
            def _(e):
                run('act', e)

            @block.vector
            def _(e):
                run('dve', e)

            @block.gpsimd
            def _(e):
                run('pool', e)

            @block.sync
            def _(e):
                run('sp', e)


class Rec:
    def __init__(self, sched):
        self.s = sched
        self.cur = None
        self.lists = None

    def op(self, *a, **k):
        if self.cur is None:
            return self.s.op(*a, **k)
        self.cur.append((a, k))

    def play(self, item):
        a, k = item
        return self.s.op(*a, **k)

    def barrier(self, *a, **k):
        return self.s.barrier(*a, **k)

    def emit(self):
        return self.s.emit()


def interleave(rec, A, B):
    na, nb = len(A), len(B)
    i = j = 0
    while i < na or j < nb:
        if j >= nb or (i < na and i * nb <= j * na):
            rec.play(A[i])
            i += 1
        else:
            rec.play(B[j])
            j += 1


def bc(ap, shape):
    return ap.to_broadcast(list(shape))


class Builder:
    def __init__(self, NT, DEPTH, G, NSEQ=16):
        self.NT, self.L, self.G, self.NSEQ = NT, DEPTH, G, NSEQ
        self.TP = NT * 128
        self.TS = NSEQ * 4
        self.nc = bass.Bass("TRN2", target_bir_lowering=False)
        self.uid = 0
        self.sb_c = self.SB_LO
        self.sb_p = self.SB_HI
        self.sb_s = self.SB_HI

    SB_LO = 16512
    SB_HI = 229344

    def sb(self, name, shape, dt=F32, nsub=1, region='c'):
        n = 1
        for d in shape[1:]:
            n *= d
        nbytes = ((n * (2 if dt == BF16 else 4)) + 31) // 32 * 32
        if region == 'c':
            off = self.sb_c
            self.sb_c += nbytes
        elif region == 'p':
            self.sb_p -= nbytes
            off = self.sb_p
        else:
            self.sb_s -= nbytes
            off = self.sb_s
        assert self.sb_c <= min(self.sb_p, self.sb_s), ('SBUF overflow', name, self.sb_c, self.sb_p, self.sb_s)
        t = self.nc.alloc_sbuf_tensor_at('sb_' + name, list(shape), dt, offset=off)
        return Buf(name, t, nsub)

    def din(self, name, shape):
        t = self.nc.dram_tensor(name, list(shape), F32, kind="ExternalInput")
        return Buf('d_' + name, t)

    def dout(self, name, shape):
        t = self.nc.dram_tensor(name, list(shape), F32, kind="ExternalOutput")
        return Buf('d_' + name, t)

    def dump(self, name, ap, shape, rkeys):
        if 'dump' not in DBG:
            return
        t = self.nc.dram_tensor('dbg_' + name, list(shape), F32, kind="ExternalOutput")
        b = Buf('dd_' + name, t)
        self.dma(t.ap(), ap, rkeys, [b])

    def bank(self):
        if self.pool == 'f':
            b = self.banks[self.bif % 3]
            self.bif += 1
        elif self.pool == 'b':
            b = self.banks[3 + self.bib % 3]
            self.bib += 1
        else:
            b = self.banks[self.bi % len(self.banks)]
            self.bi += 1
        return b

    def dma(self, out_ap, in_ap, r, w, q='sp'):
        self.S.op(q, lambda e, o=out_ap, i=in_ap: e.dma_start(out=o, in_=i), r=r, w=w, dma=True)

    def build(self):
        nc = self.nc
        L, NT, G, NSEQ, TP, TS = self.L, self.NT, self.G, self.NSEQ, self.TP, self.TS
        with ExitStack() as es:
            self.es = es
            S = self.S = Rec(Sched(nc, es))
            I = self.I = {}
            for name, shape in [
                ('xpT', [D, TP]), ('xsT', [D, TS]), ('w_in', [L, D, NIN]), ('w_out', [L, D, D]),
                ('ckT', [L, NSEQ, 128, 128]), ('ck', [L, NSEQ, 128, 128]), ('cv', [L, NSEQ, 128, 128]),
                ('scT', [L, 1536, NSEQ * 3]), ('sd', [L, NSEQ, 4, 128, 128]),
                ('normg', [128, L * 8]), ('convw', [128, L * 48]), ('dng', [128, L]),
                ('gqk', [128, L * 640]), ('sinkb', [128, L * 8]), ('alogb', [128, L * 4]), ('dtbb', [128, L * 4]),
                ('cosp', [128, NT * 8]), ('sinp', [128, NT * 8]), ('coss', [128, 8]), ('sins', [128, 8]),
                ('cmat', [128, 11 * 128]),
            ]:
                I[name] = self.din(name, shape)
            O = self.O = {}
            for name, shape in [
                ('ypT', [D, TP]), ('ysT', [D, TS]),
                ('pwk', [L, 128, 128]), ('pwv', [L, 128, 128]), ('pconv', [L, 128, 36]), ('pdelta', [L, 4, 128, 128]),
                ('swk', [L, NSEQ, 128, 128]), ('swv', [L, NSEQ, 128, 128]), ('sconv', [L, NSEQ, 3, 1536]),
                ('sdelta', [L, NSEQ, 4, 128, 128]),
            ]:
                O[name] = self.dout(name, shape)

            self.banks = [Buf('ps%d' % i, es.enter_context(nc.psum_tensor('ps%d' % i, [128, 512], F32))) for i in range(6)]
            self.bi = 0
            self.bif = 0
            self.bib = 0
            self.pool = 'all'
            self.bank_long = Buf('ps6', es.enter_context(nc.psum_tensor('ps6', [128, 512], F32)))
            self.psb = Buf('psb', es.enter_context(nc.psum_tensor('psb', [128, 1024], BF16)))

            cm = self.sb('cmat', [128, 11, 128])
            self.dma(cm[:, :, :], I['cmat'].t.ap().rearrange("p (a b) -> p a b", a=11), [I['cmat']], [cm])
            self.cm = cm
            ONES, IDENT, UM, NML, NMU, MPREV, MCUR, US, VS, NMLS, NMUS = range(11)
            identb = self.sb('identb', [128, 128], BF16)
            self.dma(identb[:, :], I['cmat'].t.ap()[:, 128:256], [I['cmat']], [identb], q='pool')
            msp = self.sb('msp', [128, 64], region='s')
            msc = self.sb('msc', [128, 64], region='s')
            P_ = {}
            for name, n in [('normg', L * 8), ('convw', L * 48), ('dng', L), ('sinkb', L * 8),
                            ('alogb', L * 4), ('dtbb', L * 4), ('cosp', NT * 8), ('sinp', NT * 8), ('coss', 8), ('sins', 8)]:
                P_[name] = self.sb('c_' + name, [128, n])
                self.dma(P_[name][:, :], I[name].t.ap(), [I[name]], [P_[name]])
            esink = self.sb('esink', [128, L * 8])
            negA = self.sb('negA', [128, L * 4])
            S.op('act', lambda e: e.activation(out=esink[:, :], in_=P_['sinkb'][:, :], func=AF.Exp), r=[P_['sinkb']], w=[esink])
            S.op('act', lambda e: e.activation(out=negA[:, :], in_=P_['alogb'][:, :], func=AF.Exp), r=[P_['alogb']], w=[negA])
            S.op('dve', lambda e: e.tensor_scalar(out=negA[:, :], in0=negA[:, :], scalar1=-1.0, scalar2=None, op0=ALU.mult), r=[negA], w=[negA])
            self.P_ = P_

            xTb = [self.sb('xT%d' % i, [128, 8, 128], region='p') for i in range(3)]
            xsT = self.sb('xsT', [128, 8, TS], region='s')
            Wi = [self.sb('Wi%d' % i, [128, 8, NIN], BF16) for i in range(1)]
            Wo = [self.sb('Wo%d' % i, [128, 8, D], BF16) for i in range(1)]
            gqk_l = self.sb('gqk_l', [128, 640])
            rstd = self.sb('rstd', [128, 128])
            xnT = self.sb('xnT', [128, 8, 128], BF16)
            zq = self.sb('zq', [128, 640])
            kvs = self.sb('kvs', [128, 136])
            gat = self.sb('gat', [128, 512])
            ss10 = self.sb('ss10', [128, 10])
            qr = self.sb('qr', [128, 10, 64])
            rt = [self.sb('rt%d' % i, [128, 10, 8]) for i in range(4)]
            QTraw = self.sb('QT', [128, 512], BF16)
            KT1 = [self.sb('KT%d' % i, [128, 128], BF16, region='p') for i in range(2)]
            KT = [KT1 for l in range(L)]
            VA1 = [self.sb('VA%d' % i, [128, 2, 65], BF16, region='p') for i in range(2)]
            VA = [VA1 for l in range(L)]
            KTs = self.sb('KTs', [128, 64], BF16, region='s')
            VAs = self.sb('VAs', [128, 2, 65], BF16, region='s')
            pT = [self.sb('pT%d' % i, [128, 4, 128], BF16) for i in range(4)]
            den = self.sb('den', [128, 8])
            att = self.sb('att', [128, 8, 64])
            attg = self.sb('attg', [128, 512], BF16)
            mixTP = [self.sb('mixT%d' % i, [128, 8, 128], BF16) for i in range(2)]
            zdT = self.sb('zdT', [128, 12, 131], region='p')
            zdTs = self.sb('zdTs', [128, 12, NSEQ, 7], region='s')
            scst = self.sb('scst', [128, 12, NSEQ * 3], region='s')
            convst1 = self.sb('convst', [128, 12, 3], region='p')
            convst = [convst1 for l in range(L)]
            gateTP = [self.sb('gateT%d' % i, [128, 4, 128]) for i in range(2)]
            cacc = self.sb('cacc', [128, 12, 128])
            sqt = cacc
            rn8 = None
            qkn = self.sb('qkn', [128, 8, 128])
            sc4 = {n: self.sb('sc_' + n, [128, 4]) for n in ['beta', 'nbeta', 'g', 'sp', 'Gc', 'Gt', 'eG', 'bg', 'ekd', 'gl']}
            gb = self.sb('gb', [128, 4, 128])
            glbc = self.sb('glbc', [128, 4, 64], region='s')
            qgTP = [self.sb('qgT%d' % i, [128, 4, 128]) for i in range(2)]
            argL = gb
            tLU = self.sb('tLU', [128, 8, 128])
            DLU = tLU
            rn8 = tLU
            CxP = [self.sb('CxP%d' % i, [128, 4, 128], NEU_DT) for i in range(2)]
            CtP = [self.sb('CtP%d' % i, [128, 4, 128], NEU_DT) for i in range(2)]
            R0P = [self.sb('R0P%d' % i, [128, 4, 128], NEU_DT) for i in range(2)]
            XT = self.sb('XT', [128, 4, 128], NEU_DT)
            XtT = self.sb('XtT', [128, 4, 128], NEU_DT)
            RT = self.sb('RT', [128, 4, 128], NEU_DT)
            aqkTP = [self.sb('aqkT%d' % i, [128, 4, 128]) for i in range(2)]
            kdP = [self.sb('kd%d' % i, [128, 4, 128]) for i in range(2)]
            kbgP = [self.sb('kbg%d' % i, [128, 4, 128]) for i in range(2)]
            vbP = [self.sb('vb%d' % i, [128, 4, 128]) for i in range(2)]
            glP = [self.sb('gl%d' % i, [128, 4]) for i in range(2)]
            u_sb = self.sb('u_sb', [128, 4, 128], region='p')
            wT = self.sb('wT', [128, 4, 128])
            vnew = self.sb('vnew', [128, 4, 128])
            vnT = self.sb('vnT', [128, 4, 64], region='s')
            oT = self.sb('oT', [128, 4, 128])
            osq = self.sb('osq', [128, 4, 128])
            Sst1 = self.sb('Sst', [128, 4, 128], region='p')
            Sst = [Sst1 for l in range(L)]
            Sq = [self.sb('Sq%d' % i, [128, 4, 128], region='s') for i in range(2)]
            vmb = [self.sb('vm%d' % i, [64, 4, 128], region='s') for i in range(2)]
            KcT = self.sb('KcT', [128, NSEQ, 128], BF16, region='s')
            VcA = self.sb('VcA', [128, NSEQ, 2, 65], BF16, region='s')
            OTs = self.sb('OTs', [65, 2, 256], region='s')

            for i in range(2):
                S.op('pool', lambda e, t=VA1[i]: e.memset(t[:, :, 64:65], 1.0), w=[VA1[i]])

            wparity = [0]

            def load_weights(l):
                p = 0
                self.dma(gqk_l[:, :], I['gqk'].t.ap()[:, l * 640:(l + 1) * 640], [I['gqk']], [gqk_l])
                for kc in range(8):
                    self.dma(Wi[p][:, kc, :], I['w_in'].t.ap()[l, kc * 128:(kc + 1) * 128, :], [I['w_in']], [Wi[p]], q='pool')
                for kc in range(8):
                    self.dma(Wo[p][:, kc, :], I['w_out'].t.ap()[l, kc * 128:(kc + 1) * 128, :], [I['w_out']], [Wo[p]], q='pool')
                return Wi[p], Wo[p]

            USE_R = False

            def f32(ap):
                return ap.bitcast(F32) if ap.dtype == F32R else ap

            def rr(ap):
                if USE_R and ap.dtype == F32:
                    return ap.bitcast(F32R)
                return ap

            def mm(out, lhsT, rhs, start=True, stop=True):
                return lambda e: e.matmul(out, lhsT=rr(lhsT), rhs=rr(rhs), start=start, stop=stop)

            def mms(lst):
                def f(e):
                    ins = None
                    for (o, a, b, st, sp) in lst:
                        ins = e.matmul(o, lhsT=rr(a), rhs=rr(b), start=st, stop=sp)
                    return ins
                return f

            def trs(lst, ident):
                def f(e):
                    ins = None
                    for (o, a, idn) in lst:
                        ins = e.transpose(o, a, idn)
                    return ins
                return f

            def tile_layer(l, T, xt_ap, W_i, W_o, samp, tidx, first, last_of_seq, xbuf, record=False, post=None):
                xkeys = [xbuf]
                pp2 = 0 if samp else tidx % 2
                mixT, gateT, qgT, aqkT, kd, kbg, vb, glv = mixTP[pp2], gateTP[pp2], qgTP[pp2], aqkTP[pp2], kdP[pp2], kbgP[pp2], vbP[pp2], glP[pp2]
                if record:
                    S.lists = {'f': [], 'b': []}
                    S.cur = S.lists['f']
                    self.pool = 'f'
                QT = Buf('QT', QTraw[:, 0:4 * T].rearrange("p (a b) -> p a b", a=4))
                np_ = l * 8
                S.op('act', lambda e: e.activation(out=sqt[:, 0:8, 0:T], in_=xt_ap(None), func=AF.Square), r=xkeys, w=[sqt])
                b = self.bank()
                S.op('pe', mms([(b[:, 0:T], cm[:, ONES, :], sqt[:, kc, 0:T], kc == 0, kc == 7) for kc in range(8)]), r=[sqt, cm], w=[b])
                S.op('act', lambda e, b=b: e.activation(out=rstd[:, 0:T], in_=b[:, 0:T], func=AF.Ln, scale=1.0 / D, bias=EPS), r=[b], w=[rstd])
                S.op('act', lambda e: e.activation(out=rstd[:, 0:T], in_=rstd[:, 0:T], func=AF.Exp, scale=-0.5), r=[rstd], w=[rstd])

                def fxn(e):
                    ins = None
                    for kc in range(8):
                        ins = e.scalar_tensor_tensor(out=xnT[:, kc, 0:T], in0=xt_ap(kc), scalar=P_['normg'][:, np_ + kc:np_ + kc + 1],
                                                     in1=rstd[:, 0:T], op0=ALU.mult, op1=ALU.mult)
                    return ins
                S.op('dve', fxn, r=xkeys + [rstd, P_['normg']], w=[xnT])

                def tokproj(c0, n, dst_ops):
                    b = self.bank()
                    S.op('pe', mms([(b[0:T, 0:n], xnT[:, kc, 0:T], W_i[:, kc, c0:c0 + n], kc == 0, kc == 7) for kc in range(8)]),
                         r=[xnT, W_i], w=[b])
                    return b
                bq = tokproj(C_AQ, 512, None)
                S.op('act', lambda e, b=bq: e.copy(out=zq[0:T, 0:512].rearrange("p (j g d) -> p g j d", g=2, d=64), in_=b[0:T, 0:512].rearrange("p (g j d) -> p g j d", g=2, d=64)), r=[bq], w=[zq])
                bkv = tokproj(C_AK, 256, None)
                S.op('act', lambda e, b=bkv: e.copy(out=zq[0:T, 512:640], in_=b[0:T, 0:128]), r=[bkv], w=[zq])
                S.op('act', lambda e, b=bkv: e.copy(out=kvs[0:T, 0:128], in_=b[0:T, 128:256]), r=[bkv], w=[kvs])
                bg_ = tokproj(C_AG, 512, None)
                S.op('act', lambda e, b=bg_: e.activation(out=gat[0:T, :], in_=b[0:T, 0:512], func=AF.Silu), r=[bg_], w=[gat])
                bs = tokproj(C_DB, 8, None)
                S.op('act', lambda e, b=bs: e.copy(out=kvs[0:T, 128:136], in_=b[0:T, 0:8]), r=[bs], w=[kvs])
                if samp:
                    for j in range(3):
                        bz = tokproj(C_DQ + 512 * j, 512, None)
                        S.op('act', lambda e, b=bz, j=j: e.copy(out=cacc[0:T, 4 * j:4 * j + 4, :].rearrange("p a b -> p (a b)"), in_=b[0:T, 0:512]), r=[bz], w=[cacc])
                    for i in range(1, 4):
                        if 'nosconv' in DBG:
                            break
                        self.dma(O['sconv'].t.ap()[l, :, i - 1, :], cacc[i:T:4, :, :].rearrange("p a b -> p (a b)"), [cacc], [O['sconv']])

                if samp and STOP == 's1':
                    return
                S.op('act', lambda e: e.activation(out=tLU[0:T, 0:5, :].rearrange("p a b -> p (a b)"), in_=zq[0:T, :], func=AF.Square), r=[zq], w=[tLU])
                S.op('dve', lambda e: e.tensor_reduce(out=ss10[0:T, :], in_=tLU[0:T, 0:5, :].rearrange("p a (c b) -> p (a c) b", c=2), axis=AX.X, op=ALU.add),
                     r=[tLU], w=[ss10])
                S.op('act', lambda e: e.activation(out=ss10[0:T, :], in_=ss10[0:T, :], func=AF.Ln, scale=1.0 / 64, bias=EPS), r=[ss10], w=[ss10])
                S.op('act', lambda e: e.activation(out=ss10[0:T, :], in_=ss10[0:T, :], func=AF.Exp, scale=-0.5), r=[ss10], w=[ss10])
                S.op('dve', lambda e: e.tensor_tensor(out=qr[0:T, :, :], in0=zq[0:T, :].rearrange("p (a b) -> p a b", a=10),
                                                      in1=bc(ss10[0:T, :, None], [T, 10, 64]), op=ALU.mult), r=[zq, ss10], w=[qr])
                S.op('dve', lambda e: e.tensor_tensor(out=qr[0:T, :, :], in0=qr[0:T, :, :],
                                                      in1=gqk_l[0:T, :].rearrange("p (a b) -> p a b", a=10), op=ALU.mult),
                     r=[qr, gqk_l], w=[qr])
                if samp:
                    cos_ap = P_['coss'][0:T, :]
                    sin_ap = P_['sins'][0:T, :]
                else:
                    cos_ap = P_['cosp'][0:T, tidx * 8:(tidx + 1) * 8]
                    sin_ap = P_['sinp'][0:T, tidx * 8:(tidx + 1) * 8]
                cosb = bc(cos_ap.unsqueeze(1), [T, 10, 8])
                sinb = bc(sin_ap.unsqueeze(1), [T, 10, 8])
                x1 = qr[0:T, :, 0:8]
                x2 = qr[0:T, :, 8:16]
                S.op('dve', lambda e: e.tensor_tensor(out=rt[0][0:T], in0=x1, in1=cosb, op=ALU.mult), r=[qr, P_['cosp'], P_['coss']], w=[rt[0]])
                S.op('dve', lambda e: e.tensor_tensor(out=rt[1][0:T], in0=x2, in1=sinb, op=ALU.mult), r=[qr, P_['sinp'], P_['sins']], w=[rt[1]])
                S.op('pool', lambda e: e.tensor_tensor(out=rt[2][0:T], in0=x2, in1=cosb, op=ALU.mult), r=[qr, P_['cosp'], P_['coss']], w=[rt[2]])
                S.op('pool', lambda e: e.tensor_tensor(out=rt[3][0:T], in0=x1, in1=sinb, op=ALU.mult), r=[qr, P_['sinp'], P_['sins']], w=[rt[3]])
                S.op('dve', lambda e: e.tensor_tensor(out=x1, in0=rt[0][0:T], in1=rt[1][0:T], op=ALU.subtract), r=[rt[0], rt[1], rt[2], rt[3]], w=[qr])
                S.op('dve', lambda e: e.tensor_tensor(out=x2, in0=rt[2][0:T], in1=rt[3][0:T], op=ALU.add), r=[rt[2], rt[3]], w=[qr])
                if samp and 'noswk' in DBG:
                    pass
                elif samp:
                    for i in range(4):
                        self.dma(O['swk'].t.ap()[l, :, 124 + i, :], qr[i:T:4, 8:10, :].rearrange("p a b -> p (a b)"), [qr], [O['swk']])
                        self.dma(O['swv'].t.ap()[l, :, 124 + i, :], kvs[i:T:4, 0:128], [kvs], [O['swv']])
                    self.dma(O['swk'].t.ap()[l, :, 0:124, :], I['ck'].t.ap()[l, :, 4:128, :], [I['ck']], [O['swk']])
                    self.dma(O['swv'].t.ap()[l, :, 0:124, :], I['cv'].t.ap()[l, :, 4:128, :], [I['cv']], [O['swv']])
                elif last_of_seq:
                    self.dma(O['pwk'].t.ap()[l], qr[0:T, 8:10, :], [qr], [O['pwk']])
                    self.dma(O['pwv'].t.ap()[l], kvs[0:T, 0:128], [kvs], [O['pwv']])
                par = tidx % 2
                KTc = KTs if samp else KT[l][par]
                VAc = VAs if samp else VA[l][par]
                b = self.bank()
                S.op('pe', trs([(b[:, j * T:(j + 1) * T], qr[0:T, 2 * j:2 * j + 2, :].rearrange("p a b -> p (a b)"), cm[0:T, IDENT, 0:T]) for j in range(4)], None), r=[qr, cm], w=[b])
                S.op('act', lambda e, b=b: e.copy(out=QT[:, :, :], in_=b[:, 0:4 * T].rearrange("p (a b) -> p a b", a=4)), r=[b], w=[QT])
                b2 = self.bank()
                S.op('pe', trs([(b2[:, 0:T], qr[0:T, 8:10, :].rearrange("p a b -> p (a b)"), cm[0:T, IDENT, 0:T])], None), r=[qr, cm], w=[b2])
                S.op('act', lambda e, b=b2: e.copy(out=KTc[:, 0:T], in_=b[:, 0:T]), r=[b2], w=[KTc])
                S.op('dve', lambda e: e.tensor_copy(out=VAc[0:T, :, 0:64], in_=kvs[0:T, 0:128].rearrange("p (a b) -> p a b", a=2)), r=[kvs], w=[VAc])

                if samp and STOP == 's2':
                    return
                poA = self.bank()
                poB = self.bank()
                if not samp:
                    blocks = []
                    if not first:
                        blocks.append((KT[l][1 - par], VA[l][1 - par], MPREV))
                    blocks.append((KTc, VAc, MCUR))
                    pts = []
                    n = 0
                    for (ktb, vab, mslot) in blocks:
                        for kvh in range(2):
                            h0 = 64 * kvh
                            bsc = self.bank()
                            S.op('pe', mm(bsc[:, :], ktb[h0:h0 + 64, :], QT[h0:h0 + 64, :, :].rearrange("p a b -> p (a b)")), r=[ktb, QT], w=[bsc])
                            p = pT[n]
                            n += 1
                            S.op('act', lambda e, p=p, b=bsc: e.activation(out=p[:, :, :], in_=b[:, :].rearrange("p (a b) -> p a b", a=4),
                                                                          func=AF.Exp, scale=0.125), r=[bsc], w=[p])
                            S.op('dve', lambda e, p=p, m=mslot: e.tensor_tensor(out=p[:, :, :], in0=p[:, :, :], in1=bc(cm[:, m, :].unsqueeze(1), [128, 4, 128]),
                                                                               op=ALU.mult), r=[p, cm], w=[p])
                            pts.append((p, vab, kvh))
                    for h in range(8):
                        kvh, j = h // 4, h % 4
                        po = poA if kvh == 0 else poB
                        lst = [(p, vab) for (p, vab, kv) in pts if kv == kvh]
                        S.op('pe', mms([(po[:, j * 65:(j + 1) * 65], p[:, j, :], vab[:, kvh, :], i == 0, i == len(lst) - 1) for i, (p, vab) in enumerate(lst)]),
                             r=[x[0] for x in lst] + [x[1] for x in lst], w=[po])
                else:
                    self.dma(KcT[:, :, :], I['ckT'].t.ap()[l].rearrange("s c k -> c s k"), [I['ckT']], [KcT], q='pool')
                    for a_ in range(2):
                        self.dma(VcA[:, :, a_, 0:64], I['cv'].t.ap()[l][:, :, a_ * 64:(a_ + 1) * 64].rearrange("s k c -> k s c"), [I['cv']], [VcA], q='pool')
                    pc = pT[0]
                    pp = pT[1]
                    pcf = pc[0:64, :, :].rearrange("p a b -> p (a b)")
                    ppf = pp[:, :, :].rearrange("p a b -> p (a b)")
                    for kvh in range(2):
                        bsc = self.bank()
                        S.op('pe', mm(bsc[0:64, 0:256], KTs[64 * kvh:64 * kvh + 64, 0:64], QT[64 * kvh:64 * kvh + 64, :, :].rearrange("p a b -> p (a b)")),
                             r=[KTs, QT], w=[bsc])
                        S.op('act', lambda e, b=bsc, kvh=kvh: e.activation(out=pcf[:, kvh * 256:(kvh + 1) * 256], in_=b[0:64, 0:256], func=AF.Exp, scale=0.125), r=[bsc], w=[pc])
                    S.op('dve', lambda e: e.tensor_tensor(out=pcf.rearrange("p (a b) -> p a b", a=8), in0=pcf.rearrange("p (a b) -> p a b", a=8),
                                                           in1=bc(msc[0:64, :].unsqueeze(1), [64, 8, 64]), op=ALU.mult), r=[pc, msc], w=[pc])
                    if STOP == 's2a':
                        return
                    for kvh in range(2):
                        bsp = self.bank()
                        lst = []
                        for s in range(NSEQ):
                            for j in range(4):
                                lst.append((bsp[:, j * 64 + 4 * s:j * 64 + 4 * s + 4], KcT[64 * kvh:64 * kvh + 64, s, :], QT[64 * kvh:64 * kvh + 64, j, 4 * s:4 * s + 4], True, True))
                        S.op('pe', mms(lst), r=[KcT, QT], w=[bsp])
                        S.op('act', lambda e, b=bsp, kvh=kvh: e.activation(out=ppf[:, kvh * 256:(kvh + 1) * 256], in_=b[:, 0:256], func=AF.Exp, scale=0.125), r=[bsp], w=[pp])
                    S.op('dve', lambda e: e.tensor_tensor(out=ppf.rearrange("p (a b) -> p a b", a=8), in0=ppf.rearrange("p (a b) -> p a b", a=8),
                                                           in1=bc(msp[:, :].unsqueeze(1), [128, 8, 64]), op=ALU.mult), r=[pp, msp], w=[pp])
                    if STOP == 's2b':
                        return
                    for kvh in range(2):
                        bo = self.bank()
                        lst = [(bo[0:65, 0:256], VAs[0:64, kvh, :], pcf[:, kvh * 256:(kvh + 1) * 256], True, False)]
                        for s in range(NSEQ):
                            for j in range(4):
                                c0 = j * 64 + 4 * s
                                lst.append((bo[0:65, c0:c0 + 4], VcA[:, s, kvh, :], ppf[:, kvh * 256 + c0:kvh * 256 + c0 + 4], False, (s == NSEQ - 1 and j == 3)))
                        S.op('pe', mms(lst), r=[VAs, VcA, pc, pp], w=[bo])
                        S.op('act', lambda e, b=bo, kvh=kvh: e.copy(out=OTs[:, kvh, :], in_=b[0:65, 0:256]), r=[bo], w=[OTs])
                    if STOP == 's2c':
                        return
                    for kvh in range(2):
                        po = poA if kvh == 0 else poB
                        S.op('pe', trs([(po[0:64, j * 65:(j + 1) * 65], OTs[:, kvh, j * 64:(j + 1) * 64], cm[0:65, IDENT, 0:65]) for j in range(4)], None),
                             r=[OTs, cm], w=[po])
                if samp and STOP == 's2d':
                    return
                for i, po in enumerate((poA, poB)):
                    S.op('dve', lambda e, po=po, i=i: e.tensor_tensor(out=den[0:T, 4 * i:4 * i + 4], in0=po[0:T, 0:260].rearrange("p (a b) -> p a b", a=4)[:, :, 64],
                                                                      in1=esink[0:T, l * 8 + 4 * i:l * 8 + 4 * i + 4], op=ALU.add), r=[po, esink], w=[den])
                S.op('dve', lambda e: e.reciprocal(out=den[0:T, :], in_=den[0:T, :]), r=[den], w=[den])
                for i, po in enumerate((poA, poB)):
                    S.op('dve', lambda e, po=po, i=i: e.tensor_tensor(out=att[0:T, 4 * i:4 * i + 4, :], in0=po[0:T, 0:260].rearrange("p (a b) -> p a b", a=4)[:, :, 0:64],
                                                                      in1=bc(den[0:T, 4 * i:4 * i + 4, None], [T, 4, 64]), op=ALU.mult), r=[po, den], w=[att])
                S.op('dve', lambda e: e.tensor_tensor(out=attg[0:T, :], in0=att[0:T, :, :].rearrange("p a b -> p (a b)"), in1=gat[0:T, :], op=ALU.mult),
                     r=[att, gat], w=[attg])
                S.op('pe', trs([(self.psb[:, j * T:(j + 1) * T], attg[0:T, j * 128:(j + 1) * 128], identb[0:T, 0:T]) for j in range(4)], None),
                     r=[attg, identb], w=[self.psb])
                S.op('act', lambda e: e.copy(out=mixT[:, 0:4, 0:T], in_=self.psb[:, 0:4 * T].rearrange("p (a b) -> p a b", a=4)), r=[self.psb], w=[mixT])

                if samp and STOP == 's3':
                    return
                if samp:
                    self.dma(scst[:, :, :], I['scT'].t.ap()[l].rearrange("(c p) f -> p c f", p=128), [I['scT']], [scst])
                    S.op('pool', lambda e: e.tensor_copy(out=zdTs[:, :, :, 0:3], in_=scst[:, :, :].rearrange("p c (s j) -> p c s j", j=3)), r=[scst], w=[zdTs])
                else:
                    S.op('pool', lambda e: e.tensor_copy(out=zdT[:, :, 0:3], in_=convst[l][:, :, :]), r=[convst[l]], w=[zdT])
                for rnd in range(4):
                    b = self.bank()
                    lst = []
                    for cc in range(4):
                        c = rnd * 4 + cc
                        col = (C_DQ + 128 * c) if c < 12 else (C_DG + 128 * (c - 12))
                        for kc in range(8):
                            lst.append((b[:, cc * T:(cc + 1) * T], W_i[:, kc, col:col + 128], xnT[:, kc, 0:T], kc == 0, kc == 7))
                    S.op('pe', mms(lst), r=[W_i, xnT], w=[b])
                    bv = b[:, 0:4 * T].rearrange("p (a b) -> p a b", a=4)
                    if rnd < 3:
                        if samp:
                            S.op('act', lambda e, bv=bv, rnd=rnd: e.copy(out=zdTs[:, 4 * rnd:4 * rnd + 4, :, 3:7],
                                                                          in_=bv.rearrange("p a (s j) -> p a s j", j=4)), r=[b], w=[zdTs])
                        else:
                            S.op('act', lambda e, bv=bv, rnd=rnd: e.copy(out=zdT[:, 4 * rnd:4 * rnd + 4, 3:3 + T], in_=bv), r=[b], w=[zdT])
                    else:
                        S.op('act', lambda e, bv=bv: e.activation(out=gateT[:, :, 0:T], in_=bv, func=AF.Silu), r=[b], w=[gateT])
                if not samp:
                    S.op('pool', lambda e: e.tensor_copy(out=convst[l][:, :, :], in_=zdT[:, :, T:T + 3]), r=[zdT], w=[convst[l]])
                cw0 = l * 48

                def tap(c, j):
                    if samp:
                        return zdTs[:, c, :, j:j + 4]
                    return zdT[:, c, j:j + T]

                def acc(c):
                    if samp:
                        return cacc[:, c, 0:T].rearrange("p (s j) -> p s j", j=4)
                    return cacc[:, c, 0:T]
                zk = [zdTs] if samp else [zdT]

                def fconv0(e):
                    ins = None
                    for c in range(12):
                        ins = e.activation(out=acc(c), in_=tap(c, 3), func=AF.Copy, scale=P_['convw'][:, cw0 + 36 + c:cw0 + 37 + c])
                    return ins
                S.op('act', fconv0, r=zk + [P_['convw']], w=[cacc])
                for j in range(3):
                    def fconv(e, j=j):
                        ins = None
                        for c in range(12):
                            ins = e.scalar_tensor_tensor(out=acc(c), in0=tap(c, j), scalar=P_['convw'][:, cw0 + 12 * j + c:cw0 + 12 * j + c + 1],
                                                         in1=acc(c), op0=ALU.mult, op1=ALU.add)
                        return ins
                    S.op('dve', fconv, r=zk + [P_['convw'], cacc], w=[cacc])
                S.op('act', lambda e: e.activation(out=cacc[:, :, 0:T], in_=cacc[:, :, 0:T], func=AF.Silu), r=[cacc], w=[cacc])
                S.op('act', lambda e: e.activation(out=qkn[:, 0:8, 0:T], in_=cacc[:, 0:8, 0:T], func=AF.Square), r=[cacc], w=[qkn])
                for half in range(2):
                    b = self.bank()
                    S.op('pe', mms([(b[:, cc * T:(cc + 1) * T], cm[:, ONES, :], qkn[:, 4 * half + cc, 0:T], True, True) for cc in range(4)]), r=[qkn, cm], w=[b])
                    sc, bi_ = (128.0, 128.0 * EPS) if half == 0 else (1.0, EPS)
                    S.op('act', lambda e, b=b, half=half, sc=sc, bi_=bi_: e.activation(out=rn8[:, 4 * half:4 * half + 4, 0:T],
                                                                                        in_=b[:, 0:4 * T].rearrange("p (a b) -> p a b", a=4),
                                                                                        func=AF.Ln, scale=sc, bias=bi_), r=[b], w=[rn8])
                S.op('act', lambda e: e.activation(out=rn8[:, 0:8, 0:T], in_=rn8[:, 0:8, 0:T], func=AF.Exp, scale=-0.5), r=[rn8], w=[rn8])
                S.op('dve', lambda e: e.tensor_tensor(out=qkn[:, :, 0:T], in0=cacc[:, 0:8, 0:T], in1=rn8[:, 0:8, 0:T], op=ALU.mult), r=[cacc, rn8], w=[qkn])
                if samp and l == 0:
                    self.dump('ycv', cacc[:, :, 0:T], [128, 12, T], [cacc])
                    self.dump('qkn', qkn[:, :, 0:T], [128, 8, T], [qkn])
                if samp and STOP == 's4':
                    return
                db = kvs[0:T, 128:132]
                da = kvs[0:T, 132:136]
                S.op('act', lambda e: e.activation(out=sc4['beta'][0:T, :], in_=db, func=AF.Sigmoid), r=[kvs], w=[sc4['beta']])
                S.op('pool', lambda e: e.tensor_scalar(out=sc4['nbeta'][0:T, :], in0=sc4['beta'][0:T, :], scalar1=-1.0, scalar2=None, op0=ALU.mult),
                     r=[sc4['beta']], w=[sc4['nbeta']])
                S.op('dve', lambda e: e.tensor_tensor(out=sc4['sp'][0:T, :], in0=da, in1=P_['dtbb'][0:T, l * 4:l * 4 + 4], op=ALU.add), r=[kvs, P_['dtbb']], w=[sc4['sp']])
                S.op('act', lambda e: e.activation(out=sc4['sp'][0:T, :], in_=sc4['sp'][0:T, :], func=AF.Exp), r=[sc4['sp']], w=[sc4['sp']])
                S.op('act', lambda e: e.activation(out=sc4['sp'][0:T, :], in_=sc4['sp'][0:T, :], func=AF.Ln, bias=1.0), r=[sc4['sp']], w=[sc4['sp']])
                S.op('dve', lambda e: e.tensor_tensor(out=sc4['g'][0:T, :], in0=sc4['sp'][0:T, :], in1=negA[0:T, l * 4:l * 4 + 4], op=ALU.mult), r=[sc4['sp'], negA], w=[sc4['g']])
                um, vm = (US, VS) if samp else (UM, ONES)
                nml, nmu = (NMLS, NMUS) if samp else (NML, NMU)
                b = self.bank()
                S.op('pe', mms([(b[0:T, 0:4], cm[0:T, um, 0:T], sc4['g'][0:T, :], True, True),
                                (b[0:T, 4:8], cm[0:T, vm, 0:T], sc4['g'][0:T, :], True, True)]), r=[cm, sc4['g']], w=[b])
                S.op('act', lambda e, b=b: e.copy(out=sc4['Gc'][0:T, :], in_=b[0:T, 0:4]), r=[b], w=[sc4['Gc']])
                S.op('act', lambda e, b=b: e.copy(out=sc4['Gt'][0:T, :], in_=b[0:T, 4:8]), r=[b], w=[sc4['Gt']])
                S.op('act', lambda e: e.activation(out=sc4['eG'][0:T, :], in_=sc4['Gc'][0:T, :], func=AF.Exp), r=[sc4['Gc']], w=[sc4['eG']])
                S.op('dve', lambda e: e.tensor_tensor(out=sc4['bg'][0:T, :], in0=sc4['beta'][0:T, :], in1=sc4['eG'][0:T, :], op=ALU.mult), r=[sc4['beta'], sc4['eG']], w=[sc4['bg']])
                S.op('dve', lambda e: e.tensor_tensor(out=sc4['ekd'][0:T, :], in0=sc4['Gt'][0:T, :], in1=sc4['Gc'][0:T, :], op=ALU.subtract), r=[sc4['Gt'], sc4['Gc']], w=[sc4['ekd']])
                S.op('act', lambda e: e.activation(out=sc4['ekd'][0:T, :], in_=sc4['ekd'][0:T, :], func=AF.Exp), r=[sc4['ekd']], w=[sc4['ekd']])
                S.op('act', lambda e: e.activation(out=glv[0:T, :], in_=sc4['Gt'][0:T, :], func=AF.Exp), r=[sc4['Gt']], w=[glv])
                S.op('dve', lambda e: e.tensor_copy(out=gb[0:T, :, :], in_=bc(sc4['g'][0:T, :, None], [T, 4, 128])), r=[sc4['g']], w=[gb])
                bG = self.bank()
                S.op('pe', mms([(bG[:, h * T:(h + 1) * T], gb[0:T, h, :], cm[0:T, um, 0:T], True, True) for h in range(4)]), r=[gb, cm], w=[bG])
                bGv = bG[:, 0:4 * T].rearrange("p (a b) -> p a b", a=4)
                S.op('act', lambda e: e.activation(out=tLU[:, 0:4, 0:T], in_=bGv, func=AF.Exp), r=[bG], w=[tLU])
                S.op('dve', lambda e: e.tensor_tensor(out=qgT[:, :, 0:T], in0=qkn[:, 0:4, 0:T], in1=tLU[:, 0:4, 0:T], op=ALU.mult), r=[qkn, tLU], w=[qgT])
                if samp:
                    bGt = self.bank()
                    S.op('pe', mms([(bGt[:, h * T:(h + 1) * T], gb[0:T, h, :], cm[0:T, vm, 0:T], True, True) for h in range(4)]), r=[gb, cm], w=[bGt])
                    S.op('act', lambda e: e.activation(out=glbc[:, :, 0:T], in_=bGt[:, 0:4 * T].rearrange("p (a b) -> p a b", a=4), func=AF.Exp), r=[bGt], w=[glbc])
                S.op('dve', lambda e: e.tensor_tensor(out=argL[0:T, :, 0:T], in0=bc(sc4['Gc'][0:T, :, None], [T, 4, T]), in1=bGv[0:T], op=ALU.subtract),
                     r=[sc4['Gc'], bG], w=[argL])
                S.op('dve', lambda e: e.tensor_tensor(out=tLU[0:T, 0:4, 0:T], in0=argL[0:T, :, 0:T], in1=bc(cm[0:T, nml, 0:T].unsqueeze(1), [T, 4, T]), op=ALU.add),
                     r=[argL, cm], w=[tLU.k(0)])
                S.op('dve', lambda e: e.tensor_tensor(out=tLU[0:T, 4:8, 0:T], in0=bc(cm[0:T, nmu, 0:T].unsqueeze(1), [T, 4, T]), in1=argL[0:T, :, 0:T], op=ALU.subtract),
                     r=[argL, cm], w=[tLU.k(0)])
                S.op('act', lambda e: e.activation(out=DLU[0:T, :, 0:T], in_=tLU[0:T, :, 0:T], func=AF.Exp), r=[tLU], w=[DLU])
                bKK = self.bank()
                S.op('pe', mms([(bKK[0:T, h * T:(h + 1) * T], qkn[:, 4 + h, 0:T], qkn[:, 4 + h, 0:T], True, True) for h in range(4)]), r=[qkn], w=[bKK])
                bQK = self.bank()
                S.op('pe', mms([(bQK[0:T, h * T:(h + 1) * T], qkn[:, 4 + h, 0:T], qkn[:, h, 0:T], True, True) for h in range(4)]), r=[qkn], w=[bQK])
                Ct = CtP[pp2]
                S.op('dve', lambda e: e.tensor_tensor(out=Ct[0:T, :, 0:T], in0=bKK[0:T, 0:4 * T].rearrange("p (a b) -> p a b", a=4), in1=DLU[0:T, 0:4, 0:T], op=ALU.mult),
                     r=[bKK, DLU], w=[Ct])
                S.op('dve', lambda e: e.tensor_tensor(out=Ct[0:T, :, 0:T], in0=f32(Ct[0:T, :, 0:T]), in1=bc(sc4['nbeta'][0:T, :, None], [T, 4, T]), op=ALU.mult),
                     r=[Ct, sc4['nbeta']], w=[Ct])
                S.op('dve', lambda e: e.tensor_tensor(out=aqkT[0:T, :, 0:T], in0=bQK[0:T, 0:4 * T].rearrange("p (a b) -> p a b", a=4), in1=DLU[0:T, 4:8, 0:T], op=ALU.mult),
                     r=[bQK, DLU], w=[aqkT])
                bC = self.bank()
                S.op('pe', trs([(bC[0:T, h * T:(h + 1) * T], f32(Ct[0:T, h, 0:T]), cm[0:T, IDENT, 0:T]) for h in range(4)], None), r=[Ct, cm], w=[bC])
                Cx = CxP[pp2]
                S.op('act', lambda e: e.copy(out=Cx[0:T, :, 0:T], in_=bC[0:T, 0:4 * T].rearrange("p (a b) -> p a b", a=4)), r=[bC], w=[Cx])
                R = R0P[pp2]
                S.op('dve', lambda e, R=R: e.tensor_tensor(out=R[0:T, :, 0:T], in0=f32(Cx[0:T, :, 0:T]), in1=bc(cm[0:T, IDENT, 0:T].unsqueeze(1), [T, 4, T]), op=ALU.add),
                     r=[Cx, cm], w=[R])
                bk = self.bank()
                S.op('pe', trs([(bk[0:T, h * 128:(h + 1) * 128], qkn[:, 4 + h, 0:T], cm[:, IDENT, :]) for h in range(4)], None), r=[qkn, cm], w=[bk])
                bv_ = self.bank()
                S.op('pe', trs([(bv_[0:T, h * 128:(h + 1) * 128], cacc[:, 8 + h, 0:T], cm[:, IDENT, :]) for h in range(4)], None), r=[cacc, cm], w=[bv_])
                bkv4 = bk[0:T, :].rearrange("p (a b) -> p a b", a=4)
                S.op('dve', lambda e: e.tensor_tensor(out=kd[0:T], in0=bkv4, in1=bc(sc4['ekd'][0:T, :, None], [T, 4, 128]), op=ALU.mult), r=[bk, sc4['ekd']], w=[kd])
                S.op('dve', lambda e: e.tensor_tensor(out=kbg[0:T], in0=bkv4, in1=bc(sc4['bg'][0:T, :, None], [T, 4, 128]), op=ALU.mult), r=[bk, sc4['bg']], w=[kbg])
                S.op('dve', lambda e: e.tensor_tensor(out=vb[0:T], in0=bv_[0:T, :].rearrange("p (a b) -> p a b", a=4), in1=bc(sc4['beta'][0:T, :, None], [T, 4, 128]), op=ALU.mult),
                     r=[bv_, sc4['beta']], w=[vb])
                if record:
                    S.cur = S.lists['b']
                    self.pool = 'b'
                nlev = 1 if samp else 6
                X, Xt = Cx, Ct
                own = (CxP[pp2], CtP[pp2], R0P[pp2])
                alt = (XT, XtT, RT)
                for lev in range(nlev):
                    lastlev = lev == nlev - 1
                    Xn, Xtn, Rn = alt if lev % 2 == 0 else own
                    bXt = self.bank()
                    S.op('pe', mms([(bXt[0:T, h * T:(h + 1) * T], X[0:T, h, 0:T], Xt[0:T, h, 0:T], True, True) for h in range(4)]), r=[X, Xt], w=[bXt])
                    if not lastlev:
                        bX = self.bank()
                        S.op('pe', mms([(bX[0:T, h * T:(h + 1) * T], Xt[0:T, h, 0:T], X[0:T, h, 0:T], True, True) for h in range(4)]), r=[X, Xt], w=[bX])
                    S.op('act', lambda e, Xtn=Xtn, b=bXt: e.copy(out=Xtn[0:T, :, 0:T], in_=b[0:T, 0:4 * T].rearrange("p (a b) -> p a b", a=4)), r=[bXt], w=[Xtn])
                    if not lastlev:
                        S.op('act', lambda e, Xn=Xn, b=bX: e.copy(out=Xn[0:T, :, 0:T], in_=b[0:T, 0:4 * T].rearrange("p (a b) -> p a b", a=4)), r=[bX], w=[Xn])
                    bR = self.bank()
                    S.op('pe', mms([(bR[0:T, h * T:(h + 1) * T], Xtn[0:T, h, 0:T], R[0:T, h, 0:T], True, True) for h in range(4)]), r=[Xtn, R], w=[bR])
                    S.op('dve', lambda e, Rn=Rn, R=R, b=bR: e.tensor_tensor(out=Rn[0:T, :, 0:T], in0=f32(R[0:T, :, 0:T]), in1=b[0:T, 0:4 * T].rearrange("p (a b) -> p a b", a=4), op=ALU.add),
                         r=[R, bR], w=[Rn])
                    R = Rn
                    X, Xt = Xn, Xtn
                if samp and STOP == 's5':
                    return
                if samp and l == 0:
                    self.dump('g', sc4['g'][0:T, :], [T, 4], [sc4['g']])
                    self.dump('beta', sc4['beta'][0:T, :], [T, 4], [sc4['beta']])
                    self.dump('Gc', sc4['Gc'][0:T, :], [T, 4], [sc4['Gc']])
                    self.dump('Gt', sc4['Gt'][0:T, :], [T, 4], [sc4['Gt']])
                    self.dump('R', f32(R[0:T, :, 0:T]), [T, 4, T], [R])
                    self.dump('DLU', DLU[0:T, :, 0:T], [T, 8, T], [DLU])
                    self.dump('aqkT', aqkT[0:T, :, 0:T], [T, 4, T], [aqkT])
                bw = self.bank()
                S.op('pe', mms([(bw[:, h * T:(h + 1) * T], kbg[0:T, h, :], f32(R[0:T, h, 0:T]), True, True) for h in range(4)]), r=[kbg, R], w=[bw])
                S.op('act', lambda e: e.copy(out=wT[:, :, 0:T], in_=bw[:, 0:4 * T].rearrange("p (a b) -> p a b", a=4)), r=[bw], w=[wT])

                if not samp:
                    bu = self.bank()
                    S.op('pe', mms([(bu[0:T, h * 128:(h + 1) * 128], f32(R[0:T, h, 0:T]), vb[0:T, h, :], True, True) for h in range(4)]), r=[vb, R], w=[bu])
                    S.op('act', lambda e: e.copy(out=u_sb[0:T], in_=bu[0:T, :].rearrange("p (a b) -> p a b", a=4)), r=[bu], w=[u_sb])
                    St = Sst[l]
                    bws = self.bank()
                    S.op('pe', mms([(bws[0:T, h * 128:(h + 1) * 128], wT[:, h, 0:T], St[:, h, :], True, True) for h in range(4)]), r=[wT, St], w=[bws])
                    S.op('dve', lambda e: e.tensor_tensor(out=vnew[0:T], in0=u_sb[0:T], in1=bws[0:T, :].rearrange("p (a b) -> p a b", a=4), op=ALU.subtract), r=[u_sb, bws], w=[vnew])
                    bo = self.bank()
                    lst = []
                    for h in range(4):
                        lst.append((bo[:, h * T:(h + 1) * T], St[:, h, :], qgT[:, h, 0:T], True, False))
                        lst.append((bo[:, h * T:(h + 1) * T], vnew[0:T, h, :], aqkT[0:T, h, 0:T], False, True))
                    S.op('pe', mms(lst), r=[St, qgT, vnew, aqkT], w=[bo])
                    S.op('act', lambda e: e.copy(out=oT[:, :, 0:T], in_=bo[:, 0:4 * T].rearrange("p (a b) -> p a b", a=4)), r=[bo], w=[oT])
                    bS = self.bank()
                    S.op('pe', mms([(bS[:, h * 128:(h + 1) * 128], kd[0:T, h, :], vnew[0:T, h, :], True, True) for h in range(4)]), r=[kd, vnew], w=[bS])

                    def fS(e):
                        ins = None
                        for h in range(4):
                            ins = e.scalar_tensor_tensor(out=St[:, h, :], in0=St[:, h, :], scalar=glv[:, h:h + 1], in1=bS[:, h * 128:(h + 1) * 128],
                                                         op0=ALU.mult, op1=ALU.add)
                        return ins
                    S.op('dve', fS, r=[St, glv, bS], w=[St])
                    if last_of_seq:
                        self.dma(O['pdelta'].t.ap()[l].rearrange("h k v -> k h v"), St[:, :, :], [St], [O['pdelta']])
                        self.dma(O['pconv'].t.ap()[l], convst[l][:, :, :].rearrange("p a b -> p (a b)"), [convst[l]], [O['pconv']])
                else:
                    bu = self.bank()
                    S.op('pe', mms([(bu[:, h * T:(h + 1) * T], vb[0:T, h, :], f32(R[0:T, h, 0:T]), True, True) for h in range(4)]), r=[vb, R], w=[bu])
                    S.op('act', lambda e: e.copy(out=vnT[:, :, 0:T], in_=bu[:, 0:4 * T].rearrange("p (a b) -> p a b", a=4)), r=[bu], w=[vnT])
                    bws = self.bank()
                    for s in range(NSEQ):
                        sq_ = Sq[s % 2]
                        self.dma(sq_[:, :, :], I['sd'].t.ap()[l, s].rearrange("h k v -> k h v"), [I['sd']], [sq_])
                        S.op('pe', mms([(bws[:, h * T + 4 * s:h * T + 4 * s + 4], sq_[:, h, :], wT[:, h, 4 * s:4 * s + 4], True, True) for h in range(4)]), r=[sq_, wT], w=[bws])
                    S.op('dve', lambda e: e.tensor_tensor(out=vnT[:, :, 0:T], in0=vnT[:, :, 0:T], in1=bws[:, 0:4 * T].rearrange("p (a b) -> p a b", a=4), op=ALU.subtract),
                         r=[vnT, bws], w=[vnT])
                    bvt = self.bank()
                    S.op('pe', trs([(bvt[0:T, h * 128:(h + 1) * 128], vnT[:, h, 0:T], cm[:, IDENT, :]) for h in range(4)], None), r=[vnT, cm], w=[bvt])
                    S.op('act', lambda e: e.copy(out=vnew[0:T], in_=bvt[0:T, :].rearrange("p (a b) -> p a b", a=4)), r=[bvt], w=[vnew])
                    bo = self.bank_long
                    bo2 = self.bank()
                    lst = [(bo2[:, h * T:(h + 1) * T], vnew[0:T, h, :], aqkT[0:T, h, 0:T], True, True) for h in range(4)]
                    S.op('pe', mms(lst), r=[vnew, aqkT], w=[bo2])
                    S.op('act', lambda e: e.copy(out=osq[:, :, 0:T], in_=bo2[:, 0:4 * T].rearrange("p (a b) -> p a b", a=4)), r=[bo2], w=[osq])
                    for s in range(NSEQ):
                        sq_ = Sq[s % 2]
                        self.dma(sq_[:, :, :], I['sd'].t.ap()[l, s].rearrange("h k v -> k h v"), [I['sd']], [sq_])
                        S.op('pe', mms([(bo[:, h * T + 4 * s:h * T + 4 * s + 4], sq_[:, h, :], qgT[:, h, 4 * s:4 * s + 4], True, True) for h in range(4)]), r=[sq_, qgT], w=[bo])
                        vms = vmb[s % 2]
                        S.op('dve', lambda e, s=s, vms=vms: e.tensor_scalar(out=vms[0:T, :, :], in0=vnew[0:T, :, :], scalar1=cm[0:T, VS, 4 * s:4 * s + 1], scalar2=None, op0=ALU.mult),
                             r=[vnew, cm], w=[vms])
                        bS = self.bank()
                        S.op('pe', mms([(bS[:, h * 128:(h + 1) * 128], kd[0:T, h, :], vms[0:T, h, :], True, True) for h in range(4)]), r=[kd, vms], w=[bS])

                        def fS(e, s=s, sq_=sq_, bS=bS):
                            ins = None
                            for h in range(4):
                                ins = e.scalar_tensor_tensor(out=sq_[:, h, :], in0=sq_[:, h, :], scalar=glbc[:, h, 4 * s:4 * s + 1], in1=bS[:, h * 128:(h + 1) * 128],
                                                             op0=ALU.mult, op1=ALU.add)
                            return ins
                        S.op('dve', fS, r=[sq_, glbc, bS], w=[sq_])
                        self.dma(O['sdelta'].t.ap()[l, s].rearrange("h k v -> k h v"), sq_[:, :, :], [sq_], [O['sdelta']])
                    S.op('dve', lambda e: e.tensor_tensor(out=oT[:, :, 0:T], in0=osq[:, :, 0:T], in1=bo[:, 0:4 * T].rearrange("p (a b) -> p a b", a=4), op=ALU.add), r=[bo, osq], w=[oT])

                if samp and STOP == 's6':
                    return
                if samp and l == 0:
                    self.dump('oT', oT[:, :, 0:T], [128, 4, T], [oT])
                    self.dump('vnew', vnew[0:T, :, :], [T, 4, 128], [vnew])
                    self.dump('kd', kd[0:T, :, :], [T, 4, 128], [kd])
                    self.dump('wT', wT[:, :, 0:T], [128, 4, T], [wT])
                S.op('act', lambda e: e.activation(out=osq[:, :, 0:T], in_=oT[:, :, 0:T], func=AF.Square), r=[oT], w=[osq])
                b = self.bank()
                S.op('pe', mms([(b[:, h * T:(h + 1) * T], cm[:, ONES, :], osq[:, h, 0:T], True, True) for h in range(4)]), r=[osq, cm], w=[b])
                S.op('act', lambda e, b=b: e.activation(out=osq[:, :, 0:T], in_=b[:, 0:4 * T].rearrange("p (a b) -> p a b", a=4), func=AF.Ln, scale=1.0 / 128, bias=EPS),
                     r=[b], w=[osq])
                S.op('act', lambda e: e.activation(out=osq[:, :, 0:T], in_=osq[:, :, 0:T], func=AF.Exp, scale=-0.5), r=[osq], w=[osq])
                S.op('dve', lambda e: e.tensor_tensor(out=oT[:, :, 0:T], in0=oT[:, :, 0:T], in1=osq[:, :, 0:T], op=ALU.mult), r=[oT, osq], w=[oT])
                S.op('dve', lambda e: e.scalar_tensor_tensor(out=mixT[:, 4:8, 0:T], in0=oT[:, :, 0:T], scalar=P_['dng'][:, l:l + 1], in1=gateT[:, :, 0:T],
                                                             op0=ALU.mult, op1=ALU.mult), r=[oT, gateT, P_['dng']], w=[mixT])
                for half in range(2):
                    b = self.bank()
                    lst = []
                    for mc in range(4):
                        m = half * 4 + mc
                        for ec in range(8):
                            lst.append((b[:, mc * T:(mc + 1) * T], W_o[:, ec, m * 128:(m + 1) * 128], mixT[:, ec, 0:T], ec == 0, ec == 7))
                    S.op('pe', mms(lst), r=[W_o, mixT], w=[b])

                    def fres(e, b=b, half=half):
                        ins = None
                        for mc in range(4):
                            xa = xt_ap(half * 4 + mc)
                            ins = e.tensor_tensor(out=xa, in0=xa, in1=b[:, mc * T:(mc + 1) * T], op=ALU.add)
                        return ins
                    S.op('dve', fres, r=xkeys + [b], w=xkeys)

                if post is not None:
                    post()
                if record:
                    lists = S.lists
                    S.cur = None
                    S.lists = None
                    self.pool = 'all'
                    return lists

            ypk = Buf('d_ypT', O['ypT'].t, nsub=NT)
            for l in range(L):
                W_i, W_o = load_weights(l)
                S.op('pool', lambda e: e.memset(Sst1[:, :, :], 0.0), w=[Sst1])
                S.op('pool', lambda e: e.memset(convst1[:, :, :], 0.0), w=[convst1])
                src = I['xpT'] if l == 0 else ypk

                def ld(t):
                    xb = xTb[t % 3]
                    rk = [I['xpT']] if l == 0 else ypk.k(t)
                    self.dma(xb[:, :, :], src.t.ap()[:, t * 128:(t + 1) * 128].rearrange("(c p) t -> p c t", p=128), rk, [xb])
                ld(0)
                prevB = []
                for t in range(NT):
                    if t + 1 < NT:
                        ld(t + 1)
                    xb = xTb[t % 3]

                    def xt_ap(kc, xb=xb):
                        if kc is None:
                            return xb[:, :, :]
                        return xb[:, kc, :]

                    def post(t=t, xb=xb):
                        self.dma(O['ypT'].t.ap()[:, t * 128:(t + 1) * 128].rearrange("(c p) t -> p c t", p=128), xb[:, :, :], [xb], ypk.k(t))
                    lists = tile_layer(l, 128, xt_ap, W_i, W_o, False, t, t == 0, t == NT - 1, xb, record=True, post=post)
                    interleave(S, lists['f'], prevB)
                    prevB = lists['b']
                interleave(S, [], prevB)
            import os
            DO_SAMPLE = os.environ.get('K_NOSAMPLE') is None
            if not DO_SAMPLE:
                S.emit()
                return nc
            scr = self.sb('barscr', [128, 8])
            S.barrier({'act': lambda e: e.activation(out=scr[:, 0:2], in_=cm[:, ONES, 0:2], func=AF.Copy),
                       'dve': lambda e: e.memset(scr[:, 2:4], 0.0),
                       'pool': lambda e: e.memset(scr[:, 4:6], 0.0)})
            S.op('pool', lambda e: e.memset(VAs[:, :, 64:65], 1.0), w=[VAs])
            S.op('pool', lambda e: e.memset(VcA[:, :, :, 64:65], 1.0), w=[VcA])
            S.op('pool', lambda e: e.tensor_copy(out=msp[:, :], in_=cm[:, NMLS, 64:128]), r=[cm], w=[msp])
            S.op('pool', lambda e: e.tensor_copy(out=msc[:, :], in_=cm[:, NMUS, 64:128]), r=[cm], w=[msc])
            for kc in range(8):
                self.dma(xsT[:, kc, :], I['xsT'].t.ap()[kc * 128:(kc + 1) * 128, :], [I['xsT']], [xsT])
            for l in range(L):
                W_i, W_o = load_weights(l)

                def xs_ap(kc):
                    if kc is None:
                        return xsT[:, :, :]
                    return xsT[:, kc, :]
                tile_layer(l, TS, xs_ap, W_i, W_o, True, 0, False, False, xsT)
            for kc in range(8):
                self.dma(O['ysT'].t.ap()[kc * 128:(kc + 1) * 128, :], xsT[:, kc, :], [xsT], [O['ysT']])
            S.emit()
        return nc


def _consts(NT, NSEQ, PAST):
    i = np.arange(128)
    cm = np.zeros((128, 11, 128), np.float32)
    cm[:, 0] = 1.0
    cm[:, 1] = np.eye(128)
    cm[:, 2] = (i[:, None] <= i[None, :])
    cm[:, 3] = np.where(i[:, None] > i[None, :], 0.0, NEG)
    cm[:, 4] = np.where(i[None, :] >= i[:, None], 0.0, NEG)
    cm[:, 5] = (i[:, None] > i[None, :])
    cm[:, 6] = (i[:, None] <= i[None, :])
    j = np.arange(64)
    same = (j[:, None] // 4) == (j[None, :] // 4)
    cm[:64, 7, :64] = same & (j[:, None] <= j[None, :])
    cm[:64, 8, :64] = same
    cm[:64, 9, :64] = np.where(same & (j[:, None] > j[None, :]), 0.0, NEG)
    cm[:64, 10, :64] = np.where(same & (j[None, :] >= j[:, None]), 0.0, NEG)
    cm[:, 9, 64:128] = (i[:, None] > (j[None, :] % 4))
    cm[:64, 10, 64:128] = same & (j[:, None] <= j[None, :])
    half = 8
    inv = np.power(np.float32(500000.0), -np.arange(half, dtype=np.float32) / half).astype(np.float32)
    pos = (np.arange(NT)[None, :] * 128 + i[:, None]).astype(np.float32)
    ang = pos[:, :, None] * inv[None, None, :]
    cosp = np.cos(ang).astype(np.float32).reshape(128, NT * 8)
    sinp = np.sin(ang).astype(np.float32).reshape(128, NT * 8)
    poss = (PAST + (i % 4)).astype(np.float32)
    angs = poss[:, None] * inv[None, :]
    return cm.reshape(128, 11 * 128), cosp, sinp, np.cos(angs).astype(np.float32), np.sin(angs).astype(np.float32)


_CACHE = {}


def run(inputs, NT, L, G, NSEQ, PAST, ncores, nprompt):
    key = (NT, L, G, NSEQ)
    if key not in _CACHE:
        _CACHE[key] = Builder(NT, L, G, NSEQ).build()
    nc = _CACHE[key]
    f = lambda a: np.ascontiguousarray(np.asarray(a, dtype=np.float32))
    x_prompt = f(inputs['x_prompt'])
    x_sample = f(inputs['x_sample'])
    cm, cosp, sinp, coss, sins = _consts(NT, NSEQ, PAST)
    bcast = lambda a: f(np.broadcast_to(np.asarray(a, np.float32).reshape(1, -1), (128, a.size)))
    shared = {
        'w_in': f(inputs['w_in']), 'w_out': f(inputs['w_out']),
        'normg': f(np.asarray(inputs['norm_g']).reshape(L, 8, 128).transpose(2, 0, 1).reshape(128, L * 8)),
        'convw': f(np.asarray(inputs['conv_w']).reshape(L, 4, 12, 128).transpose(3, 0, 1, 2).reshape(128, L * 48)),
        'dng': f(np.asarray(inputs['dn_norm_g']).T),
        'gqk': bcast(np.concatenate([np.tile(np.asarray(inputs['q_norm_g']), (1, 8)), np.tile(np.asarray(inputs['k_norm_g']), (1, 2))], axis=1)),
        'sinkb': bcast(np.asarray(inputs['sinks'])), 'alogb': bcast(np.asarray(inputs['a_log'])), 'dtbb': bcast(np.asarray(inputs['dt_bias'])),
        'cosp': cosp, 'sinp': sinp, 'coss': coss, 'sins': sins, 'cmat': cm,
    }
    ck = f(inputs['cache_win_k']).reshape(L, -1, 128, 128)
    cv = f(inputs['cache_win_v']).reshape(L, -1, 128, 128)
    sc = f(inputs['state_conv'])
    sd = f(inputs['state_delta'])
    in_maps = []
    for c in range(ncores):
        b = c % nprompt
        sl = slice(c * NSEQ, (c + 1) * NSEQ)
        m = dict(shared)
        m['xpT'] = f(x_prompt[b].T)
        m['xsT'] = f(x_sample[sl].reshape(NSEQ * 4, D).T)
        m['ck'] = f(ck[:, sl])
        m['ckT'] = f(ck[:, sl].transpose(0, 1, 3, 2))
        m['cv'] = f(cv[:, sl])
        m['scT'] = f(sc[:, sl].transpose(0, 3, 1, 2).reshape(L, 1536, NSEQ * 3))
        m['sd'] = f(sd[:, sl])
        in_maps.append(m)
    res = run_bass_kernel_spmd(nc, in_maps, core_ids=list(range(ncores)))
    R = res.results
    global LAST
    LAST = R
    TP = NT * 128
    y_prompt = np.stack([R[b]['ypT'].T for b in range(nprompt)])
    y_sample = np.concatenate([R[c]['ysT'].T.reshape(NSEQ, 4, D) for c in range(ncores)], 0)
    pwk = np.stack([R[b]['pwk'] for b in range(nprompt)], 1).reshape(L, nprompt, 128, 2, 64)
    pwv = np.stack([R[b]['pwv'] for b in range(nprompt)], 1).reshape(L, nprompt, 128, 2, 64)
    pconv = np.stack([R[b]['pconv'].reshape(L, 128, 12, 3).transpose(0, 3, 2, 1).reshape(L, 3, 1536) for b in range(nprompt)], 1)
    pdelta = np.stack([R[b]['pdelta'] for b in range(nprompt)], 1)
    swk = np.concatenate([R[c]['swk'] for c in range(ncores)], 1).reshape(L, -1, 128, 2, 64)
    swv = np.concatenate([R[c]['swv'] for c in range(ncores)], 1).reshape(L, -1, 128, 2, 64)
    sconv = np.concatenate([R[c]['sconv'] for c in range(ncores)], 1)
    sdelta = np.concatenate([R[c]['sdelta'] for c in range(ncores)], 1)
    outs = (y_prompt, y_sample, pwk, pwv, pconv, pdelta, swk, swv, sconv, sdelta)
    return tuple(np.ascontiguousarray(o, dtype=np.float32) for o in outs)


def kernel(**inputs):
    return run(inputs, NT=64, L=4, G=2, NSEQ=16, PAST=8192, ncores=8, nprompt=2)
```

```python
import os
import numpy as np
import concourse.bass as bass
import concourse.mybir as mybir
from concourse.bass_utils import run_bass_kernel_spmd
from contextlib import ExitStack

F32 = mybir.dt.float32
BF16 = mybir.dt.bfloat16
ALU = mybir.AluOpType
F32R = mybir.dt.float32r
NEU_DT = F32R if os.environ.get('K_NEUF32') is None else mybir.dt.float32
AF = mybir.ActivationFunctionType
AX = mybir.AxisListType

D = 1024
NIN = 3336
EPS = 1e-6
import os
DBG = set(os.environ.get('K_DBG', '').split(','))
STOP = os.environ.get('K_STOP', '')
BBIAS = float(os.environ.get('K_BBIAS', '0'))
NEG = -1.0e9
C_AQ, C_AK, C_AV, C_AG, C_DQ, C_DG, C_DB, C_DA = 0, 512, 640, 768, 1280, 2816, 3328, 3332


class Buf:
    def __init__(self, name, t, nsub=1):
        self.name = name
        self.t = t
        self.nsub = nsub

    def keys(self):
        return [(self.name, i) for i in range(self.nsub)]

    def k(self, *idx):
        return [(self.name, i) for i in idx]

    def __getitem__(self, key):
        return self.t[key]


def _keys(lst):
    out = []
    for x in lst:
        if isinstance(x, Buf):
            out.extend(x.keys())
        elif isinstance(x, list):
            out.extend(x)
        else:
            out.append(x)
    return out


class Sched:
    ENGS = ['pe', 'act', 'dve', 'pool', 'sp']

    def __init__(self, nc, es, ndma=None):
        self.nc = nc
        ndma = ndma or {'sp': 24, 'act': 4, 'pool': 12}
        self.ops = {e: [] for e in self.ENGS}
        self.esem = {e: es.enter_context(nc.semaphore('se_' + e)) for e in self.ENGS}
        self.ecount = {e: 0 for e in self.ENGS}
        self.dsems = {q: [es.enter_context(nc.semaphore('sd_%s%d' % (q, i))) for i in range(n)]
                      for q, n in ndma.items()}
        self.dcount = {q: [0] * n for q, n in ndma.items()}
        self.dnext = {q: 0 for q in ndma}
        self.lastw = {}
        self.readers = {}
        self.waited = {e: {} for e in self.ENGS}
        self.semobj = {}
        self.nops = 0
        self.bar_tok = None

    def barrier(self, nops):
        keys = list(set(self.lastw) | set(self.readers))
        tok = None
        for e in ['act', 'dve', 'pool']:
            tok = self.op(e, nops[e], w=keys)
        self.bar_tok = tok

    def op(self, eng, fn, r=(), w=(), dma=False):
        r = _keys(r)
        w = _keys(w)
        deps = []
        for k in r:
            if k in self.lastw:
                deps.append(self.lastw[k])
            elif self.bar_tok is not None:
                deps.append(self.bar_tok)
        for k in w:
            if k in self.lastw:
                deps.append(self.lastw[k])
            elif self.bar_tok is not None:
                deps.append(self.bar_tok)
            deps.extend(self.readers.get(k, ()))
        if dma:
            j = self.dnext[eng]
            self.dnext[eng] = (j + 1) % len(self.dsems[eng])
            sem = self.dsems[eng][j]
            prev = self.dcount[eng][j]
            if prev > 0:
                deps.append((id(sem), prev))
            self.dcount[eng][j] = prev + 16
            tok = (id(sem), prev + 16)
            sig = (sem, 16)
        else:
            sem = self.esem[eng]
            self.ecount[eng] += 1
            tok = (id(sem), self.ecount[eng])
            sig = (sem, 1)
        self.semobj[id(sem)] = sem
        need = {}
        for (sid, v) in deps:
            if (not dma) and eng == 'pe' and sid == id(self.esem['pe']):
                continue
            if self.waited[eng].get(sid, 0) >= v:
                continue
            if need.get(sid, 0) < v:
                need[sid] = v
        for sid, v in need.items():
            self.waited[eng][sid] = v
        waits = [(self.semobj[sid], v) for sid, v in need.items()]
        self.ops[eng].append((waits, fn, sig))
        for k in w:
            self.lastw[k] = tok
            self.readers[k] = []
        for k in r:
            self.readers.setdefault(k, []).append(tok)
        self.nops += 1
        return tok

    def emit(self):
        nc = self.nc
        finals = []
        for q in self.dsems:
            for sem, c in zip(self.dsems[q], self.dcount[q]):
                if c > 0:
                    finals.append((sem, c))
        for e in self.ENGS:
            if e != 'sp' and self.ecount[e] > 0:
                finals.append((self.esem[e], self.ecount[e]))
        sched = self

        def run(engname, eng):
            for (waits, fn, sig) in sched.ops[engname]:
                for (sem, v) in waits:
                    eng.wait_ge(sem, v)
                ins = fn(eng)
                ins.then_inc(sig[0], sig[1])
            if engname == 'sp':
                for (sem, v) in finals:
                    eng.wait_ge(sem, v)

        with nc.Block() as block:
            @block.tensor
            def _(e):
                run('pe', e)

            @block.scalar
            def _(e):
                run('act', e)

            @block.vector
            def _(e):
                run('dve', e)

            @block.gpsimd
            def _(e):
                run('pool', e)

            @block.sync
            def _(e):
                run('sp', e)


class Rec:
    def __init__(self, sched):
        self.s = sched
        self.cur = None
        self.lists = None

    def op(self, *a, **k):
        if self.cur is None:
            return self.s.op(*a, **k)
        self.cur.append((a, k, float(getattr(a[1], 'cost', 1.0))))

    def play(self, item):
        a, k = item[0], item[1]
        return self.s.op(*a, **k)

    def barrier(self, *a, **k):
        return self.s.barrier(*a, **k)

    def emit(self):
        return self.s.emit()


class Merger:
    EST = {'pe': float(os.environ.get('K_EPE', '230')), 'act': float(os.environ.get('K_EACT', '480')), 'dve': float(os.environ.get('K_EDVE', '430')), 'pool': 700.0, 'sp': 100.0}
    LAT = float(os.environ.get('K_LAT', '120'))

    def __init__(self, rec):
        self.rec = rec
        self.lastw = {}
        self.readers = {}
        self.engfree = {}

    def _ready(self, item):
        (eng, fn), k = item[0], item[1]
        r = _keys(k.get('r', ()))
        w = _keys(k.get('w', ()))
        t = 0.0
        for key in r:
            t = max(t, self.lastw.get(key, 0.0))
        for key in w:
            t = max(t, self.lastw.get(key, 0.0), self.readers.get(key, 0.0))
        return max(t + self.LAT, self.engfree.get(eng, 0.0)), r, w

    def _commit(self, item, start, r, w):
        (eng, fn), k, cost = item
        if k.get('dma'):
            self.engfree[eng] = start + 100.0
            end = start + 2500.0
        else:
            end = start + self.EST[eng] * cost
            self.engfree[eng] = end
        for key in w:
            self.lastw[key] = end
            self.readers[key] = 0.0
        for key in r:
            self.readers[key] = max(self.readers.get(key, 0.0), end)
        self.rec.play(item)

    def merge(self, streams, gate=None, bias=None):
        idx = [0] * len(streams)
        while True:
            best = None
            for si, st in enumerate(streams):
                if idx[si] >= len(st):
                    continue
                if gate is not None and si == gate[1] and idx[gate[0]] < gate[2]:
                    continue
                c = self._ready(st[idx[si]])
                key = c[0] - (bias[si] if bias else 0.0)
                if best is None or key < best[2]:
                    best = (si, c, key)
            if best is None:
                break
            si, c = best[0], best[1]
            self._commit(streams[si][idx[si]], *c)
            idx[si] += 1


def bc(ap, shape):
    return ap.to_broadcast(list(shape))


class Builder:
    def __init__(self, NT, DEPTH, G, NSEQ=16):
        self.NT, self.L, self.G, self.NSEQ = NT, DEPTH, G, NSEQ
        self.TP = NT * 128
        self.TS = NSEQ * 4
        self.nc = bass.Bass("TRN2", target_bir_lowering=False)
        self.uid = 0
        self.sb_c = self.SB_LO
        self.sb_p = self.SB_HI
        self.sb_s = self.SB_HI

    SB_LO = 16512
    SB_HI = 229344

    def sb(self, name, shape, dt=F32, nsub=1, region='c'):
        n = 1
        for d in shape[1:]:
            n *= d
        nbytes = ((n * (2 if dt == BF16 else 4)) + 31) // 32 * 32
        if region == 'c':
            off = self.sb_c
            self.sb_c += nbytes
        elif region == 'p':
            self.sb_p -= nbytes
            off = self.sb_p
        else:
            self.sb_s -= nbytes
            off = self.sb_s
        assert self.sb_c <= min(self.sb_p, self.sb_s), ('SBUF overflow', name, self.sb_c, self.sb_p, self.sb_s)
        t = self.nc.alloc_sbuf_tensor_at('sb_' + name, list(shape), dt, offset=off)
        return Buf(name, t, nsub)

    def din(self, name, shape):
        t = self.nc.dram_tensor(name, list(shape), F32, kind="ExternalInput")
        return Buf('d_' + name, t)

    def dout(self, name, shape):
        t = self.nc.dram_tensor(name, list(shape), F32, kind="ExternalOutput")
        return Buf('d_' + name, t)

    def dump(self, name, ap, shape, rkeys):
        if 'dump' not in DBG:
            return
        t = self.nc.dram_tensor('dbg_' + name, list(shape), F32, kind="ExternalOutput")
        b = Buf('dd_' + name, t)
        self.dma(t.ap(), ap, rkeys, [b])

    def bank(self):
        if self.pool == 'a':
            b = self.banks[self.bif % 3]
            self.bif += 1
        elif self.pool == 'd':
            b = self.banks[3 + self.bid % 2]
            self.bid += 1
        elif self.pool == 'b':
            b = (self.banks[5], self.bank_long)[self.bib % 2]
            self.bib += 1
        else:
            b = self.banks[self.bi % len(self.banks)]
            self.bi += 1
        return b

    def dma(self, out_ap, in_ap, r, w, q='sp'):
        self.S.op(q, lambda e, o=out_ap, i=in_ap: e.dma_start(out=o, in_=i), r=r, w=w, dma=True)

    def build(self):
        nc = self.nc
        L, NT, G, NSEQ, TP, TS = self.L, self.NT, self.G, self.NSEQ, self.TP, self.TS
        with ExitStack() as es:
            self.es = es
            S = self.S = Rec(Sched(nc, es))
            I = self.I = {}
            for name, shape in [
                ('xpT', [D, TP]), ('xsT', [D, TS]), ('w_in', [L, D, NIN]), ('w_out', [L, D, D]),
                ('ckT', [L, NSEQ, 128, 128]), ('ck', [L, NSEQ, 128, 128]), ('cv', [L, NSEQ, 128, 128]),
                ('scT', [L, 1536, NSEQ * 3]), ('sd', [L, NSEQ, 4, 128, 128]),
                ('normg', [128, L * 8]), ('convw', [128, L * 48]), ('dng', [128, L]),
                ('gqk', [128, L * 640]), ('sinkb', [128, L * 8]), ('alogb', [128, L * 4]), ('dtbb', [128, L * 4]),
                ('cosp', [128, NT * 8]), ('sinp', [128, NT * 8]), ('coss', [128, 8]), ('sins', [128, 8]),
                ('cmat', [128, 11 * 128]),
            ]:
                I[name] = self.din(name, shape)
            O = self.O = {}
            for name, shape in [
                ('ypT', [D, TP]), ('ysT', [D, TS]),
                ('pwk', [L, 128, 128]), ('pwv', [L, 128, 128]), ('pconv', [L, 128, 36]), ('pdelta', [L, 4, 128, 128]),
                ('swk', [L, NSEQ, 128, 128]), ('swv', [L, NSEQ, 128, 128]), ('sconv', [L, NSEQ, 3, 1536]),
                ('sdelta', [L, NSEQ, 4, 128, 128]),
            ]:
                O[name] = self.dout(name, shape)

            self.banks = [Buf('ps%d' % i, es.enter_context(nc.psum_tensor('ps%d' % i, [128, 512], F32))) for i in range(6)]
            self.bi = 0
            self.bif = 0
            self.bib = 0
            self.bid = 0
            self.pool = 'all'
            self.bank_long = Buf('ps6', es.enter_context(nc.psum_tensor('ps6', [128, 512], F32)))
            self.psb = Buf('psb', es.enter_context(nc.psum_tensor('psb', [128, 1024], BF16)))

            cm = self.sb('cmat', [128, 11, 128])
            self.dma(cm[:, :, :], I['cmat'].t.ap().rearrange("p (a b) -> p a b", a=11), [I['cmat']], [cm])
            self.cm = cm
            ONES, IDENT, UM, NML, NMU, MPREV, MCUR, US, VS, NMLS, NMUS = range(11)
            identb = self.sb('identb', [128, 128], BF16)
            self.dma(identb[:, :], I['cmat'].t.ap()[:, 128:256], [I['cmat']], [identb], q='pool')
            msp = self.sb('msp', [128, 64], region='s')
            msc = self.sb('msc', [128, 64], region='s')
            P_ = {}
            for name, n in [('normg', L * 8), ('convw', L * 48), ('dng', L), ('sinkb', L * 8),
                            ('alogb', L * 4), ('dtbb', L * 4), ('cosp', NT * 8), ('sinp', NT * 8), ('coss', 8), ('sins', 8)]:
                P_[name] = self.sb('c_' + name, [128, n])
                self.dma(P_[name][:, :], I[name].t.ap(), [I[name]], [P_[name]])
            esink = self.sb('esink', [128, L * 8])
            negA = self.sb('negA', [128, L * 4])
            S.op('act', lambda e: e.activation(out=esink[:, :], in_=P_['sinkb'][:, :], func=AF.Exp), r=[P_['sinkb']], w=[esink])
            S.op('act', lambda e: e.activation(out=negA[:, :], in_=P_['alogb'][:, :], func=AF.Exp), r=[P_['alogb']], w=[negA])
            S.op('dve', lambda e: e.tensor_scalar(out=negA[:, :], in0=negA[:, :], scalar1=-1.0, scalar2=None, op0=ALU.mult), r=[negA], w=[negA])
            self.P_ = P_

            xTb = [self.sb('xT%d' % i, [128, 8, 128], region='p') for i in range(3)]
            xsT = self.sb('xsT', [128, 8, TS], region='s')
            Wi = [self.sb('Wi%d' % i, [128, 8, NIN], BF16) for i in range(1)]
            Wo = [self.sb('Wo%d' % i, [128, 8, D], BF16) for i in range(1)]
            gqk_l = self.sb('gqk_l', [128, 640])
            sq2 = self.sb('sq2', [128, 640])
            rstd = self.sb('rstd', [128, 128])
            xnT = self.sb('xnT', [128, 8, 128], BF16)
            zq = self.sb('zq', [128, 640])
            kvs = self.sb('kvs', [128, 136])
            gat = self.sb('gat', [128, 512])
            ss10 = self.sb('ss10', [128, 10])
            qr = self.sb('qr', [128, 10, 64])
            rt = [self.sb('rt%d' % i, [128, 10, 8]) for i in range(4)]
            QTraw = self.sb('QT', [128, 512], BF16)
            KT1 = [self.sb('KT%d' % i, [128, 128], BF16, region='p') for i in range(2)]
            KT = [KT1 for l in range(L)]
            VA1 = [self.sb('VA%d' % i, [128, 2, 65], BF16, region='p') for i in range(2)]
            VA = [VA1 for l in range(L)]
            KTs = self.sb('KTs', [128, 64], BF16, region='s')
            VAs = self.sb('VAs', [128, 2, 65], BF16, region='s')
            pT = [self.sb('pT%d' % i, [128, 4, 128], BF16) for i in range(4)]
            den = self.sb('den', [128, 8])
            att = self.sb('att', [128, 8, 64])
            attg = self.sb('attg', [128, 512], BF16)
            mixTP = [self.sb('mixT%d' % i, [128, 8, 128], BF16) for i in range(2)]
            zdT = self.sb('zdT', [128, 12, 131], region='p')
            zdTs = self.sb('zdTs', [128, 12, NSEQ, 7], region='s')
            scst = self.sb('scst', [128, 12, NSEQ * 3], region='s')
            convst1 = self.sb('convst', [128, 12, 3], region='p')
            convst = [convst1 for l in range(L)]
            gateTP = [self.sb('gateT%d' % i, [128, 4, 128]) for i in range(2)]
            cacc = self.sb('cacc', [128, 12, 128])
            sqt = cacc
            rn8 = None
            qkn = self.sb('qkn', [128, 8, 128])
            sc4 = {n: self.sb('sc_' + n, [128, 4]) for n in ['beta', 'nbeta', 'g', 'sp', 'Gc', 'Gt', 'eG', 'bg', 'ekd', 'gl']}
            gb = self.sb('gb', [128, 4, 128])
            glbc = self.sb('glbc', [128, 4, 64], region='s')
            qgTP = [self.sb('qgT%d' % i, [128, 4, 128]) for i in range(2)]
            argL = gb
            tLU = self.sb('tLU', [128, 8, 128])
            DLU = tLU
            rn8 = tLU
            CxP = [self.sb('CxP%d' % i, [128, 4, 128], NEU_DT) for i in range(2)]
            CtP = [self.sb('CtP%d' % i, [128, 4, 128], NEU_DT) for i in range(2)]
            R0P = [self.sb('R0P%d' % i, [128, 4, 128], NEU_DT) for i in range(2)]
            XT = self.sb('XT', [128, 4, 128], NEU_DT)
            XtT = self.sb('XtT', [128, 4, 128], NEU_DT)
            RT = self.sb('RT', [128, 4, 128], NEU_DT)
            aqkTP = [self.sb('aqkT%d' % i, [128, 4, 128]) for i in range(2)]
            kdP = [self.sb('kd%d' % i, [128, 4, 128]) for i in range(2)]
            kbgP = [self.sb('kbg%d' % i, [128, 4, 128]) for i in range(2)]
            vbP = [self.sb('vb%d' % i, [128, 4, 128]) for i in range(2)]
            glP = [self.sb('gl%d' % i, [128, 4]) for i in range(2)]
            u_sb = self.sb('u_sb', [128, 4, 128], region='p')
            wT = self.sb('wT', [128, 4, 128])
            vnew = self.sb('vnew', [128, 4, 128])
            vnT = self.sb('vnT', [128, 4, 64], region='s')
            oT = self.sb('oT', [128, 4, 128])
            osq = self.sb('osq', [128, 4, 128])
            Sst1 = self.sb('Sst', [128, 4, 128], region='p')
            Sst = [Sst1 for l in range(L)]
            Sq = [self.sb('Sq%d' % i, [128, 4, 128], region='s') for i in range(2)]
            vmb = [self.sb('vm%d' % i, [64, 4, 128], region='s') for i in range(2)]
            KcT = self.sb('KcT', [128, NSEQ, 128], BF16, region='s')
            VcA = self.sb('VcA', [128, NSEQ, 2, 65], BF16, region='s')
            OTs = self.sb('OTs', [65, 2, 256], region='s')

            for i in range(2):
                S.op('pool', lambda e, t=VA1[i]: e.memset(t[:, :, 64:65], 1.0), w=[VA1[i]])

            wparity = [0]

            def load_weights(l):
                p = 0
                self.dma(gqk_l[:, :], I['gqk'].t.ap()[:, l * 640:(l + 1) * 640], [I['gqk']], [gqk_l])
                for kc in range(8):
                    self.dma(Wi[p][:, kc, :], I['w_in'].t.ap()[l, kc * 128:(kc + 1) * 128, :], [I['w_in']], [Wi[p]], q='pool')
                for kc in range(8):
                    self.dma(Wo[p][:, kc, :], I['w_out'].t.ap()[l, kc * 128:(kc + 1) * 128, :], [I['w_out']], [Wo[p]], q='pool')
                return Wi[p], Wo[p]

            USE_R = False

            def f32(ap):
                return ap.bitcast(F32) if ap.dtype == F32R else ap

            def rr(ap):
                if USE_R and ap.dtype == F32:
                    return ap.bitcast(F32R)
                return ap

            def mm(out, lhsT, rhs, start=True, stop=True):
                return lambda e: e.matmul(out, lhsT=rr(lhsT), rhs=rr(rhs), start=start, stop=stop)

            def mms(lst):
                def f(e):
                    ins = None
                    for (o, a, b, st, sp) in lst:
                        ins = e.matmul(o, lhsT=rr(a), rhs=rr(b), start=st, stop=sp)
                    return ins
                f.cost = len(lst)
                return f

            def trs(lst, ident):
                def f(e):
                    ins = None
                    for (o, a, idn) in lst:
                        ins = e.transpose(o, a, idn)
                    return ins
                f.cost = len(lst)
                return f

            def tile_layer(l, T, xt_ap, W_i, W_o, samp, tidx, first, last_of_seq, xbuf, record=False, post=None):
                xkeys = [xbuf]
                pp2 = 0 if samp else tidx % 2
                mixT, gateT, qgT, aqkT, kd, kbg, vb, glv = mixTP[pp2], gateTP[pp2], qgTP[pp2], aqkTP[pp2], kdP[pp2], kbgP[pp2], vbP[pp2], glP[pp2]
                if record:
                    S.lists = {'a': [], 'd': [], 'b': [], 'np': 0}
                    S.cur = S.lists['a']
                    self.pool = 'a'
                QT = Buf('QT', QTraw[:, 0:4 * T].rearrange("p (a b) -> p a b", a=4))
                np_ = l * 8
                S.op('act', lambda e: e.activation(out=sqt[:, 0:8, 0:T], in_=xt_ap(None), func=AF.Square), r=xkeys, w=[sqt])
                b = self.bank()
                S.op('pe', mms([(b[:, 0:T], cm[:, ONES, :], sqt[:, kc, 0:T], kc == 0, kc == 7) for kc in range(8)]), r=[sqt, cm], w=[b])
                S.op('act', lambda e, b=b: e.activation(out=rstd[:, 0:T], in_=b[:, 0:T], func=AF.Ln, scale=1.0 / D, bias=EPS), r=[b], w=[rstd])
                S.op('act', lambda e: e.activation(out=rstd[:, 0:T], in_=rstd[:, 0:T], func=AF.Exp, scale=-0.5), r=[rstd], w=[rstd])

                def fxn(e):
                    ins = None
                    for kc in range(8):
                        ins = e.scalar_tensor_tensor(out=xnT[:, kc, 0:T], in0=xt_ap(kc), scalar=P_['normg'][:, np_ + kc:np_ + kc + 1],
                                                     in1=rstd[:, 0:T], op0=ALU.mult, op1=ALU.mult)
                    return ins
                fxn.cost = 8
                S.op('dve', fxn, r=xkeys + [rstd, P_['normg']], w=[xnT])

                def tokproj(c0, n, dst_ops):
                    b = self.bank()
                    S.op('pe', mms([(b[0:T, 0:n], xnT[:, kc, 0:T], W_i[:, kc, c0:c0 + n], kc == 0, kc == 7) for kc in range(8)]),
                         r=[xnT, W_i], w=[b])
                    return b
                bq = tokproj(C_AQ, 512, None)
                S.op('act', lambda e, b=bq: e.copy(out=zq[0:T, 0:512].rearrange("p (j g d) -> p g j d", g=2, d=64), in_=b[0:T, 0:512].rearrange("p (g j d) -> p g j d", g=2, d=64)), r=[bq], w=[zq])
                bkv = tokproj(C_AK, 256, None)
                S.op('act', lambda e, b=bkv: e.copy(out=zq[0:T, 512:640], in_=b[0:T, 0:128]), r=[bkv], w=[zq])
                S.op('act', lambda e, b=bkv: e.copy(out=kvs[0:T, 0:128], in_=b[0:T, 128:256]), r=[bkv], w=[kvs])
                bg_ = tokproj(C_AG, 512, None)
                S.op('act', lambda e, b=bg_: e.activation(out=gat[0:T, :], in_=b[0:T, 0:512], func=AF.Silu), r=[bg_], w=[gat])
                bs = tokproj(C_DB, 8, None)
                S.op('act', lambda e, b=bs: e.copy(out=kvs[0:T, 128:136], in_=b[0:T, 0:8]), r=[bs], w=[kvs])
                if samp:
                    for j in range(3):
                        bz = tokproj(C_DQ + 512 * j, 512, None)
                        S.op('act', lambda e, b=bz, j=j: e.copy(out=cacc[0:T, 4 * j:4 * j + 4, :].rearrange("p a b -> p (a b)"), in_=b[0:T, 0:512]), r=[bz], w=[cacc])
                    for i in range(1, 4):
                        if 'nosconv' in DBG:
                            break
                        self.dma(O['sconv'].t.ap()[l, :, i - 1, :], cacc[i:T:4, :, :].rearrange("p a b -> p (a b)"), [cacc], [O['sconv']])

                if record:
                    S.lists['np'] = len(S.lists['a'])
                if samp and STOP == 's1':
                    return
                S.op('act', lambda e: e.activation(out=sq2[0:T, :], in_=zq[0:T, :], func=AF.Square), r=[zq], w=[sq2])
                S.op('dve', lambda e: e.tensor_reduce(out=ss10[0:T, :], in_=sq2[0:T, :].rearrange("p (a b) -> p a b", a=10), axis=AX.X, op=ALU.add),
                     r=[sq2], w=[ss10])
                S.op('act', lambda e: e.activation(out=ss10[0:T, :], in_=ss10[0:T, :], func=AF.Ln, scale=1.0 / 64, bias=EPS), r=[ss10], w=[ss10])
                S.op('act', lambda e: e.activation(out=ss10[0:T, :], in_=ss10[0:T, :], func=AF.Exp, scale=-0.5), r=[ss10], w=[ss10])
                S.op('dve', lambda e: e.tensor_tensor(out=qr[0:T, :, :], in0=zq[0:T, :].rearrange("p (a b) -> p a b", a=10),
                                                      in1=bc(ss10[0:T, :, None], [T, 10, 64]), op=ALU.mult), r=[zq, ss10], w=[qr])
                S.op('dve', lambda e: e.tensor_tensor(out=qr[0:T, :, :], in0=qr[0:T, :, :],
                                                      in1=gqk_l[0:T, :].rearrange("p (a b) -> p a b", a=10), op=ALU.mult),
                     r=[qr, gqk_l], w=[qr])
                if samp:
                    cos_ap = P_['coss'][0:T, :]
                    sin_ap = P_['sins'][0:T, :]
                else:
                    cos_ap = P_['cosp'][0:T, tidx * 8:(tidx + 1) * 8]
                    sin_ap = P_['sinp'][0:T, tidx * 8:(tidx + 1) * 8]
                cosb = bc(cos_ap.unsqueeze(1), [T, 10, 8])
                sinb = bc(sin_ap.unsqueeze(1), [T, 10, 8])
                x1 = qr[0:T, :, 0:8]
                x2 = qr[0:T, :, 8:16]
                S.op('dve', lambda e: e.tensor_tensor(out=rt[0][0:T], in0=x1, in1=cosb, op=ALU.mult), r=[qr, P_['cosp'], P_['coss']], w=[rt[0]])
                S.op('dve', lambda e: e.tensor_tensor(out=rt[1][0:T], in0=x2, in1=sinb, op=ALU.mult), r=[qr, P_['sinp'], P_['sins']], w=[rt[1]])
                S.op('pool', lambda e: e.tensor_tensor(out=rt[2][0:T], in0=x2, in1=cosb, op=ALU.mult), r=[qr, P_['cosp'], P_['coss']], w=[rt[2]])
                S.op('pool', lambda e: e.tensor_tensor(out=rt[3][0:T], in0=x1, in1=sinb, op=ALU.mult), r=[qr, P_['sinp'], P_['sins']], w=[rt[3]])
                S.op('dve', lambda e: e.tensor_tensor(out=x1, in0=rt[0][0:T], in1=rt[1][0:T], op=ALU.subtract), r=[rt[0], rt[1], rt[2], rt[3]], w=[qr])
                S.op('dve', lambda e: e.tensor_tensor(out=x2, in0=rt[2][0:T], in1=rt[3][0:T], op=ALU.add), r=[rt[2], rt[3]], w=[qr])
                if samp and 'noswk' in DBG:
                    pass
                elif samp:
                    for i in range(4):
                        self.dma(O['swk'].t.ap()[l, :, 124 + i, :], qr[i:T:4, 8:10, :].rearrange("p a b -> p (a b)"), [qr], [O['swk']])
                        self.dma(O['swv'].t.ap()[l, :, 124 + i, :], kvs[i:T:4, 0:128], [kvs], [O['swv']])
                    self.dma(O['swk'].t.ap()[l, :, 0:124, :], I['ck'].t.ap()[l, :, 4:128, :], [I['ck']], [O['swk']])
                    self.dma(O['swv'].t.ap()[l, :, 0:124, :], I['cv'].t.ap()[l, :, 4:128, :], [I['cv']], [O['swv']])
                elif last_of_seq:
                    self.dma(O['pwk'].t.ap()[l], qr[0:T, 8:10, :], [qr], [O['pwk']])
                    self.dma(O['pwv'].t.ap()[l], kvs[0:T, 0:128], [kvs], [O['pwv']])
                par = tidx % 2
                KTc = KTs if samp else KT[l][par]
                VAc = VAs if samp else VA[l][par]
                b = self.bank()
                S.op('pe', trs([(b[:, j * T:(j + 1) * T], qr[0:T, 2 * j:2 * j + 2, :].rearrange("p a b -> p (a b)"), cm[0:T, IDENT, 0:T]) for j in range(4)], None), r=[qr, cm], w=[b])
                S.op('act', lambda e, b=b: e.copy(out=QT[:, :, :], in_=b[:, 0:4 * T].rearrange("p (a b) -> p a b", a=4)), r=[b], w=[QT])
                b2 = self.bank()
                S.op('pe', trs([(b2[:, 0:T], qr[0:T, 8:10, :].rearrange("p a b -> p (a b)"), cm[0:T, IDENT, 0:T])], None), r=[qr, cm], w=[b2])
                S.op('act', lambda e, b=b2: e.copy(out=KTc[:, 0:T], in_=b[:, 0:T]), r=[b2], w=[KTc])
                S.op('dve', lambda e: e.tensor_copy(out=VAc[0:T, :, 0:64], in_=kvs[0:T, 0:128].rearrange("p (a b) -> p a b", a=2)), r=[kvs], w=[VAc])

                if samp and STOP == 's2':
                    return
                poA = self.bank()
                poB = self.bank()
                if not samp:
                    blocks = []
                    if not first:
                        blocks.append((KT[l][1 - par], VA[l][1 - par], MPREV))
                    blocks.append((KTc, VAc, MCUR))
                    pts = []
                    n = 0
                    for (ktb, vab, mslot) in blocks:
                        for kvh in range(2):
                            h0 = 64 * kvh
                            bsc = self.bank()
                            S.op('pe', mm(bsc[:, :], ktb[h0:h0 + 64, :], QT[h0:h0 + 64, :, :].rearrange("p a b -> p (a b)")), r=[ktb, QT], w=[bsc])
                            p = pT[n]
                            n += 1
                            S.op('act', lambda e, p=p, b=bsc: e.activation(out=p[:, :, :], in_=b[:, :].rearrange("p (a b) -> p a b", a=4),
                                                                          func=AF.Exp, scale=0.125), r=[bsc], w=[p])
                            S.op('dve', lambda e, p=p, m=mslot: e.tensor_tensor(out=p[:, :, :], in0=p[:, :, :], in1=bc(cm[:, m, :].unsqueeze(1), [128, 4, 128]),
                                                                               op=ALU.mult), r=[p, cm], w=[p])
                            pts.append((p, vab, kvh))
                    for h in range(8):
                        kvh, j = h // 4, h % 4
                        po = poA if kvh == 0 else poB
                        lst = [(p, vab) for (p, vab, kv) in pts if kv == kvh]
                        S.op('pe', mms([(po[:, j * 65:(j + 1) * 65], p[:, j, :], vab[:, kvh, :], i == 0, i == len(lst) - 1) for i, (p, vab) in enumerate(lst)]),
                             r=[x[0] for x in lst] + [x[1] for x in lst], w=[po])
                else:
                    self.dma(KcT[:, :, :], I['ckT'].t.ap()[l].rearrange("s c k -> c s k"), [I['ckT']], [KcT], q='pool')
                    for a_ in range(2):
                        self.dma(VcA[:, :, a_, 0:64], I['cv'].t.ap()[l][:, :, a_ * 64:(a_ + 1) * 64].rearrange("s k c -> k s c"), [I['cv']], [VcA], q='pool')
                    pc = pT[0]
                    pp = pT[1]
                    pcf = pc[0:64, :, :].rearrange("p a b -> p (a b)")
                    ppf = pp[:, :, :].rearrange("p a b -> p (a b)")
                    for kvh in range(2):
                        bsc = self.bank()
                        S.op('pe', mm(bsc[0:64, 0:256], KTs[64 * kvh:64 * kvh + 64, 0:64], QT[64 * kvh:64 * kvh + 64, :, :].rearrange("p a b -> p (a b)")),
                             r=[KTs, QT], w=[bsc])
                        S.op('act', lambda e, b=bsc, kvh=kvh: e.activation(out=pcf[:, kvh * 256:(kvh + 1) * 256], in_=b[0:64, 0:256], func=AF.Exp, scale=0.125), r=[bsc], w=[pc])
                    S.op('dve', lambda e: e.tensor_tensor(out=pcf.rearrange("p (a b) -> p a b", a=8), in0=pcf.rearrange("p (a b) -> p a b", a=8),
                                                           in1=bc(msc[0:64, :].unsqueeze(1), [64, 8, 64]), op=ALU.mult), r=[pc, msc], w=[pc])
                    if STOP == 's2a':
                        return
                    for kvh in range(2):
                        bsp = self.bank()
                        lst = []
                        for s in range(NSEQ):
                            for j in range(4):
                                lst.append((bsp[:, j * 64 + 4 * s:j * 64 + 4 * s + 4], KcT[64 * kvh:64 * kvh + 64, s, :], QT[64 * kvh:64 * kvh + 64, j, 4 * s:4 * s + 4], True, True))
                        S.op('pe', mms(lst), r=[KcT, QT], w=[bsp])
                        S.op('act', lambda e, b=bsp, kvh=kvh: e.activation(out=ppf[:, kvh * 256:(kvh + 1) * 256], in_=b[:, 0:256], func=AF.Exp, scale=0.125), r=[bsp], w=[pp])
                    S.op('dve', lambda e: e.tensor_tensor(out=ppf.rearrange("p (a b) -> p a b", a=8), in0=ppf.rearrange("p (a b) -> p a b", a=8),
                                                           in1=bc(msp[:, :].unsqueeze(1), [128, 8, 64]), op=ALU.mult), r=[pp, msp], w=[pp])
                    if STOP == 's2b':
                        return
                    for kvh in range(2):
                        bo = self.bank()
                        lst = [(bo[0:65, 0:256], VAs[0:64, kvh, :], pcf[:, kvh * 256:(kvh + 1) * 256], True, False)]
                        for s in range(NSEQ):
                            for j in range(4):
                                c0 = j * 64 + 4 * s
                                lst.append((bo[0:65, c0:c0 + 4], VcA[:, s, kvh, :], ppf[:, kvh * 256 + c0:kvh * 256 + c0 + 4], False, (s == NSEQ - 1 and j == 3)))
                        S.op('pe', mms(lst), r=[VAs, VcA, pc, pp], w=[bo])
                        S.op('act', lambda e, b=bo, kvh=kvh: e.copy(out=OTs[:, kvh, :], in_=b[0:65, 0:256]), r=[bo], w=[OTs])
                    if STOP == 's2c':
                        return
                    for kvh in range(2):
                        po = poA if kvh == 0 else poB
                        S.op('pe', trs([(po[0:64, j * 65:(j + 1) * 65], OTs[:, kvh, j * 64:(j + 1) * 64], cm[0:65, IDENT, 0:65]) for j in range(4)], None),
                             r=[OTs, cm], w=[po])
                if samp and STOP == 's2d':
                    return
                for i, po in enumerate((poA, poB)):
                    S.op('dve', lambda e, po=po, i=i: e.tensor_tensor(out=den[0:T, 4 * i:4 * i + 4], in0=po[0:T, 0:260].rearrange("p (a b) -> p a b", a=4)[:, :, 64],
                                                                      in1=esink[0:T, l * 8 + 4 * i:l * 8 + 4 * i + 4], op=ALU.add), r=[po, esink], w=[den])
                S.op('dve', lambda e: e.reciprocal(out=den[0:T, :], in_=den[0:T, :]), r=[den], w=[den])
                for i, po in enumerate((poA, poB)):
                    S.op('dve', lambda e, po=po, i=i: e.tensor_tensor(out=att[0:T, 4 * i:4 * i + 4, :], in0=po[0:T, 0:260].rearrange("p (a b) -> p a b", a=4)[:, :, 0:64],
                                                                      in1=bc(den[0:T, 4 * i:4 * i + 4, None], [T, 4, 64]), op=ALU.mult), r=[po, den], w=[att])
                S.op('dve', lambda e: e.tensor_tensor(out=attg[0:T, :], in0=att[0:T, :, :].rearrange("p a b -> p (a b)"), in1=gat[0:T, :], op=ALU.mult),
                     r=[att, gat], w=[attg])
                S.op('pe', trs([(self.psb[:, j * T:(j + 1) * T], attg[0:T, j * 128:(j + 1) * 128], identb[0:T, 0:T]) for j in range(4)], None),
                     r=[attg, identb], w=[self.psb])
                S.op('act', lambda e: e.copy(out=mixT[:, 0:4, 0:T], in_=self.psb[:, 0:4 * T].rearrange("p (a b) -> p a b", a=4)), r=[self.psb], w=[mixT])

                if record:
                    S.cur = S.lists['d']
                    self.pool = 'd'
                if samp and STOP == 's3':
                    return
                if samp:
                    self.dma(scst[:, :, :], I['scT'].t.ap()[l].rearrange("(c p) f -> p c f", p=128), [I['scT']], [scst])
                    S.op('pool', lambda e: e.tensor_copy(out=zdTs[:, :, :, 0:3], in_=scst[:, :, :].rearrange("p c (s j) -> p c s j", j=3)), r=[scst], w=[zdTs])
                else:
                    S.op('pool', lambda e: e.tensor_copy(out=zdT[:, :, 0:3], in_=convst[l][:, :, :]), r=[convst[l]], w=[zdT])
                for rnd in range(4):
                    b = self.bank()
                    for cc in range(4):
                        c = rnd * 4 + cc
                        col = (C_DQ + 128 * c) if c < 12 else (C_DG + 128 * (c - 12))
                        lst = []
                        for kc in range(8):
                            lst.append((b[:, cc * T:(cc + 1) * T], W_i[:, kc, col:col + 128], xnT[:, kc, 0:T], kc == 0, kc == 7))
                        S.op('pe', mms(lst), r=[W_i, xnT], w=[b])
                    bv = b[:, 0:4 * T].rearrange("p (a b) -> p a b", a=4)
                    if rnd < 3:
                        if samp:
                            S.op('act', lambda e, bv=bv, rnd=rnd: e.copy(out=zdTs[:, 4 * rnd:4 * rnd + 4, :, 3:7],
                                                                          in_=bv.rearrange("p a (s j) -> p a s j", j=4)), r=[b], w=[zdTs])
                        else:
                            S.op('act', lambda e, bv=bv, rnd=rnd: e.copy(out=zdT[:, 4 * rnd:4 * rnd + 4, 3:3 + T], in_=bv), r=[b], w=[zdT])
                    else:
                        S.op('act', lambda e, bv=bv: e.activation(out=gateT[:, :, 0:T], in_=bv, func=AF.Silu), r=[b], w=[gateT])
                if not samp:
                    S.op('pool', lambda e: e.tensor_copy(out=convst[l][:, :, :], in_=zdT[:, :, T:T + 3]), r=[zdT], w=[convst[l]])
                cw0 = l * 48

                def tap(c, j):
                    if samp:
                        return zdTs[:, c, :, j:j + 4]
                    return zdT[:, c, j:j + T]

                def acc(c):
                    if samp:
                        return cacc[:, c, 0:T].rearrange("p (s j) -> p s j", j=4)
                    return cacc[:, c, 0:T]
                zk = [zdTs] if samp else [zdT]

                def fconv0(e):
                    ins = None
                    for c in range(12):
                        ins = e.activation(out=acc(c), in_=tap(c, 3), func=AF.Copy, scale=P_['convw'][:, cw0 + 36 + c:cw0 + 37 + c])
                    return ins
                fconv0.cost = 12
                S.op('act', fconv0, r=zk + [P_['convw']], w=[cacc])
                for j in range(3):
                    def fconv(e, j=j):
                        ins = None
                        for c in range(12):
                            ins = e.scalar_tensor_tensor(out=acc(c), in0=tap(c, j), scalar=P_['convw'][:, cw0 + 12 * j + c:cw0 + 12 * j + c + 1],
                                                         in1=acc(c), op0=ALU.mult, op1=ALU.add)
                        return ins
                    fconv.cost = 12
                    S.op('dve', fconv, r=zk + [P_['convw'], cacc], w=[cacc])
                S.op('act', lambda e: e.activation(out=cacc[:, :, 0:T], in_=cacc[:, :, 0:T], func=AF.Silu), r=[cacc], w=[cacc])
                S.op('act', lambda e: e.activation(out=qkn[:, 0:8, 0:T], in_=cacc[:, 0:8, 0:T], func=AF.Square), r=[cacc], w=[qkn])
                for half in range(2):
                    b = self.bank()
                    S.op('pe', mms([(b[:, cc * T:(cc + 1) * T], cm[:, ONES, :], qkn[:, 4 * half + cc, 0:T], True, True) for cc in range(4)]), r=[qkn, cm], w=[b])
                    sc, bi_ = (128.0, 128.0 * EPS) if half == 0 else (1.0, EPS)
                    S.op('act', lambda e, b=b, half=half, sc=sc, bi_=bi_: e.activation(out=rn8[:, 4 * half:4 * half + 4, 0:T],
                                                                                        in_=b[:, 0:4 * T].rearrange("p (a b) -> p a b", a=4),
                                                                                        func=AF.Ln, scale=sc, bias=bi_), r=[b], w=[rn8])
                S.op('act', lambda e: e.activation(out=rn8[:, 0:8, 0:T], in_=rn8[:, 0:8, 0:T], func=AF.Exp, scale=-0.5), r=[rn8], w=[rn8])
                S.op('dve', lambda e: e.tensor_tensor(out=qkn[:, :, 0:T], in0=cacc[:, 0:8, 0:T], in1=rn8[:, 0:8, 0:T], op=ALU.mult), r=[cacc, rn8], w=[qkn])
                if samp and l == 0:
                    self.dump('ycv', cacc[:, :, 0:T], [128, 12, T], [cacc])
                    self.dump('qkn', qkn[:, :, 0:T], [128, 8, T], [qkn])
                if samp and STOP == 's4':
                    return
                db = kvs[0:T, 128:132]
                da = kvs[0:T, 132:136]
                S.op('act', lambda e: e.activation(out=sc4['beta'][0:T, :], in_=db, func=AF.Sigmoid), r=[kvs], w=[sc4['beta']])
                S.op('pool', lambda e: e.tensor_scalar(out=sc4['nbeta'][0:T, :], in0=sc4['beta'][0:T, :], scalar1=-1.0, scalar2=None, op0=ALU.mult),
                     r=[sc4['beta']], w=[sc4['nbeta']])
                S.op('dve', lambda e: e.tensor_tensor(out=sc4['sp'][0:T, :], in0=da, in1=P_['dtbb'][0:T, l * 4:l * 4 + 4], op=ALU.add), r=[kvs, P_['dtbb']], w=[sc4['sp']])
                S.op('act', lambda e: e.activation(out=sc4['sp'][0:T, :], in_=sc4['sp'][0:T, :], func=AF.Exp), r=[sc4['sp']], w=[sc4['sp']])
                S.op('act', lambda e: e.activation(out=sc4['sp'][0:T, :], in_=sc4['sp'][0:T, :], func=AF.Ln, bias=1.0), r=[sc4['sp']], w=[sc4['sp']])
                S.op('dve', lambda e: e.tensor_tensor(out=sc4['g'][0:T, :], in0=sc4['sp'][0:T, :], in1=negA[0:T, l * 4:l * 4 + 4], op=ALU.mult), r=[sc4['sp'], negA], w=[sc4['g']])
                um, vm = (US, VS) if samp else (UM, ONES)
                nml, nmu = (NMLS, NMUS) if samp else (NML, NMU)
                b = self.bank()
                S.op('pe', mms([(b[0:T, 0:4], cm[0:T, um, 0:T], sc4['g'][0:T, :], True, True),
                                (b[0:T, 4:8], cm[0:T, vm, 0:T], sc4['g'][0:T, :], True, True)]), r=[cm, sc4['g']], w=[b])
                S.op('act', lambda e, b=b: e.copy(out=sc4['Gc'][0:T, :], in_=b[0:T, 0:4]), r=[b], w=[sc4['Gc']])
                S.op('act', lambda e, b=b: e.copy(out=sc4['Gt'][0:T, :], in_=b[0:T, 4:8]), r=[b], w=[sc4['Gt']])
                S.op('act', lambda e: e.activation(out=sc4['eG'][0:T, :], in_=sc4['Gc'][0:T, :], func=AF.Exp), r=[sc4['Gc']], w=[sc4['eG']])
                S.op('dve', lambda e: e.tensor_tensor(out=sc4['bg'][0:T, :], in0=sc4['beta'][0:T, :], in1=sc4['eG'][0:T, :], op=ALU.mult), r=[sc4['beta'], sc4['eG']], w=[sc4['bg']])
                S.op('dve', lambda e: e.tensor_tensor(out=sc4['ekd'][0:T, :], in0=sc4['Gt'][0:T, :], in1=sc4['Gc'][0:T, :], op=ALU.subtract), r=[sc4['Gt'], sc4['Gc']], w=[sc4['ekd']])
                S.op('act', lambda e: e.activation(out=sc4['ekd'][0:T, :], in_=sc4['ekd'][0:T, :], func=AF.Exp), r=[sc4['ekd']], w=[sc4['ekd']])
                S.op('act', lambda e: e.activation(out=glv[0:T, :], in_=sc4['Gt'][0:T, :], func=AF.Exp), r=[sc4['Gt']], w=[glv])
                S.op('dve', lambda e: e.tensor_copy(out=gb[0:T, :, :], in_=bc(sc4['g'][0:T, :, None], [T, 4, 128])), r=[sc4['g']], w=[gb])
                bG = self.bank()
                S.op('pe', mms([(bG[:, h * T:(h + 1) * T], gb[0:T, h, :], cm[0:T, um, 0:T], True, True) for h in range(4)]), r=[gb, cm], w=[bG])
                bGv = bG[:, 0:4 * T].rearrange("p (a b) -> p a b", a=4)
                S.op('act', lambda e: e.activation(out=tLU[:, 0:4, 0:T], in_=bGv, func=AF.Exp), r=[bG], w=[tLU])
                S.op('dve', lambda e: e.tensor_tensor(out=qgT[:, :, 0:T], in0=qkn[:, 0:4, 0:T], in1=tLU[:, 0:4, 0:T], op=ALU.mult), r=[qkn, tLU], w=[qgT])
                if samp:
                    bGt = self.bank()
                    S.op('pe', mms([(bGt[:, h * T:(h + 1) * T], gb[0:T, h, :], cm[0:T, vm, 0:T], True, True) for h in range(4)]), r=[gb, cm], w=[bGt])
                    S.op('act', lambda e: e.activation(out=glbc[:, :, 0:T], in_=bGt[:, 0:4 * T].rearrange("p (a b) -> p a b", a=4), func=AF.Exp), r=[bGt], w=[glbc])
                S.op('dve', lambda e: e.tensor_tensor(out=argL[0:T, :, 0:T], in0=bc(sc4['Gc'][0:T, :, None], [T, 4, T]), in1=bGv[0:T], op=ALU.subtract),
                     r=[sc4['Gc'], bG], w=[argL])
                S.op('dve', lambda e: e.tensor_tensor(out=tLU[0:T, 0:4, 0:T], in0=argL[0:T, :, 0:T], in1=bc(cm[0:T, nml, 0:T].unsqueeze(1), [T, 4, T]), op=ALU.add),
                     r=[argL, cm], w=[tLU.k(0)])
                S.op('dve', lambda e: e.tensor_tensor(out=tLU[0:T, 4:8, 0:T], in0=bc(cm[0:T, nmu, 0:T].unsqueeze(1), [T, 4, T]), in1=argL[0:T, :, 0:T], op=ALU.subtract),
                     r=[argL, cm], w=[tLU.k(0)])
                S.op('act', lambda e: e.activation(out=DLU[0:T, :, 0:T], in_=tLU[0:T, :, 0:T], func=AF.Exp), r=[tLU], w=[DLU])
                bKK = self.bank()
                S.op('pe', mms([(bKK[0:T, h * T:(h + 1) * T], qkn[:, 4 + h, 0:T], qkn[:, 4 + h, 0:T], True, True) for h in range(4)]), r=[qkn], w=[bKK])
                bQK = self.bank()
                S.op('pe', mms([(bQK[0:T, h * T:(h + 1) * T], qkn[:, 4 + h, 0:T], qkn[:, h, 0:T], True, True) for h in range(4)]), r=[qkn], w=[bQK])
                Ct = CtP[pp2]
                S.op('dve', lambda e: e.tensor_tensor(out=Ct[0:T, :, 0:T], in0=bKK[0:T, 0:4 * T].rearrange("p (a b) -> p a b", a=4), in1=DLU[0:T, 0:4, 0:T], op=ALU.mult),
                     r=[bKK, DLU], w=[Ct])
                S.op('dve', lambda e: e.tensor_tensor(out=Ct[0:T, :, 0:T], in0=f32(Ct[0:T, :, 0:T]), in1=bc(sc4['nbeta'][0:T, :, None], [T, 4, T]), op=ALU.mult),
                     r=[Ct, sc4['nbeta']], w=[Ct])
                S.op('dve', lambda e: e.tensor_tensor(out=aqkT[0:T, :, 0:T], in0=bQK[0:T, 0:4 * T].rearrange("p (a b) -> p a b", a=4), in1=DLU[0:T, 4:8, 0:T], op=ALU.mult),
                     r=[bQK, DLU], w=[aqkT])
                bC = self.bank()
                S.op('pe', trs([(bC[0:T, h * T:(h + 1) * T], f32(Ct[0:T, h, 0:T]), cm[0:T, IDENT, 0:T]) for h in range(4)], None), r=[Ct, cm], w=[bC])
                Cx = CxP[pp2]
                S.op('act', lambda e: e.copy(out=Cx[0:T, :, 0:T], in_=bC[0:T, 0:4 * T].rearrange("p (a b) -> p a b", a=4)), r=[bC], w=[Cx])
                R = R0P[pp2]
                S.op('dve', lambda e, R=R: e.tensor_tensor(out=R[0:T, :, 0:T], in0=f32(Cx[0:T, :, 0:T]), in1=bc(cm[0:T, IDENT, 0:T].unsqueeze(1), [T, 4, T]), op=ALU.add),
                     r=[Cx, cm], w=[R])
                bk = self.bank()
                S.op('pe', trs([(bk[0:T, h * 128:(h + 1) * 128], qkn[:, 4 + h, 0:T], cm[:, IDENT, :]) for h in range(4)], None), r=[qkn, cm], w=[bk])
                bv_ = self.bank()
                S.op('pe', trs([(bv_[0:T, h * 128:(h + 1) * 128], cacc[:, 8 + h, 0:T], cm[:, IDENT, :]) for h in range(4)], None), r=[cacc, cm], w=[bv_])
                bkv4 = bk[0:T, :].rearrange("p (a b) -> p a b", a=4)
                S.op('dve', lambda e: e.tensor_tensor(out=kd[0:T], in0=bkv4, in1=bc(sc4['ekd'][0:T, :, None], [T, 4, 128]), op=ALU.mult), r=[bk, sc4['ekd']], w=[kd])
                S.op('dve', lambda e: e.tensor_tensor(out=kbg[0:T], in0=bkv4, in1=bc(sc4['bg'][0:T, :, None], [T, 4, 128]), op=ALU.mult), r=[bk, sc4['bg']], w=[kbg])
                S.op('dve', lambda e: e.tensor_tensor(out=vb[0:T], in0=bv_[0:T, :].rearrange("p (a b) -> p a b", a=4), in1=bc(sc4['beta'][0:T, :, None], [T, 4, 128]), op=ALU.mult),
                     r=[bv_, sc4['beta']], w=[vb])
                if record:
                    S.cur = S.lists['b']
                    self.pool = 'b'
                nlev = 1 if samp else 6
                X, Xt = Cx, Ct
                own = (CxP[pp2], CtP[pp2], R0P[pp2])
                alt = (XT, XtT, RT)
                for lev in range(nlev):
                    lastlev = lev == nlev - 1
                    Xn, Xtn, Rn = alt if lev % 2 == 0 else own
                    bXt = self.bank()
                    S.op('pe', mms([(bXt[0:T, h * T:(h + 1) * T], X[0:T, h, 0:T], Xt[0:T, h, 0:T], True, True) for h in range(4)]), r=[X, Xt], w=[bXt])
                    if not lastlev:
                        bX = self.bank()
                        S.op('pe', mms([(bX[0:T, h * T:(h + 1) * T], Xt[0:T, h, 0:T], X[0:T, h, 0:T], True, True) for h in range(4)]), r=[X, Xt], w=[bX])
                    S.op('act', lambda e, Xtn=Xtn, b=bXt: e.copy(out=Xtn[0:T, :, 0:T], in_=b[0:T, 0:4 * T].rearrange("p (a b) -> p a b", a=4)), r=[bXt], w=[Xtn])
                    if not lastlev:
                        S.op('act', lambda e, Xn=Xn, b=bX: e.copy(out=Xn[0:T, :, 0:T], in_=b[0:T, 0:4 * T].rearrange("p (a b) -> p a b", a=4)), r=[bX], w=[Xn])
                    bR = self.bank()
                    S.op('pe', mms([(bR[0:T, h * T:(h + 1) * T], Xtn[0:T, h, 0:T], R[0:T, h, 0:T], True, True) for h in range(4)]), r=[Xtn, R], w=[bR])
                    S.op('dve', lambda e, Rn=Rn, R=R, b=bR: e.tensor_tensor(out=Rn[0:T, :, 0:T], in0=f32(R[0:T, :, 0:T]), in1=b[0:T, 0:4 * T].rearrange("p (a b) -> p a b", a=4), op=ALU.add),
                         r=[R, bR], w=[Rn])
                    R = Rn
                    X, Xt = Xn, Xtn
                if samp and STOP == 's5':
                    return
                if samp and l == 0:
                    self.dump('g', sc4['g'][0:T, :], [T, 4], [sc4['g']])
                    self.dump('beta', sc4['beta'][0:T, :], [T, 4], [sc4['beta']])
                    self.dump('Gc', sc4['Gc'][0:T, :], [T, 4], [sc4['Gc']])
                    self.dump('Gt', sc4['Gt'][0:T, :], [T, 4], [sc4['Gt']])
                    self.dump('R', f32(R[0:T, :, 0:T]), [T, 4, T], [R])
                    self.dump('DLU', DLU[0:T, :, 0:T], [T, 8, T], [DLU])
                    self.dump('aqkT', aqkT[0:T, :, 0:T], [T, 4, T], [aqkT])
                bw = self.bank()
                S.op('pe', mms([(bw[:, h * T:(h + 1) * T], kbg[0:T, h, :], f32(R[0:T, h, 0:T]), True, True) for h in range(4)]), r=[kbg, R], w=[bw])
                S.op('act', lambda e: e.copy(out=wT[:, :, 0:T], in_=bw[:, 0:4 * T].rearrange("p (a b) -> p a b", a=4)), r=[bw], w=[wT])

                if not samp:
                    bu = self.bank()
                    S.op('pe', mms([(bu[0:T, h * 128:(h + 1) * 128], f32(R[0:T, h, 0:T]), vb[0:T, h, :], True, True) for h in range(4)]), r=[vb, R], w=[bu])
                    S.op('act', lambda e: e.copy(out=u_sb[0:T], in_=bu[0:T, :].rearrange("p (a b) -> p a b", a=4)), r=[bu], w=[u_sb])
                    St = Sst[l]
                    bws = self.bank()
                    S.op('pe', mms([(bws[0:T, h * 128:(h + 1) * 128], wT[:, h, 0:T], St[:, h, :], True, True) for h in range(4)]), r=[wT, St], w=[bws])
                    S.op('dve', lambda e: e.tensor_tensor(out=vnew[0:T], in0=u_sb[0:T], in1=bws[0:T, :].rearrange("p (a b) -> p a b", a=4), op=ALU.subtract), r=[u_sb, bws], w=[vnew])
                    bo = self.bank()
                    lst = []
                    for h in range(4):
                        lst.append((bo[:, h * T:(h + 1) * T], St[:, h, :], qgT[:, h, 0:T], True, False))
                        lst.append((bo[:, h * T:(h + 1) * T], vnew[0:T, h, :], aqkT[0:T, h, 0:T], False, True))
                    S.op('pe', mms(lst), r=[St, qgT, vnew, aqkT], w=[bo])
                    S.op('act', lambda e: e.copy(out=oT[:, :, 0:T], in_=bo[:, 0:4 * T].rearrange("p (a b) -> p a b", a=4)), r=[bo], w=[oT])
                    bS = self.bank()
                    S.op('pe', mms([(bS[:, h * 128:(h + 1) * 128], kd[0:T, h, :], vnew[0:T, h, :], True, True) for h in range(4)]), r=[kd, vnew], w=[bS])

                    def fS(e):
                        ins = None
                        for h in range(4):
                            ins = e.scalar_tensor_tensor(out=St[:, h, :], in0=St[:, h, :], scalar=glv[:, h:h + 1], in1=bS[:, h * 128:(h + 1) * 128],
                                                         op0=ALU.mult, op1=ALU.add)
                        return ins
                    S.op('dve', fS, r=[St, glv, bS], w=[St])
                    if last_of_seq:
                        self.dma(O['pdelta'].t.ap()[l].rearrange("h k v -> k h v"), St[:, :, :], [St], [O['pdelta']])
                        self.dma(O['pconv'].t.ap()[l], convst[l][:, :, :].rearrange("p a b -> p (a b)"), [convst[l]], [O['pconv']])
                else:
                    bu = self.bank()
                    S.op('pe', mms([(bu[:, h * T:(h + 1) * T], vb[0:T, h, :], f32(R[0:T, h, 0:T]), True, True) for h in range(4)]), r=[vb, R], w=[bu])
                    S.op('act', lambda e: e.copy(out=vnT[:, :, 0:T], in_=bu[:, 0:4 * T].rearrange("p (a b) -> p a b", a=4)), r=[bu], w=[vnT])
                    bws = self.bank()
                    for s in range(NSEQ):
                        sq_ = Sq[s % 2]
                        self.dma(sq_[:, :, :], I['sd'].t.ap()[l, s].rearrange("h k v -> k h v"), [I['sd']], [sq_])
                        S.op('pe', mms([(bws[:, h * T + 4 * s:h * T + 4 * s + 4], sq_[:, h, :], wT[:, h, 4 * s:4 * s + 4], True, True) for h in range(4)]), r=[sq_, wT], w=[bws])
                    S.op('dve', lambda e: e.tensor_tensor(out=vnT[:, :, 0:T], in0=vnT[:, :, 0:T], in1=bws[:, 0:4 * T].rearrange("p (a b) -> p a b", a=4), op=ALU.subtract),
                         r=[vnT, bws], w=[vnT])
                    bvt = self.bank()
                    S.op('pe', trs([(bvt[0:T, h * 128:(h + 1) * 128], vnT[:, h, 0:T], cm[:, IDENT, :]) for h in range(4)], None), r=[vnT, cm], w=[bvt])
                    S.op('act', lambda e: e.copy(out=vnew[0:T], in_=bvt[0:T, :].rearrange("p (a b) -> p a b", a=4)), r=[bvt], w=[vnew])
                    bo = self.bank_long
                    bo2 = self.bank()
                    lst = [(bo2[:, h * T:(h + 1) * T], vnew[0:T, h, :], aqkT[0:T, h, 0:T], True, True) for h in range(4)]
                    S.op('pe', mms(lst), r=[vnew, aqkT], w=[bo2])
                    S.op('act', lambda e: e.copy(out=osq[:, :, 0:T], in_=bo2[:, 0:4 * T].rearrange("p (a b) -> p a b", a=4)), r=[bo2], w=[osq])
                    for s in range(NSEQ):
                        sq_ = Sq[s % 2]
                        self.dma(sq_[:, :, :], I['sd'].t.ap()[l, s].rearrange("h k v -> k h v"), [I['sd']], [sq_])
                        S.op('pe', mms([(bo[:, h * T + 4 * s:h * T + 4 * s + 4], sq_[:, h, :], qgT[:, h, 4 * s:4 * s + 4], True, True) for h in range(4)]), r=[sq_, qgT], w=[bo])
                        vms = vmb[s % 2]
                        S.op('dve', lambda e, s=s, vms=vms: e.tensor_scalar(out=vms[0:T, :, :], in0=vnew[0:T, :, :], scalar1=cm[0:T, VS, 4 * s:4 * s + 1], scalar2=None, op0=ALU.mult),
                             r=[vnew, cm], w=[vms])
                        bS = self.bank()
                        S.op('pe', mms([(bS[:, h * 128:(h + 1) * 128], kd[0:T, h, :], vms[0:T, h, :], True, True) for h in range(4)]), r=[kd, vms], w=[bS])

                        def fS(e, s=s, sq_=sq_, bS=bS):
                            ins = None
                            for h in range(4):
                                ins = e.scalar_tensor_tensor(out=sq_[:, h, :], in0=sq_[:, h, :], scalar=glbc[:, h, 4 * s:4 * s + 1], in1=bS[:, h * 128:(h + 1) * 128],
                                                             op0=ALU.mult, op1=ALU.add)
                            return ins
                        S.op('dve', fS, r=[sq_, glbc, bS], w=[sq_])
                        self.dma(O['sdelta'].t.ap()[l, s].rearrange("h k v -> k h v"), sq_[:, :, :], [sq_], [O['sdelta']])
                    S.op('dve', lambda e: e.tensor_tensor(out=oT[:, :, 0:T], in0=osq[:, :, 0:T], in1=bo[:, 0:4 * T].rearrange("p (a b) -> p a b", a=4), op=ALU.add), r=[bo, osq], w=[oT])

                if samp and STOP == 's6':
                    return
                if samp and l == 0:
                    self.dump('oT', oT[:, :, 0:T], [128, 4, T], [oT])
                    self.dump('vnew', vnew[0:T, :, :], [T, 4, 128], [vnew])
                    self.dump('kd', kd[0:T, :, :], [T, 4, 128], [kd])
                    self.dump('wT', wT[:, :, 0:T], [128, 4, T], [wT])
                S.op('act', lambda e: e.activation(out=osq[:, :, 0:T], in_=oT[:, :, 0:T], func=AF.Square), r=[oT], w=[osq])
                b = self.bank()
                S.op('pe', mms([(b[:, h * T:(h + 1) * T], cm[:, ONES, :], osq[:, h, 0:T], True, True) for h in range(4)]), r=[osq, cm], w=[b])
                S.op('act', lambda e, b=b: e.activation(out=osq[:, :, 0:T], in_=b[:, 0:4 * T].rearrange("p (a b) -> p a b", a=4), func=AF.Ln, scale=1.0 / 128, bias=EPS),
                     r=[b], w=[osq])
                S.op('act', lambda e: e.activation(out=osq[:, :, 0:T], in_=osq[:, :, 0:T], func=AF.Exp, scale=-0.5), r=[osq], w=[osq])
                S.op('dve', lambda e: e.tensor_tensor(out=oT[:, :, 0:T], in0=oT[:, :, 0:T], in1=osq[:, :, 0:T], op=ALU.mult), r=[oT, osq], w=[oT])
                S.op('dve', lambda e: e.scalar_tensor_tensor(out=mixT[:, 4:8, 0:T], in0=oT[:, :, 0:T], scalar=P_['dng'][:, l:l + 1], in1=gateT[:, :, 0:T],
                                                             op0=ALU.mult, op1=ALU.mult), r=[oT, gateT, P_['dng']], w=[mixT])
                for half in range(2):
                    b = self.bank()
                    for mc in range(4):
                        m = half * 4 + mc
                        lst = []
                        for ec in range(8):
                            lst.append((b[:, mc * T:(mc + 1) * T], W_o[:, ec, m * 128:(m + 1) * 128], mixT[:, ec, 0:T], ec == 0, ec == 7))
                        S.op('pe', mms(lst), r=[W_o, mixT], w=[b])

                    def fres(e, b=b, half=half):
                        ins = None
                        for mc in range(4):
                            xa = xt_ap(half * 4 + mc)
                            ins = e.tensor_tensor(out=xa, in0=xa, in1=b[:, mc * T:(mc + 1) * T], op=ALU.add)
                        return ins
                    S.op('dve', fres, r=xkeys + [b], w=xkeys)

                if post is not None:
                    post()
                if record:
                    lists = S.lists
                    S.cur = None
                    S.lists = None
                    self.pool = 'all'
                    return lists

            ypk = Buf('d_ypT', O['ypT'].t, nsub=NT)
            for l in range(L):
                W_i, W_o = load_weights(l)
                S.op('pool', lambda e: e.memset(Sst1[:, :, :], 0.0), w=[Sst1])
                S.op('pool', lambda e: e.memset(convst1[:, :, :], 0.0), w=[convst1])
                src = I['xpT'] if l == 0 else ypk

                def ld(t):
                    xb = xTb[t % 3]
                    rk = [I['xpT']] if l == 0 else ypk.k(t)
                    self.dma(xb[:, :, :], src.t.ap()[:, t * 128:(t + 1) * 128].rearrange("(c p) t -> p c t", p=128), rk, [xb])
                ld(0)
                prevB = []
                merger = Merger(S)
                for t in range(NT):
                    if t + 1 < NT:
                        ld(t + 1)
                    xb = xTb[t % 3]

                    def xt_ap(kc, xb=xb):
                        if kc is None:
                            return xb[:, :, :]
                        return xb[:, kc, :]

                    def post(t=t, xb=xb):
                        self.dma(O['ypT'].t.ap()[:, t * 128:(t + 1) * 128].rearrange("(c p) t -> p c t", p=128), xb[:, :, :], [xb], ypk.k(t))
                    lists = tile_layer(l, 128, xt_ap, W_i, W_o, False, t, t == 0, t == NT - 1, xb, record=True, post=post)
                    merger.merge([lists['a'], lists['d'], prevB], gate=(0, 1, lists['np']), bias=[0.0, 0.0, BBIAS])
                    prevB = lists['b']
                merger.merge([prevB])
            import os
            DO_SAMPLE = os.environ.get('K_NOSAMPLE') is None
            if not DO_SAMPLE:
                S.emit()
                return nc
            scr = self.sb('barscr', [128, 8])
            S.barrier({'act': lambda e: e.activation(out=scr[:, 0:2], in_=cm[:, ONES, 0:2], func=AF.Copy),
                       'dve': lambda e: e.memset(scr[:, 2:4], 0.0),
                       'pool': lambda e: e.memset(scr[:, 4:6], 0.0)})
            S.op('pool', lambda e: e.memset(VAs[:, :, 64:65], 1.0), w=[VAs])
            S.op('pool', lambda e: e.memset(VcA[:, :, :, 64:65], 1.0), w=[VcA])
            S.op('pool', lambda e: e.tensor_copy(out=msp[:, :], in_=cm[:, NMLS, 64:128]), r=[cm], w=[msp])
            S.op('pool', lambda e: e.tensor_copy(out=msc[:, :], in_=cm[:, NMUS, 64:128]), r=[cm], w=[msc])
            for kc in range(8):
                self.dma(xsT[:, kc, :], I['xsT'].t.ap()[kc * 128:(kc + 1) * 128, :], [I['xsT']], [xsT])
            for l in range(L):
                W_i, W_o = load_weights(l)

                def xs_ap(kc):
                    if kc is None:
                        return xsT[:, :, :]
                    return xsT[:, kc, :]
                tile_layer(l, TS, xs_ap, W_i, W_o, True, 0, False, False, xsT)
            for kc in range(8):
                self.dma(O['ysT'].t.ap()[kc * 128:(kc + 1) * 128, :], xsT[:, kc, :], [xsT], [O['ysT']])
            S.emit()
        return nc


def _consts(NT, NSEQ, PAST):
    i = np.arange(128)
    cm = np.zeros((128, 11, 128), np.float32)
    cm[:, 0] = 1.0
    cm[:, 1] = np.eye(128)
    cm[:, 2] = (i[:, None] <= i[None, :])
    cm[:, 3] = np.where(i[:, None] > i[None, :], 0.0, NEG)
    cm[:, 4] = np.where(i[None, :] >= i[:, None], 0.0, NEG)
    cm[:, 5] = (i[:, None] > i[None, :])
    cm[:, 6] = (i[:, None] <= i[None, :])
    j = np.arange(64)
    same = (j[:, None] // 4) == (j[None, :] // 4)
    cm[:64, 7, :64] = same & (j[:, None] <= j[None, :])
    cm[:64, 8, :64] = same
    cm[:64, 9, :64] = np.where(same & (j[:, None] > j[None, :]), 0.0, NEG)
    cm[:64, 10, :64] = np.where(same & (j[None, :] >= j[:, None]), 0.0, NEG)
    cm[:, 9, 64:128] = (i[:, None] > (j[None, :] % 4))
    cm[:64, 10, 64:128] = same & (j[:, None] <= j[None, :])
    half = 8
    inv = np.power(np.float32(500000.0), -np.arange(half, dtype=np.float32) / half).astype(np.float32)
    pos = (np.arange(NT)[None, :] * 128 + i[:, None]).astype(np.float32)
    ang = pos[:, :, None] * inv[None, None, :]
    cosp = np.cos(ang).astype(np.float32).reshape(128, NT * 8)
    sinp = np.sin(ang).astype(np.float32).reshape(128, NT * 8)
    poss = (PAST + (i % 4)).astype(np.float32)
    angs = poss[:, None] * inv[None, :]
    return cm.reshape(128, 11 * 128), cosp, sinp, np.cos(angs).astype(np.float32), np.sin(angs).astype(np.float32)


_CACHE = {}


def run(inputs, NT, L, G, NSEQ, PAST, ncores, nprompt):
    key = (NT, L, G, NSEQ)
    if key not in _CACHE:
        _CACHE[key] = Builder(NT, L, G, NSEQ).build()
    nc = _CACHE[key]
    f = lambda a: np.ascontiguousarray(np.asarray(a, dtype=np.float32))
    x_prompt = f(inputs['x_prompt'])
    x_sample = f(inputs['x_sample'])
    cm, cosp, sinp, coss, sins = _consts(NT, NSEQ, PAST)
    bcast = lambda a: f(np.broadcast_to(np.asarray(a, np.float32).reshape(1, -1), (128, a.size)))
    shared = {
        'w_in': f(inputs['w_in']), 'w_out': f(inputs['w_out']),
        'normg': f(np.asarray(inputs['norm_g']).reshape(L, 8, 128).transpose(2, 0, 1).reshape(128, L * 8)),
        'convw': f(np.asarray(inputs['conv_w']).reshape(L, 4, 12, 128).transpose(3, 0, 1, 2).reshape(128, L * 48)),
        'dng': f(np.asarray(inputs['dn_norm_g']).T),
        'gqk': bcast(np.concatenate([np.tile(np.asarray(inputs['q_norm_g']), (1, 8)), np.tile(np.asarray(inputs['k_norm_g']), (1, 2))], axis=1)),
        'sinkb': bcast(np.asarray(inputs['sinks'])), 'alogb': bcast(np.asarray(inputs['a_log'])), 'dtbb': bcast(np.asarray(inputs['dt_bias'])),
        'cosp': cosp, 'sinp': sinp, 'coss': coss, 'sins': sins, 'cmat': cm,
    }
    ck = f(inputs['cache_win_k']).reshape(L, -1, 128, 128)
    cv = f(inputs['cache_win_v']).reshape(L, -1, 128, 128)
    sc = f(inputs['state_conv'])
    sd = f(inputs['state_delta'])
    in_maps = []
    for c in range(ncores):
        b = c % nprompt
        sl = slice(c * NSEQ, (c + 1) * NSEQ)
        m = dict(shared)
        m['xpT'] = f(x_prompt[b].T)
        m['xsT'] = f(x_sample[sl].reshape(NSEQ * 4, D).T)
        m['ck'] = f(ck[:, sl])
        m['ckT'] = f(ck[:, sl].transpose(0, 1, 3, 2))
        m['cv'] = f(cv[:, sl])
        m['scT'] = f(sc[:, sl].transpose(0, 3, 1, 2).reshape(L, 1536, NSEQ * 3))
        m['sd'] = f(sd[:, sl])
        in_maps.append(m)
    res = run_bass_kernel_spmd(nc, in_maps, core_ids=list(range(ncores)))
    R = res.results
    global LAST
    LAST = R
    TP = NT * 128
    y_prompt = np.stack([R[b]['ypT'].T for b in range(nprompt)])
    y_sample = np.concatenate([R[c]['ysT'].T.reshape(NSEQ, 4, D) for c in range(ncores)], 0)
    pwk = np.stack([R[b]['pwk'] for b in range(nprompt)], 1).reshape(L, nprompt, 128, 2, 64)
    pwv = np.stack([R[b]['pwv'] for b in range(nprompt)], 1).reshape(L, nprompt, 128, 2, 64)
    pconv = np.stack([R[b]['pconv'].reshape(L, 128, 12, 3).transpose(0, 3, 2, 1).reshape(L, 3, 1536) for b in range(nprompt)], 1)
    pdelta = np.stack([R[b]['pdelta'] for b in range(nprompt)], 1)
    swk = np.concatenate([R[c]['swk'] for c in range(ncores)], 1).reshape(L, -1, 128, 2, 64)
    swv = np.concatenate([R[c]['swv'] for c in range(ncores)], 1).reshape(L, -1, 128, 2, 64)
    sconv = np.concatenate([R[c]['sconv'] for c in range(ncores)], 1)
    sdelta = np.concatenate([R[c]['sdelta'] for c in range(ncores)], 1)
    outs = (y_prompt, y_sample, pwk, pwv, pconv, pdelta, swk, swv, sconv, sdelta)
    return tuple(np.ascontiguousarray(o, dtype=np.float32) for o in outs)


def kernel(**inputs):
    return run(inputs, NT=64, L=4, G=2, NSEQ=16, PAST=8192, ncores=8, nprompt=2)
```
